# Optimizing a Trainium2 kernel written in Bass

```python
import math
import jax
import jax.numpy as jnp
from jax import lax
import numpy as np

D_MODEL = 1024
BATCH = 4
SEQ = 8192
DEPTH = 4

N_MIXERS = 3

GDN_DK = 128
GDN_DV = 128
GDN_HEADS = D_MODEL // GDN_DK
GDN_CONV = 4
GDN_CHUNK = 64
GDN_QK = GDN_HEADS * GDN_DK
GDN_V = GDN_HEADS * GDN_DV
GDN_IN = 2 * GDN_QK + 2 * GDN_V + 2 * GDN_HEADS

MOBA_DH = 128
MOBA_HEADS = D_MODEL // MOBA_DH
MOBA_BLOCK = 256
MOBA_TOPK = 3
MOBA_QBLOCK = 16
ROPE_THETA = 500000.0
ROPE_DIMS = MOBA_DH // 4

RET_DK = 256
RET_DV = 512
RET_HEADS = D_MODEL // RET_DK
RET_CHUNK = 128
XPOS_BASE = 10000.0

D_FF = ((8 * D_MODEL + 3 * 256 - 1) // (3 * 256)) * 256

DEEPNORM_ALPHA = (2 * DEPTH) ** 0.25
DEEPNORM_BETA = (8 * DEPTH) ** -0.25
LN_EPS = 1e-5
NORM_EPS = 1e-6
NEG_INF = -1e30
F32 = jnp.float32

kernel_name = 'hybrid_gdn_moba_retention_deepnorm'


def layer_norm(x, g, b):
    xf = x.astype(F32)
    mu = jnp.mean(xf, -1, keepdims=True)
    var = jnp.mean(jnp.square(xf - mu), -1, keepdims=True)
    return ((xf - mu) * lax.rsqrt(var + LN_EPS) * g.astype(F32) + b.astype(F32)).astype(x.dtype)


def split_heads(t, n_heads):
    bsz, seq, width = t.shape
    return t.reshape(bsz, seq, n_heads, width // n_heads).transpose(0, 2, 1, 3)


def l2_normalize(t):
    tf = t.astype(F32)
    return tf * lax.rsqrt(jnp.sum(tf * tf, -1, keepdims=True) + NORM_EPS)


def apply_rotary(x, inv_freq):
    half = inv_freq.shape[0]
    seq = x.shape[2]
    ang = jnp.arange(seq, dtype=F32)[:, None] * inv_freq[None, :]
    cos, sin = jnp.cos(ang), jnp.sin(ang)
    xf = x.astype(F32)
    x1, x2 = xf[..., :half], xf[..., half:2 * half]
    out = jnp.concatenate([x1 * cos - x2 * sin, x1 * sin + x2 * cos, xf[..., 2 * half:]], -1)
    return out.astype(x.dtype)


def causal_depthwise_conv(x, w):
    width, ch = w.shape
    return lax.conv_general_dilated(
        x, w[:, None, :], window_strides=(1,), padding=[(width - 1, 0)],
        dimension_numbers=('NWC', 'WIO', 'NWC'), feature_group_count=ch)


def chunk_gated_delta_rule(q, k, v, g, beta):
    bsz, nh, seq, dk = q.shape
    dv = v.shape[-1]
    c = GDN_CHUNK
    n = seq // c
    q = q.astype(F32).reshape(bsz, nh, n, c, dk)
    k = k.astype(F32).reshape(bsz, nh, n, c, dk)
    v = v.astype(F32).reshape(bsz, nh, n, c, dv)
    gc = jnp.cumsum(g.reshape(bsz, nh, n, c), -1)
    beta = beta.reshape(bsz, nh, n, c)
    tril = jnp.tril(jnp.ones((c, c), bool))
    tril_strict = jnp.tril(jnp.ones((c, c), bool), -1)
    diff = gc[..., :, None] - gc[..., None, :]
    decay = jnp.where(tril, jnp.exp(jnp.where(tril, diff, 0.0)), 0.0)
    k_beta = k * beta[..., None]
    v_beta = v * beta[..., None]
    low = jnp.where(tril_strict, jnp.einsum('bhncd,bhnsd->bhncs', k_beta, k) * decay, 0.0)
    rhs = jnp.concatenate([v_beta, k_beta * jnp.exp(gc)[..., None]], -1)
    sol = lax.linalg.triangular_solve(low, rhs, left_side=True, lower=True, unit_diagonal=True)
    u, w = sol[..., :dv], sol[..., dv:]
    attn_intra = jnp.where(tril, jnp.einsum('bhncd,bhnsd->bhncs', q, k) * decay, 0.0)
    q_dec = q * jnp.exp(gc)[..., None]
    k_dec = k * jnp.exp(gc[..., -1:] - gc)[..., None]
    chunk_decay = jnp.exp(gc[..., -1])

    def step(state, inp):
        u_i, w_i, qd_i, kd_i, a_i, cd_i = inp
        v_new = u_i - w_i @ state
        o_i = qd_i @ state + a_i @ v_new
        state = state * cd_i[..., None, None] + jnp.swapaxes(kd_i, -1, -2) @ v_new
        return state, o_i

    xs = tuple(jnp.moveaxis(t, 2, 0) for t in (u, w, q_dec, k_dec, attn_intra, chunk_decay))
    state0 = jnp.zeros((bsz, nh, dk, dv), F32)
    _, o = lax.scan(step, state0, xs)
    return o.transpose(1, 2, 0, 3, 4).reshape(bsz, nh, seq, dv)


def gated_deltanet(x, w_in, w_conv, a_log, dt_bias, norm_g, w_out):
    bsz, seq, _ = x.shape
    nh = GDN_HEADS
    proj = x @ w_in
    qkv, z, b_logit, a_in = jnp.split(
        proj, [2 * GDN_QK + GDN_V, 2 * GDN_QK + 2 * GDN_V, 2 * GDN_QK + 2 * GDN_V + nh], axis=-1)
    qkv = jax.nn.silu(causal_depthwise_conv(qkv, w_conv))
    q, k, v = jnp.split(qkv, [GDN_QK, 2 * GDN_QK], axis=-1)
    q = l2_normalize(split_heads(q, nh)) * GDN_DK ** -0.5
    k = l2_normalize(split_heads(k, nh))
    v = split_heads(v, nh).astype(F32)
    beta = jax.nn.sigmoid(b_logit.astype(F32)).transpose(0, 2, 1)
    g = -(jnp.exp(a_log.astype(F32)) *
          jax.nn.softplus(a_in.astype(F32) + dt_bias.astype(F32))).transpose(0, 2, 1)
    o = chunk_gated_delta_rule(q, k, v, g, beta).transpose(0, 2, 1, 3)
    o = o * lax.rsqrt(jnp.mean(o * o, -1, keepdims=True) + NORM_EPS) * norm_g.astype(F32)
    o = o * jax.nn.silu(z.astype(F32)).reshape(bsz, seq, nh, GDN_DV)
    return o.reshape(bsz, seq, GDN_V).astype(x.dtype) @ w_out


def moba_attention(x, w_qkv, w_out):
    bsz, seq, _ = x.shape
    nh, dh, bs, qb = MOBA_HEADS, MOBA_DH, MOBA_BLOCK, MOBA_QBLOCK
    q, k, v = jnp.split(x @ w_qkv, 3, axis=-1)
    half = ROPE_DIMS // 2
    inv_freq = jnp.power(ROPE_THETA, -jnp.arange(half, dtype=F32) / half)
    q = apply_rotary(split_heads(q, nh), inv_freq)
    k = apply_rotary(split_heads(k, nh), inv_freq)
    v = split_heads(v, nh)
    nkb = -(-seq // bs)
    pad = nkb * bs - seq
    kp = jnp.pad(k, ((0, 0), (0, 0), (0, pad), (0, 0)))
    vp = jnp.pad(v, ((0, 0), (0, 0), (0, pad), (0, 0)))
    k_blocks = kp.reshape(bsz, nh, nkb, bs, dh)
    v_blocks = vp.reshape(bsz, nh, nkb, bs, dh)
    k_mean = jnp.mean(k_blocks.astype(F32), axis=3)
    topk = min(MOBA_TOPK, nkb)
    scale = dh ** -0.5
    b_idx = jnp.arange(bsz)[:, None, None, None]
    h_idx = jnp.arange(nh)[None, :, None, None]

    def one_query_block(qi):
        q0 = qi * qb
        own = q0 // bs
        q_blk = lax.dynamic_slice_in_dim(q, q0, qb, axis=2).astype(F32)
        gate = jnp.einsum('bhqd,bhnd->bhqn', q_blk, k_mean)
        gate = jnp.where(jnp.arange(nkb) < own, gate, -jnp.inf)
        _, sel = lax.top_k(gate, topk)
        sel_valid = sel < own
        k_sel = k_blocks[b_idx, h_idx, sel].astype(F32)
        v_sel = v_blocks[b_idx, h_idx, sel].astype(F32)
        s_sel = jnp.einsum('bhqd,bhqnkd->bhqnk', q_blk, k_sel) * scale
        s_sel = jnp.where(sel_valid[..., None], s_sel, NEG_INF)
        k_own = lax.dynamic_slice_in_dim(kp, own * bs, bs, axis=2).astype(F32)
        v_own = lax.dynamic_slice_in_dim(vp, own * bs, bs, axis=2).astype(F32)
        s_own = jnp.einsum('bhqd,bhkd->bhqk', q_blk, k_own) * scale
        q_pos = q0 + jnp.arange(qb)
        k_pos = own * bs + jnp.arange(bs)
        s_own = jnp.where(k_pos[None, :] <= q_pos[:, None], s_own, NEG_INF)
        s = jnp.concatenate([s_sel.reshape(bsz, nh, qb, topk * bs), s_own], -1)
        p = jax.nn.softmax(s, axis=-1)
        p_sel = p[..., :topk * bs].reshape(bsz, nh, qb, topk, bs)
        p_own = p[..., topk * bs:]
        return (jnp.einsum('bhqnk,bhqnkd->bhqd', p_sel, v_sel) +
                jnp.einsum('bhqk,bhkd->bhqd', p_own, v_own))

    o = lax.map(one_query_block, jnp.arange(seq // qb))
    o = o.transpose(1, 0, 3, 2, 4).reshape(bsz, seq, nh * dh)
    return o.astype(x.dtype) @ w_out


def chunk_retention(q, k, v, log_gamma):
    bsz, nh, seq, dk = q.shape
    dv = v.shape[-1]
    c = RET_CHUNK
    n = seq // c
    q = q.astype(F32).reshape(bsz, nh, n, c, dk)
    k = k.astype(F32).reshape(bsz, nh, n, c, dk)
    v = v.astype(F32).reshape(bsz, nh, n, c, dv)
    pos = jnp.arange(c, dtype=F32)
    tril = jnp.tril(jnp.ones((c, c), bool))
    rel = jnp.where(tril, pos[:, None] - pos[None, :], 0.0)
    dmat = jnp.where(tril, jnp.exp(log_gamma[:, None, None] * rel), 0.0)
    inner = jnp.einsum('bhncd,bhnsd->bhncs', q, k) * dmat[None, :, None]
    o_inner = jnp.einsum('bhncs,bhnsd->bhncd', inner, v)
    q_dec = q * jnp.exp(log_gamma[:, None] * (pos + 1.0))[None, :, None, :, None]
    k_dec = k * jnp.exp(log_gamma[:, None] * (c - 1.0 - pos))[None, :, None, :, None]
    chunk_decay = jnp.exp(log_gamma * c)[None, :, None, None]

    def step(state, inp):
        qd_i, kd_i, v_i = inp
        out = qd_i @ state
        state = state * chunk_decay + jnp.swapaxes(kd_i, -1, -2) @ v_i
        return state, out

    xs = tuple(jnp.moveaxis(t, 2, 0) for t in (q_dec, k_dec, v))
    state0 = jnp.zeros((bsz, nh, dk, dv), F32)
    _, o_cross = lax.scan(step, state0, xs)
    o = o_inner + o_cross.transpose(1, 2, 0, 3, 4)
    return o.reshape(bsz, nh, seq, dv)


def retention(x, w_in, gn_g, w_out):
    bsz, seq, _ = x.shape
    nh = RET_HEADS
    q, k, v, gate = jnp.split(x @ w_in, [nh * RET_DK, 2 * nh * RET_DK, 2 * nh * RET_DK + nh * RET_DV], axis=-1)
    inv_freq = jnp.power(XPOS_BASE, -jnp.linspace(0.0, 1.0, RET_DK // 2, dtype=F32))
    q = apply_rotary(split_heads(q, nh), inv_freq)
    k = apply_rotary(split_heads(k, nh), inv_freq).astype(F32) * RET_DK ** -0.5
    v = split_heads(v, nh)
    log_gamma = jnp.log1p(-jnp.exp2(-5.0 - jnp.arange(nh, dtype=F32)))
    o = chunk_retention(q, k, v, log_gamma).transpose(0, 2, 1, 3)
    mu = jnp.mean(o, -1, keepdims=True)
    var = jnp.mean(jnp.square(o - mu), -1, keepdims=True)
    o = ((o - mu) * lax.rsqrt(var + LN_EPS)).reshape(bsz, seq, nh * RET_DV) * gn_g.astype(F32)
    o = jax.nn.silu(gate.astype(F32)) * o
    return o.astype(x.dtype) @ w_out


def swiglu(x, w13, w2):
    g, u = jnp.split(x @ w13, 2, axis=-1)
    return (jax.nn.silu(g) * u) @ w2


def setup_inputs(seed: int = 0) -> dict:
    key = jax.random.key(seed)
    ks = jax.random.split(key, 24)
    n_a = len(range(0, DEPTH, N_MIXERS))
    n_b = len(range(1, DEPTH, N_MIXERS))
    n_c = len(range(2, DEPTH, N_MIXERS))

    def dense(k, shape, fan_in, gain=1.0):
        return jax.random.normal(k, shape, F32) * (gain * fan_in ** -0.5)

    def gain(k, shape):
        return 1.0 + 0.02 * jax.random.normal(k, shape, F32)

    x = jax.random.normal(ks[0], (BATCH, SEQ, D_MODEL), F32)
    a_w_in = dense(ks[1], (n_a, D_MODEL, GDN_IN), D_MODEL)
    a_conv = dense(ks[2], (n_a, GDN_CONV, 2 * GDN_QK + GDN_V), GDN_CONV)
    a_a_log = jnp.log(jax.random.uniform(ks[3], (n_a, GDN_HEADS), F32, 1.0, 16.0))
    dt = jnp.exp(jax.random.uniform(ks[4], (n_a, GDN_HEADS), F32, math.log(1e-3), math.log(1e-1)))
    a_dt_bias = dt + jnp.log(-jnp.expm1(-dt))
    a_norm_g = gain(ks[5], (n_a, GDN_DV))
    a_w_out = dense(ks[6], (n_a, GDN_V, D_MODEL), GDN_V, DEEPNORM_BETA)
    b_w_qkv = dense(ks[7], (n_b, D_MODEL, 3 * MOBA_HEADS * MOBA_DH), D_MODEL)
    b_w_out = dense(ks[8], (n_b, MOBA_HEADS * MOBA_DH, D_MODEL), MOBA_HEADS * MOBA_DH, DEEPNORM_BETA)
    c_w_in = dense(ks[9], (n_c, D_MODEL, 2 * RET_HEADS * RET_DK + 2 * RET_HEADS * RET_DV), D_MODEL)
    c_gn_g = gain(ks[10], (n_c, RET_HEADS * RET_DV))
    c_w_out = dense(ks[11], (n_c, RET_HEADS * RET_DV, D_MODEL), RET_HEADS * RET_DV, DEEPNORM_BETA)
    f_w13 = dense(ks[12], (DEPTH, D_MODEL, 2 * D_FF), D_MODEL)
    f_w2 = dense(ks[13], (DEPTH, D_FF, D_MODEL), D_FF, DEEPNORM_BETA)
    ln1_g = gain(ks[14], (DEPTH, D_MODEL))
    ln1_b = 0.02 * jax.random.normal(ks[15], (DEPTH, D_MODEL), F32)
    ln2_g = gain(ks[16], (DEPTH, D_MODEL))
    ln2_b = 0.02 * jax.random.normal(ks[17], (DEPTH, D_MODEL), F32)
    return {'x': x, 'a_w_in': a_w_in, 'a_conv': a_conv, 'a_a_log': a_a_log, 'a_dt_bias': a_dt_bias,
            'a_norm_g': a_norm_g, 'a_w_out': a_w_out, 'b_w_qkv': b_w_qkv, 'b_w_out': b_w_out,
            'c_w_in': c_w_in, 'c_gn_g': c_gn_g, 'c_w_out': c_w_out, 'f_w13': f_w13, 'f_w2': f_w2,
            'ln1_g': ln1_g, 'ln1_b': ln1_b, 'ln2_g': ln2_g, 'ln2_b': ln2_b}


def reference(x, a_w_in, a_conv, a_a_log, a_dt_bias, a_norm_g, a_w_out, b_w_qkv, b_w_out,
              c_w_in, c_gn_g, c_w_out, f_w13, f_w2, ln1_g, ln1_b, ln2_g, ln2_b):
    h = x
    for i in range(DEPTH):
        kind, j = i % N_MIXERS, i // N_MIXERS
        if kind == 0:
            mix = gated_deltanet(h, a_w_in[j], a_conv[j], a_a_log[j], a_dt_bias[j], a_norm_g[j], a_w_out[j])
        elif kind == 1:
            mix = moba_attention(h, b_w_qkv[j], b_w_out[j])
        else:
            mix = retention(h, c_w_in[j], c_gn_g[j], c_w_out[j])
        h = layer_norm(DEEPNORM_ALPHA * h + mix, ln1_g[i], ln1_b[i])
        h = layer_norm(DEEPNORM_ALPHA * h + swiglu(h, f_w13[i], f_w2[i]), ln2_g[i], ln2_b[i])
    return h
```

```python
import numpy as np
from contextlib import ExitStack
import concourse.bass as bass
import concourse.mybir as mybir
from concourse.bass_utils import run_bass_kernel_spmd

F32 = mybir.dt.float32
BF16 = mybir.dt.bfloat16
AF = mybir.ActivationFunctionType
ALU = mybir.AluOpType
AX = mybir.AxisListType

PE, ACT, DVE, POOL, SP = "PE", "ACT", "DVE", "POOL", "SP"


def MM(out, lhsT, rhs, start=True, stop=True):
    return dict(out=out, lhsT=lhsT, rhs=rhs, start=start, stop=stop)


class Prog:
    ENGS = (PE, ACT, DVE, POOL, SP)
    PAIRS = [[0, 1], [2, 3], [4, 5], [6, 7]]

    def __init__(self, name="k"):
        self.nc = bass.Bass("TRN2", target_bir_lowering=False)
        self.stack = ExitStack()
        self.pstack = None
        self.dma_sems = {}
        self.esem = None
        self.base = {e: 0 for e in self.ENGS}
        self.dram_prefix = ""
        self.tag = ""
        self.nphase = 0
        self.total_ops = {e: 0 for e in self.ENGS}
        self._reset()

    def _reset(self):
        self.ops = {e: [] for e in self.ENGS}
        self.last_write = {}
        self.readers = {}
        self.seen = {e: {} for e in self.ENGS}
        self.marked = {e: set() for e in self.ENGS}

    def begin_phase(self, tag):
        self.tag = tag
        self.pstack = ExitStack()
        self._reset()

    def sb(self, shape, dtype, name):
        return self.pstack.enter_context(self.nc.sbuf_tensor(f"{self.tag}_{name}", list(shape), dtype))

    def ps(self, shape, dtype, name):
        return self.pstack.enter_context(self.nc.psum_tensor(f"{self.tag}_{name}", list(shape), dtype))

    def dram_in(self, name, shape, dtype=F32):
        return self.nc.dram_tensor(self.dram_prefix + name, list(shape), dtype, kind="ExternalInput").ap()

    def dram_out(self, name, shape, dtype=F32):
        return self.nc.dram_tensor(name, list(shape), dtype, kind="ExternalOutput").ap()

    def dram_tmp(self, name, shape, dtype=F32):
        return self.nc.dram_tensor(name, list(shape), dtype).ap()

    def _deps(self, eng, reads, writes):
        deps = []
        for k in reads:
            if k in self.last_write:
                deps.append(self.last_write[k])
        for k in writes:
            if k in self.last_write:
                deps.append(self.last_write[k])
            for r in self.readers.get(k, ()):
                if r[0] != eng:
                    deps.append(r)
        out = {}
        for (prod, val) in deps:
            if prod == eng and (eng == PE):
                continue
            if self.seen[eng].get(prod, -1) >= val:
                continue
            if out.get(prod, -1) < val:
                out[prod] = val
        for prod, val in out.items():
            self.seen[eng][prod] = val
            if prod in self.ops:
                self.marked[prod].add(val)
        return list(out.items())

    def op(self, eng, fn, reads=(), writes=()):
        deps = self._deps(eng, reads, writes)
        idx = len(self.ops[eng])
        self.ops[eng].append(("op", deps, fn, idx))
        tag = (eng, idx)
        for k in writes:
            self.last_write[k] = tag
            self.readers[k] = []
        for k in reads:
            if k not in writes:
                self.readers.setdefault(k, []).append(tag)
        return idx

    def I(self, eng, name, kw, reads=(), writes=()):
        return self.op(eng, lambda e, name=name, kw=kw: getattr(e, name)(**kw), reads, writes)

    def _slot(self, slot):
        if slot not in self.dma_sems:
            sem = self.stack.enter_context(self.nc.semaphore(f"d{len(self.dma_sems)}"))
            self.dma_sems[slot] = [sem, 0]
        return self.dma_sems[slot]

    def dma(self, queue, out, in_, reads=(), writes=(), slot=None):
        assert slot is not None
        deps = self._deps(queue, reads, writes)
        ent = self._slot(slot)
        ent[1] += 16
        val = ent[1]
        self.ops[queue].append(("dma", deps, (out, in_, ent[0]), None))
        tag = (("dma", slot), val)
        for k in writes:
            self.last_write[k] = tag
            self.readers[k] = []
        for k in reads:
            self.readers.setdefault(k, []).append(tag)

    def cc(self, kind, alu, in_, out, slot):
        ent = self._slot(slot)
        ent[1] += 1
        self.ops[POOL].append(("cc", [], (kind, alu, in_, out, ent[0]), None))

    def end_phase(self):
        nc = self.nc
        if self.esem is None:
            self.esem = {e: self.stack.enter_context(nc.semaphore(f"e{e}")) for e in self.ENGS}
        esem = self.esem
        for e in self.ENGS:
            last = [idx for kind, _, _, idx in self.ops[e] if kind == "op"]
            if last:
                self.marked[e].add(last[-1])
        ranks = {}
        for e in self.ENGS:
            m = sorted(self.marked[e])
            ranks[e] = {idx: self.base[e] + i + 1 for i, idx in enumerate(m)}
        pre_eng = [(esem[e], self.base[e]) for e in self.ENGS if self.base[e] > 0]
        pre_dma = [(ent[0], ent[1]) for ent in self.prev_dma] if self.nphase > 0 else []

        def emit(ename, e):
            if not self.ops[ename]:
                return
            for sem, val in pre_eng + pre_dma:
                e.wait_ge(sem, val)
            for kind, deps, payload, idx in self.ops[ename]:
                for prod, val in deps:
                    if isinstance(prod, tuple):
                        e.wait_ge(self.dma_sems[prod[1]][0], val)
                    else:
                        e.wait_ge(esem[prod], ranks[prod][val])
                if kind == "op":
                    ins = payload(e)
                    if idx in ranks[ename]:
                        ins.then_inc(esem[ename], 1)
                elif kind == "dma":
                    out, in_, sem = payload
                    e.dma_start(out=out, in_=in_).then_inc(sem, 16)
                else:
                    ckind, alu, in_, out, sem = payload
                    e.collective_compute(ckind, alu, replica_groups=self.PAIRS, ins=[in_], outs=[out]).then_inc(sem)

        with nc.Block() as block:
            @block.tensor
            def _(e):
                emit(PE, e)

            @block.scalar
            def _(e):
                emit(ACT, e)

            @block.vector
            def _(e):
                emit(DVE, e)

            @block.gpsimd
            def _(e):
                emit(POOL, e)

            @block.sync
            def _(e):
                emit(SP, e)
        for e in self.ENGS:
            self.base[e] += len(self.marked[e])
            self.total_ops[e] += len(self.ops[e])
        self.prev_dma = [[ent[0], ent[1]] for ent in self.dma_sems.values()]
        self.nphase += 1
        self.pstack.close()
        self.pstack = None
        self._reset()

    def finish(self):
        if self.pstack is not None:
            self.end_phase()
        nc = self.nc
        finals = [(ent[0], ent[1]) for ent in self.dma_sems.values()] + [(self.esem[e], self.base[e]) for e in self.ENGS if self.base[e] > 0]
        with nc.Block() as block:
            @block.sync
            def _(e):
                for sem, val in finals:
                    e.wait_ge(sem, val)
        self.stack.close()
        return nc

    def stats(self):
        return {e: self.total_ops[e] + len(v) for e, v in self.ops.items()}


D = 1024
DFF = 2816
NFF = DFF // 128
ALPHA = 8 ** 0.25
LN_EPS = 1e-5


def bc_mid(ap2d, n):
    return ap2d.unsqueeze(1).broadcast_to([ap2d.shape[0], n, ap2d.shape[1]])


def emit_ln(P, y, out_f32, out_bf, g_ap, b_ap, ones, sq, st, psA, psB, keyp, N, out_keys, ykey):
    nc = P.nc
    mean, msq, var, rstd = st
    P.I(ACT, 'activation', dict(out=sq[:], in_=y[:], func=AF.Square), reads=[ykey], writes=[keyp + "sq"])
    for c in range(8):
        P.I(PE, 'matmul', MM(psA[:, :N], ones[:], y[:, c, :], start=(c == 0), stop=(c == 7)),
             reads=[ykey, "ones"], writes=[keyp + "psA"])
    for c in range(8):
        P.I(PE, 'matmul', MM(psB[:, :N], ones[:], sq[:, c, :], start=(c == 0), stop=(c == 7)),
             reads=[keyp + "sq", "ones"], writes=[keyp + "psB"])
    P.I(DVE, 'tensor_scalar', dict(out=mean[:], in0=psA[:, :N], scalar1=1.0 / D, scalar2=None, op0=ALU.mult),
         reads=[keyp + "psA"], writes=[keyp + "mean"])
    P.I(DVE, 'tensor_tensor', dict(out=msq[:], in0=mean[:], in1=mean[:], op=ALU.mult),
         reads=[keyp + "mean"], writes=[keyp + "msq"])
    P.I(DVE, 'scalar_tensor_tensor', dict(out=var[:], in0=psB[:, :N], scalar=1.0 / D, in1=msq[:],
                                               op0=ALU.mult, op1=ALU.subtract),
         reads=[keyp + "psB", keyp + "msq"], writes=[keyp + "var"])
    P.I(DVE, 'tensor_scalar', dict(out=var[:], in0=var[:], scalar1=LN_EPS, scalar2=None, op0=ALU.add),
         reads=[keyp + "var"], writes=[keyp + "var"])
    P.I(ACT, 'activation', dict(out=var[:], in_=var[:], func=AF.Sqrt),
         reads=[keyp + "var"], writes=[keyp + "var"])
    P.I(DVE, 'reciprocal', dict(out=rstd[:], in_=var[:]),
         reads=[keyp + "var"], writes=[keyp + "rstd"])
    P.I(DVE, 'tensor_tensor', dict(out=y[:], in0=y[:], in1=bc_mid(mean[:], 8), op=ALU.subtract),
         reads=[ykey, keyp + "mean"], writes=[ykey])
    P.I(DVE, 'tensor_tensor', dict(out=y[:], in0=y[:], in1=bc_mid(rstd[:], 8), op=ALU.mult),
         reads=[ykey, keyp + "rstd"], writes=[ykey])
    for c in range(8):
        P.I(ACT, 'activation', dict(out=out_f32[:, c, :], in_=y[:, c, :], func=AF.Identity,
                                              bias=b_ap[:, c:c + 1], scale=g_ap[:, c:c + 1]),
             reads=[ykey, "lnp"], writes=[out_keys[0] + str(c)])
    if out_bf is not None:
        P.I(POOL, 'tensor_copy', dict(out=out_bf[:], in_=out_f32[:]),
             reads=[out_keys[0] + str(c) for c in range(8)], writes=[out_keys[1]])


def build_P(ntok, N=256, P=None, hsrc=None, masrc=None, mbsrc=None, odst=None, single_mix=False):
    standalone = P is None
    if standalone:
        P = Prog("P")
        P.begin_phase("P")
    nc = P.nc
    if standalone:
        hT = P.dram_in("hT", [D, ntok])
        mA = P.dram_in("mA", [D, ntok])
        mB = P.dram_in("mB", [D, ntok])
    w13 = P.dram_in("w13", [D, 2 * DFF])
    w2 = P.dram_in("w2", [DFF, D])
    lnp = P.dram_in("lnp", [128, 32])
    if standalone:
        outT = P.dram_out("outT", [D, ntok])

    w13s = P.sb([128, 8, 2 * DFF], BF16, "w13s")
    w2s = P.sb([128, NFF, D], BF16, "w2s")
    lnps = P.sb([128, 32], F32, "lnps")
    ones = P.sb([128, 128], F32, "ones")
    hbuf = [P.sb([128, 8, N], F32, f"hb{i}") for i in range(2)]
    mAt = P.sb([128, 8, N], F32, "mAt")
    mBt = P.sb([128, 8, N], F32, "mBt")
    sq = P.sb([128, 8, N], F32, "sq")
    h1 = P.sb([128, 8, N], F32, "h1")
    h1b = P.sb([128, 8, N], BF16, "h1b")
    act = P.sb([128, NFF, N], BF16, "act")
    sg = [P.sb([128, N], F32, f"sg{i}") for i in range(2)]
    st = [P.sb([128, N], F32, f"st{i}") for i in range(4)]
    psum = [P.ps([128, 512], F32, f"psum{i}") for i in range(8)]

    P.I(POOL, 'memset', dict(ap=ones[:], constant=1.0), writes=["ones"])
    P.dma(SP, lnps[:], lnp[:, :], writes=["lnp"], slot="lnp")
    for k in range(8):
        P.dma(POOL, w13s[:, k, :], w13[k * 128:(k + 1) * 128, :], writes=["w13"], slot="w13")
    for j in range(NFF):
        P.dma(POOL, w2s[:, j, :], w2[j * 128:(j + 1) * 128, :], writes=["w2"], slot="w2")
    w13keys = [("w13", k) for k in range(8)]
    w2keys = [("w2", j) for j in range(NFF)]

    if standalone:
        hTv = hT.rearrange("(c p) t -> p c t", p=128)
        mAv = mA.rearrange("(c p) t -> p c t", p=128)
        mBv = mB.rearrange("(c p) t -> p c t", p=128)
        outv = outT.rearrange("(c p) t -> p c t", p=128)
        hsrc = lambda t0, n: hTv[:, :, t0:t0 + n]
        masrc = lambda t0, n: mAv[:, :, t0:t0 + n]
        mbsrc = lambda t0, n: mBv[:, :, t0:t0 + n]
        odst = lambda t0, n: outv[:, :, t0:t0 + n]

    ng = ntok // N
    for g in range(ng):
        s = g % 2
        t0 = g * N
        hb, ma, mb, o, y2 = hbuf[s], mAt, mBt, mBt, mAt
        hk, mak, mbk = f"h{s}", "y2", "mB"
        P.dma(SP, hb[:], hsrc(t0, N), writes=[hk], slot=f"ldh{s}")
        P.dma(SP, ma[:], masrc(t0, N), writes=[mak], slot=f"ldA{s}")
        if not single_mix:
            P.dma(SP, mb[:], mbsrc(t0, N), writes=[mbk] + ["mB_" + str(c) for c in range(8)], slot=f"ldB{s}")
        P.I(DVE, 'scalar_tensor_tensor', dict(out=hb[:], in0=hb[:], scalar=ALPHA, in1=ma[:],
                                                                 op0=ALU.mult, op1=ALU.add),
             reads=[hk, mak], writes=[hk])
        if not single_mix:
            P.I(POOL, 'tensor_tensor', dict(out=hb[:], in0=hb[:], in1=mb[:], op=ALU.add),
                reads=[hk, mbk], writes=[hk])
        emit_ln(P, hb, h1, h1b, lnps[:, 0:8], lnps[:, 8:16], ones, sq, st, psum[0], psum[1], "ln", N,
                ("h1_", "h1b"), hk)
        for j in range(NFF):
            pg = psum[2 + (j % 2)]
            pu = psum[4 + (j % 2)]
            pgk, puk = f"pg{j % 2}", f"pu{j % 2}"
            for k in range(8):
                P.I(PE, 'matmul', MM(pg[:, :N], w13s[:, k, j * 128:(j + 1) * 128], h1b[:, k, :],
                                                            start=(k == 0), stop=(k == 7)),
                     reads=["w13", "h1b"], writes=[pgk])
            for k in range(8):
                P.I(PE, 'matmul', MM(pu[:, :N], w13s[:, k, DFF + j * 128:DFF + (j + 1) * 128],
                                                            h1b[:, k, :], start=(k == 0), stop=(k == 7)),
                     reads=["w13", "h1b"], writes=[puk])
            sgt = sg[j % 2]
            P.I(ACT, 'activation', dict(out=sgt[:], in_=pg[:, :N], func=AF.Silu),
                 reads=[pgk], writes=[f"sg{j % 2}"])
            P.I(DVE, 'tensor_tensor', dict(out=act[:, j, :], in0=pu[:, :N], in1=sgt[:], op=ALU.mult),
                 reads=[puk, f"sg{j % 2}"], writes=[("act", j)])
        for c in range(8):
            pd = psum[6 + (c % 2)]
            pdk = f"pd{c % 2}"
            for j in range(NFF):
                P.I(PE, 'matmul', MM(pd[:, :N], w2s[:, j, c * 128:(c + 1) * 128], act[:, j, :],
                                                            start=(j == 0), stop=(j == NFF - 1)),
                     reads=["w2", ("act", j)], writes=[pdk])
            P.I(DVE, 'scalar_tensor_tensor', dict(out=y2[:, c, :], in0=h1[:, c, :], scalar=ALPHA,
                                                                   in1=pd[:, :N], op0=ALU.mult, op1=ALU.add),
                 reads=[pdk, "h1_" + str(c)], writes=["y2"])
        emit_ln(P, y2, o, None, lnps[:, 16:24], lnps[:, 24:32], ones, sq, st, psum[0], psum[1], "ln", N,
                ("mB_", None), "y2")
        P.dma(SP, odst(t0, N), o[:], reads=["mB_" + str(c) for c in range(8)] + ["mB"], writes=[("out", g)],
              slot=f"st{s}")
    if standalone:
        print("P ops:", P.stats())
        return P.finish()
    P.end_phase()


def ref_P(hT, mA, mB, w13, w2, g1, b1, g2, b2):
    import ml_dtypes
    bf = lambda a: a.astype(ml_dtypes.bfloat16).astype(np.float32)

    def ln(x, g, b):
        mu = x.mean(-1, keepdims=True)
        var = ((x - mu) ** 2).mean(-1, keepdims=True)
        return (x - mu) / np.sqrt(var + LN_EPS) * g + b
    h = hT.T
    y = ALPHA * h + mA.T + mB.T
    h1 = ln(y, g1, b1)
    gu = bf(h1) @ bf(w13)
    gg, uu = gu[:, :DFF], gu[:, DFF:]
    a = gg / (1 + np.exp(-gg)) * uu
    f = bf(a) @ bf(w2)
    return ln(ALPHA * h1 + f, g2, b2).T


D = 1024
T = 8192
RET_DK, RET_DV, RET_HEADS, RET_CHUNK = 256, 512, 4, 128
LN_EPS = 1e-5
XPOS_BASE = 10000.0


def ret_tables(nh_local_ids, T):
    inv_freq = np.power(np.float32(XPOS_BASE), -np.linspace(0.0, 1.0, RET_DK // 2, dtype=np.float32)).astype(np.float32)
    t = np.arange(T, dtype=np.float32)
    ang = (t[:, None] * inv_freq[None, :]).astype(np.float32)
    cos = np.cos(ang).astype(np.float32).T
    sin = np.sin(ang).astype(np.float32).T
    pos = (np.arange(T) % RET_CHUNK).astype(np.float64)
    tabs = []
    gam = []
    for h in nh_local_ids:
        lg = np.log1p(-np.exp2(-5.0 - h))
        dq = np.exp(lg * (pos + 1.0))
        dk = np.exp(-lg * (pos + 1.0)) * RET_DK ** -0.5
        tabs += [cos * dq, sin * dq, cos * dk, sin * dk]
        gam.append(float(np.exp(lg * RET_CHUNK)))
    return np.ascontiguousarray(np.stack(tabs).astype(np.float32)), gam


def build_ret(T, heads=None, G=512, P=None, xsrc=None, mdst=None):
    standalone = P is None
    if standalone:
        P = Prog("ret")
        P.begin_phase("ret")
    nc = P.nc
    NT = G // 128
    xT = P.dram_in("xT", [D, T]) if xsrc is None else None
    wq = P.dram_in("wq", [D, 512])
    wk = P.dram_in("wk", [D, 512])
    wv = P.dram_in("wv", [D, 1024])
    wg = P.dram_in("wg", [D, 1024])
    wo = P.dram_in("wo", [1024, D])
    gng = P.dram_in("gng", [128, 1024])
    tabs = P.dram_in("tabs", [8, 128, T])
    cmask = P.dram_in("cmask", [128, 128])
    identd = P.dram_in("ident", [128, 128])
    mixT = P.dram_out("mixT", [D, T]) if mdst is None else None
    gamd = P.dram_in("gam", [128, 2])

    wqs = P.sb([128, 8, 512], BF16, "wqs")
    wks = P.sb([128, 8, 512], BF16, "wks")
    wvs = P.sb([128, 8, 1024], BF16, "wvs")
    wgs = P.sb([128, 8, 1024], BF16, "wgs")
    wos = P.sb([128, 8, D], BF16, "wos")
    gns = P.sb([128, 1024], F32, "gns")
    msk = P.sb([128, 128], F32, "msk")
    gams = P.sb([128, 2], F32, "gams")
    idb = P.sb([128, 128], BF16, "idb")
    xb = [P.sb([128, 8, G], BF16, f"xb{i}") for i in range(2)]
    tb = [P.sb([128, 8, G], F32, f"tb{i}") for i in range(2)]
    qkT = [[P.sb([128, 2, G], BF16, f"qkT{h}{w}") for w in range(2)] for h in range(2)]
    rt = [P.sb([128, G], F32, f"rt{i}") for i in range(4)]
    ktok = [P.sb([128, 256], BF16, f"ktok{i}") for i in range(2)]
    vt = [P.sb([128, 512], BF16, f"vt{i}") for i in range(2)]
    gt = [P.sb([128, 512], F32, f"gt{i}") for i in range(2)]
    sTm = [P.sb([128, 128], BF16, f"sTm{i}") for i in range(2)]
    S32 = [P.sb([128, 2, 512], F32, f"S32_{h}") for h in range(2)]
    Sb = [P.sb([128, 2, 512], BF16, f"Sb_{h}") for h in range(2)]
    on = [P.sb([128, 512], F32, f"on{i}") for i in range(2)]
    ofin = [P.sb([128, 512], BF16, f"ofin{i}") for i in range(2)]
    junk = P.sb([128, 512], F32, "junk")
    stat = [P.sb([128, 8], F32, f"stat{i}") for i in range(2)]
    oT = [P.sb([128, 8, G], BF16, f"oT{i}") for i in range(2)]
    mo = [P.sb([128, G], F32, f"mo{i}") for i in range(2)]
    psum = [P.ps([128, 512], F32, f"psum{i}") for i in range(6)]
    pst = [P.ps([128, 1024], BF16, f"pst{i}") for i in range(2)]

    P.dma(SP, gns[:], gng[:, :], writes=["gns"], slot="c0")
    P.dma(SP, msk[:], cmask[:, :], writes=["msk"], slot="c1")
    P.dma(SP, gams[:], gamd[:, :], writes=["gams"], slot="c3")
    P.dma(POOL, idb[:], identd[:, :], writes=["idb"], slot="c2")
    for k in range(8):
        r = slice(k * 128, (k + 1) * 128)
        P.dma(POOL, wqs[:, k, :], wq[r, :], writes=["wq"], slot="wq")
        P.dma(POOL, wks[:, k, :], wk[r, :], writes=["wk"], slot="wk")
        P.dma(POOL, wvs[:, k, :], wv[r, :], writes=["wv"], slot="wv")
        P.dma(POOL, wgs[:, k, :], wg[r, :], writes=["wg"], slot="wg")
        P.dma(POOL, wos[:, k, :], wo[r, :], writes=["wo"], slot="wo")
    for h in range(2):
        P.I(POOL, 'memset', dict(ap=S32[h][:], constant=0.0), writes=[f"S32_{h}"])
        P.I(POOL, 'memset', dict(ap=Sb[h][:], constant=0.0), writes=[f"Sb_{h}"])

    if xsrc is None:
        xTv = xT.rearrange("(c p) t -> p c t", p=128)
        xsrc = lambda t0, n: xTv[:, :, t0:t0 + n]
    tabv = tabs.rearrange("n p t -> p n t")
    if mdst is None:
        mixv = mixT.rearrange("(c p) t -> p c t", p=128)
        mdst = lambda c, t0, n: mixv[:, c, t0:t0 + n]

    pp = [0]

    def proj_bank():
        pp[0] ^= 1
        return psum[pp[0]], f"pp{pp[0]}"

    cnt = [0]
    ng = T // G
    for g in range(ng):
        s = g % 2
        t0 = g * G
        xbt, tbt, oTt = xb[s], tb[s], oT[s]
        xk, tk, oTk = f"xb{s}", f"tb{s}", f"oT{s}"
        P.dma(POOL, xbt[:], xsrc(t0, G), writes=[xk], slot=f"ldx{s}")
        P.dma(SP, tbt[:], tabv[:, :, t0:t0 + G], writes=[tk], slot=f"ldt{s}")
        for h in range(2):
            for w, (ws, wkey) in enumerate(((wqs, "wq"), (wks, "wk"))):
                banks = []
                for dc in range(2):
                    pb, pbk = proj_bank()
                    col = h * 256 + dc * 128
                    for k in range(8):
                        P.I(PE, 'matmul', MM(
                            pb[:, :G], ws[:, k, col:col + 128], xbt[:, k, :], start=(k == 0), stop=(k == 7)),
                            reads=[wkey, xk], writes=[pbk])
                    banks.append((pb, pbk))
                (p1, p1k), (p2, p2k) = banks
                ct = tbt[:, h * 4 + w * 2 + 0, :]
                sn = tbt[:, h * 4 + w * 2 + 1, :]
                dst = qkT[h][w]
                dk_ = f"qkT{h}{w}"
                a, b, c_, d_ = rt
                P.I(DVE, 'tensor_tensor', dict(out=a[:], in0=p1[:, :G], in1=ct, op=ALU.mult),
                     reads=[p1k, tk], writes=["rt0"])
                P.I(DVE, 'tensor_tensor', dict(out=b[:], in0=p2[:, :G], in1=sn, op=ALU.mult),
                     reads=[p2k, tk], writes=["rt1"])
                P.I(DVE, 'tensor_tensor', dict(out=c_[:], in0=p1[:, :G], in1=sn, op=ALU.mult),
                     reads=[p1k, tk], writes=["rt2"])
                P.I(DVE, 'tensor_tensor', dict(out=d_[:], in0=p2[:, :G], in1=ct, op=ALU.mult),
                     reads=[p2k, tk], writes=["rt3"])
                P.I(POOL, 'tensor_tensor', dict(out=dst[:, 0, :], in0=a[:], in1=b[:], op=ALU.subtract),
                     reads=["rt0", "rt1"], writes=[dk_])
                P.I(POOL, 'tensor_tensor', dict(out=dst[:, 1, :], in0=c_[:], in1=d_[:], op=ALU.add),
                     reads=["rt2", "rt3"], writes=[dk_])
            qT, kT = qkT[h]
            qk_, kk_ = f"qkT{h}0", f"qkT{h}1"
            for ti in range(NT):
                cnt[0] += 1
                u = cnt[0] % 2
                tsl = slice(ti * 128, (ti + 1) * 128)
                pb, pbk = proj_bank()
                for k in range(8):
                    P.I(PE, 'matmul', MM(pb[:, :], xbt[:, k, tsl], wvs[:, k, h * 512:(h + 1) * 512],
                                                            start=(k == 0), stop=(k == 7)),
                         reads=["wv", xk], writes=[pbk])
                P.I(ACT, 'activation', dict(out=vt[u][:], in_=pb[:, :], func=AF.Copy),
                     reads=[pbk], writes=[f"vt{u}"])
                pb, pbk = proj_bank()
                for k in range(8):
                    P.I(PE, 'matmul', MM(pb[:, :], xbt[:, k, tsl], wgs[:, k, h * 512:(h + 1) * 512],
                                                            start=(k == 0), stop=(k == 7)),
                         reads=["wg", xk], writes=[pbk])
                P.I(ACT, 'activation', dict(out=gt[u][:], in_=pb[:, :], func=AF.Silu),
                     reads=[pbk], writes=[f"gt{u}"])
                ptr, ptrk = pst[0], "pst0"
                for dc in range(2):
                    P.I(PE, 'transpose', dict(out=ptr[:, dc * 128:(dc + 1) * 128], in_=kT[:, dc, tsl], identity=idb[:]),
                         reads=[kk_, "idb"], writes=[ptrk])
                P.I(DVE, 'tensor_copy', dict(out=ktok[u][:], in_=ptr[:, 0:256]),
                     reads=[ptrk], writes=[f"ktok{u}"])
                psc, psck = psum[2], "psc"
                for dc in range(2):
                    P.I(PE, 'matmul', MM(psc[:, :128], kT[:, dc, tsl], qT[:, dc, tsl], start=(dc == 0), stop=(dc == 1)),
                         reads=[kk_, qk_], writes=[psck])
                P.I(DVE, 'tensor_tensor', dict(out=sTm[u][:], in0=psc[:, :128], in1=msk[:], op=ALU.mult),
                     reads=[psck, "msk"], writes=[f"sTm{u}"])
                po, pok = psum[3], "po"
                P.I(PE, 'matmul', MM(po[:, :], sTm[u][:], vt[u][:], start=True, stop=False),
                     reads=[f"sTm{u}", f"vt{u}"], writes=[pok])
                for dc in range(2):
                    P.I(PE, 'matmul', MM(po[:, :], qT[:, dc, tsl], Sb[h][:, dc, :], start=False, stop=(dc == 1)),
                         reads=[qk_, f"Sb_{h}"], writes=[pok])
                for dc in range(2):
                    pS, pSk = psum[4 + dc], f"pS{dc}"
                    P.I(PE, 'matmul', MM(pS[:, :], ktok[u][:, dc * 128:(dc + 1) * 128], vt[u][:], start=True, stop=True),
                         reads=[f"ktok{u}", f"vt{u}"], writes=[pSk])
                    P.I(DVE, 'tensor_tensor', dict(out=S32[h][:, dc, :], in0=S32[h][:, dc, :], in1=pS[:, :], op=ALU.add),
                         reads=[pSk, f"S32_{h}"], writes=[f"S32_{h}"])
                    P.I(ACT, 'activation', dict(out=S32[h][:, dc, :], in_=S32[h][:, dc, :], func=AF.Copy, scale=gams[:, h:h + 1]),
                         reads=[f"S32_{h}", "gams"], writes=[f"S32_{h}"])
                    P.I(POOL, 'tensor_copy', dict(out=Sb[h][:, dc, :], in_=S32[h][:, dc, :]),
                         reads=[f"S32_{h}"], writes=[f"Sb_{h}"])
                stt, stk = stat[u], f"stat{u}"
                P.I(ACT, 'activation', dict(out=junk[:], in_=po[:, :], func=AF.Copy, accum_out=stt[:, 0:1]),
                     reads=[pok], writes=[stk + "a"])
                P.I(ACT, 'activation', dict(out=junk[:], in_=po[:, :], func=AF.Square, accum_out=stt[:, 1:2]),
                     reads=[pok], writes=[stk + "b"])
                P.I(DVE, 'tensor_scalar', dict(out=stt[:, 2:3], in0=stt[:, 0:1], scalar1=1.0 / 512, scalar2=None, op0=ALU.mult),
                     reads=[stk + "a"], writes=[stk + "c"])
                P.I(DVE, 'tensor_tensor', dict(out=stt[:, 3:4], in0=stt[:, 2:3], in1=stt[:, 2:3], op=ALU.mult),
                     reads=[stk + "c"], writes=[stk + "d"])
                P.I(DVE, 'scalar_tensor_tensor', dict(out=stt[:, 4:5], in0=stt[:, 1:2], scalar=1.0 / 512, in1=stt[:, 3:4],
                                                                     op0=ALU.mult, op1=ALU.subtract),
                     reads=[stk + "b", stk + "d"], writes=[stk + "e"])
                P.I(DVE, 'tensor_scalar', dict(out=stt[:, 4:5], in0=stt[:, 4:5], scalar1=LN_EPS, scalar2=None, op0=ALU.add),
                     reads=[stk + "e"], writes=[stk + "e"])
                P.I(ACT, 'activation', dict(out=stt[:, 5:6], in_=stt[:, 4:5], func=AF.Sqrt),
                     reads=[stk + "e"], writes=[stk + "f"])
                P.I(DVE, 'reciprocal', dict(out=stt[:, 6:7], in_=stt[:, 5:6]),
                     reads=[stk + "f"], writes=[stk + "g"])
                P.I(DVE, 'scalar_tensor_tensor', dict(out=stt[:, 7:8], in0=stt[:, 2:3], scalar=-1.0, in1=stt[:, 6:7],
                                                                     op0=ALU.mult, op1=ALU.mult),
                     reads=[stk + "c", stk + "g"], writes=[stk + "h"])
                P.I(ACT, 'activation', dict(out=on[u][:], in_=po[:, :], func=AF.Identity,
                                                              bias=stt[:, 7:8], scale=stt[:, 6:7]),
                     reads=[pok, stk + "g", stk + "h"], writes=[f"on{u}"])
                P.I(POOL, 'tensor_tensor', dict(out=on[u][:], in0=on[u][:], in1=gns[:, h * 512:(h + 1) * 512], op=ALU.mult),
                     reads=[f"on{u}", "gns"], writes=[f"on{u}"])
                P.I(DVE, 'tensor_tensor', dict(out=ofin[u][:], in0=on[u][:], in1=gt[u][:], op=ALU.mult),
                     reads=[f"on{u}", f"gt{u}"], writes=[f"ofin{u}"])
                ptr2, ptr2k = pst[1], "pst1"
                for fc in range(4):
                    P.I(PE, 'transpose', dict(out=ptr2[:, fc * 128:(fc + 1) * 128], in_=ofin[u][:, fc * 128:(fc + 1) * 128],
                                                              identity=idb[:]),
                         reads=[f"ofin{u}", "idb"], writes=[ptr2k])
                P.I(ACT, 'activation', dict(out=oTt[:, h * 4:(h + 1) * 4, tsl],
                                                 in_=ptr2[:, 0:512].rearrange("p (c t) -> p c t", c=4), func=AF.Copy),
                     reads=[ptr2k], writes=[(oTk, h, ti)])
        okeys = [(oTk, h, ti) for h in range(2) for ti in range(NT)]
        for c in range(8):
            pw, pwk = proj_bank()
            for k in range(8):
                P.I(PE, 'matmul', MM(pw[:, :G], wos[:, k, c * 128:(c + 1) * 128], oTt[:, k, :],
                                                            start=(k == 0), stop=(k == 7)),
                     reads=["wo"] + okeys, writes=[pwk])
            m = mo[c % 2]
            mk = f"mo{c % 2}"
            P.I(ACT, 'activation', dict(out=m[:], in_=pw[:, :G], func=AF.Copy),
                 reads=[pwk], writes=[mk])
            P.dma(SP, mdst(c, t0, G), m[:], reads=[mk], writes=[("mix", g, c)], slot=f"st{c % 2}")
    print("ret ops:", P.stats())
    if standalone:
        return P.finish()
    P.end_phase()


def ref_ret(x, w_in_h, gn_g_h, w_out_h, heads):
    T = x.shape[0]
    nh = len(heads)
    x = x.astype(np.float64)
    proj = x @ w_in_h.astype(np.float64)
    q = proj[:, :nh * 256].reshape(T, nh, 256)
    k = proj[:, nh * 256:2 * nh * 256].reshape(T, nh, 256)
    v = proj[:, 2 * nh * 256:2 * nh * 256 + nh * 512].reshape(T, nh, 512)
    gate = proj[:, 2 * nh * 256 + nh * 512:].reshape(T, nh, 512)
    inv_freq = np.power(XPOS_BASE, -np.linspace(0.0, 1.0, 128))
    ang = np.arange(T)[:, None] * inv_freq[None, :]
    cos, sin = np.cos(ang)[:, None, :], np.sin(ang)[:, None, :]

    def rot(a):
        a1, a2 = a[..., :128], a[..., 128:]
        return np.concatenate([a1 * cos - a2 * sin, a1 * sin + a2 * cos], -1)
    q = rot(q)
    k = rot(k) * 256 ** -0.5
    outs = []
    for i, h in enumerate(heads):
        lg = np.log1p(-np.exp2(-5.0 - h))
        gamma = np.exp(lg)
        S = np.zeros((256, 512))
        o = np.zeros((T, 512))
        for t in range(T):
            S = gamma * S + np.outer(k[t, i], v[t, i])
            o[t] = q[t, i] @ S
        mu = o.mean(-1, keepdims=True)
        var = ((o - mu) ** 2).mean(-1, keepdims=True)
        o = (o - mu) / np.sqrt(var + LN_EPS) * gn_g_h[i * 512:(i + 1) * 512]
        g = gate[:, i]
        outs.append(o * (g / (1 + np.exp(-g))))
    o = np.concatenate(outs, -1)
    return o @ w_out_h.astype(np.float64)


D = 1024
NORM_EPS = 1e-6
NEG = -30000.0


def gdn_consts():
    r = np.arange(128)
    c = {}
    c["ident"] = np.eye(128, dtype=np.float32)
    c["i2"] = (2.0 * np.eye(128)).astype(np.float32)
    c["triu"] = (r[:, None] <= r[None, :]).astype(np.float32)
    c["negm"] = np.where(r[:, None] <= r[None, :], 0.0, NEG).astype(np.float32)
    c["strict"] = (r[:, None] < r[None, :]).astype(np.float32)
    bd = []
    for l in range(1, 8):
        b = 1 << l
        bd.append(((r[:, None] // b) == (r[None, :] // b)).astype(np.float32))
    c["bd"] = np.ascontiguousarray(np.stack(bd, 1))
    return c


STAGE = [9]


def build_gdn(T, G=512, P=None, xsrc=None, mdst=None):
    standalone = P is None
    if standalone:
        P = Prog("gdn")
        P.begin_phase("gdn")
    NT = G // 128
    NH = 4
    xT = P.dram_in("xT", [D, T]) if xsrc is None else None
    wqkvz = P.dram_in("wqkvz", [D, 2048])
    wba = P.dram_in("wba", [D, 8])
    cw = P.dram_in("cw", [128, 12, 4])
    hp = P.dram_in("hp", [128, 8])
    ngt = P.dram_in("ngt", [128, 512])
    wo = P.dram_in("wo", [512, D])
    cd = {k: P.dram_in("c_" + k, list(v.shape)) for k, v in gdn_consts().items()}
    mixT = P.dram_out("mixT", [D, T]) if mdst is None else None

    ws = P.sb([128, 8, 2048], BF16, "ws")
    wbas = P.sb([128, 8, 8], BF16, "wbas")
    wos = P.sb([128, 4, D], BF16, "wos")
    cws = P.sb([128, 12, 4], F32, "cws")
    hps = P.sb([128, 8], F32, "hps")
    negA = P.sb([128, 4], F32, "negA")
    ngs = P.sb([128, 512], F32, "ngs")
    ident = P.sb([128, 128], F32, "ident")
    identb = P.sb([128, 128], BF16, "identb")
    i2 = P.sb([128, 128], F32, "i2")
    ones = P.sb([128, 128], F32, "ones")
    triu = P.sb([128, 128], F32, "triu")
    negm = P.sb([128, 128], F32, "negm")
    strict = P.sb([128, 128], F32, "strict")
    bd = P.sb([128, 7, 128], F32, "bd")
    xb = [P.sb([128, 8, G], BF16, f"xb{i}") for i in range(2)]
    pc = P.sb([128, 12, G + 3], F32, "pc")
    cacc = [P.sb([128, G], F32, f"cacc{i}") for i in range(2)]
    qkf = P.sb([128, G], F32, "qkf")
    sqt = P.sb([128, G], F32, "sqt")
    rin = P.sb([128, G], F32, "rin")
    qT = P.sb([128, NH, G], BF16, "qT")
    kT = P.sb([128, NH, G], BF16, "kT")
    vTf = P.sb([128, NH, G], F32, "vTf")
    zs = [P.sb([128, 512], F32, f"zs{i}") for i in range(2)]
    sm = [P.sb([128, 48], F32, f"sm{i}") for i in range(2)]
    def t4(name, dt=F32, n=1):
        return [P.sb([128, NH, 128], dt, f"{name}_{i}") for i in range(n)]
    gTri4 = t4("gTri")[0]; ngTri4 = t4("ngTri")[0]; ET4 = t4("ET")[0]; ETs4 = t4("ETs")[0]; TT4 = t4("TT")[0]; TL4 = t4("TL")[0]
    attnT4 = t4("attnT", BF16, 2)
    Nn4 = t4("Nn", F32, 2); Mm4 = t4("Mm", F32, 2); Pn4 = t4("Pn")[0]; Pm4 = t4("Pm")[0]
    Nfin4 = t4("Nfin", F32, 2)
    Vtok4 = t4("Vtok", F32, 2); Vres4 = t4("Vres")[0]; Vn4 = t4("Vn", BF16)[0]; tQS4 = t4("tQS")[0]; osb4 = t4("osb")[0]
    Kdec4 = t4("Kdec", BF16, 2); tmp4 = t4("tmp")[0]
    S32 = P.sb([128, NH, 128], F32, "S32")
    Sb = P.sb([128, NH, 128], BF16, "Sb")
    junk = P.sb([128, 128], F32, "junk")
    ofin = [P.sb([128, 512], BF16, f"ofin{i}") for i in range(2)]
    oT = [P.sb([128, NH, G], BF16, f"oT{i}") for i in range(2)]
    mo = [P.sb([128, G], F32, f"mo{i}") for i in range(2)]
    psum = [P.ps([128, 512], F32, f"psum{i}") for i in range(7)]
    psb = P.ps([128, 1024], BF16, "psb")

    P.dma(SP, cws[:], cw[:, :, :], writes=["cws"], slot="c0_1")
    P.dma(SP, hps[:], hp[:, :], writes=["hps"], slot="c0_2")
    P.dma(SP, ngs[:], ngt[:, :], writes=["ngs"], slot="c0_3")
    for nm, t in (("ident", ident), ("i2", i2), ("triu", triu), ("negm", negm), ("strict", strict)):
        P.dma(SP, t[:], cd[nm][:, :], writes=[nm], slot="c_" + nm)
    P.dma(SP, bd[:], cd["bd"][:, :, :], writes=["bd"], slot="c0_5")
    P.dma(POOL, identb[:], cd["ident"][:, :], writes=["identb"], slot="c1")
    P.I(POOL, 'memset', dict(ap=ones[:], constant=1.0), writes=["ones"])
    P.I(POOL, 'memset', dict(ap=pc[:], constant=0.0), writes=["pc"])
    P.I(POOL, 'memset', dict(ap=S32[:], constant=0.0), writes=["S32"])
    P.I(POOL, 'memset', dict(ap=Sb[:], constant=0.0), writes=["Sb"])
    for k in range(8):
        r = slice(k * 128, (k + 1) * 128)
        P.dma(POOL, ws[:, k, :], wqkvz[r, :], writes=["ws"], slot="w0")
        P.dma(POOL, wbas[:, k, :], wba[r, :], writes=["wbas"], slot="w1")
    for k in range(4):
        P.dma(POOL, wos[:, k, :], wo[k * 128:(k + 1) * 128, :], writes=["wos"], slot="w2")
    P.I(ACT, 'activation', dict(out=negA[:], in_=hps[:, 0:4], func=AF.Exp), reads=["hps"], writes=["negA"])
    P.I(DVE, 'tensor_scalar', dict(out=negA[:], in0=negA[:], scalar1=-1.0, scalar2=None, op0=ALU.mult),
        reads=["negA"], writes=["negA"])

    if xsrc is None:
        xTv = xT.rearrange("(c p) t -> p c t", p=128)
        xsrc = lambda t0, n: xTv[:, :, t0:t0 + n]
    if mdst is None:
        mixv = mixT.rearrange("(c p) t -> p c t", p=128)
        mdst = lambda c, t0, n: mixv[:, c, t0:t0 + n]
    pp = [0]

    def proj_bank():
        pp[0] ^= 1
        return psum[pp[0]], f"pp{pp[0]}"

    PN = psum[2:4]
    PSET = psum[4]
    PREC = psum[5]
    PMISC = psum[6]

    tcount = [0]
    ng = T // G
    for g in range(ng):
        s = g % 2
        t0 = g * G
        xbt, xk = xb[s], f"xb{s}"
        oTt, oTk = oT[s], f"oT{s}"
        P.dma(POOL, xbt[:], xsrc(t0, G), writes=[xk], slot=f"ldx{s}")
        for ch in range(12):
            kind, h = ch // 4, ch % 4
            pb, pbk = proj_bank()
            col = kind * 512 + h * 128
            for k in range(8):
                P.I(PE, 'matmul', MM(pb[:, :G], ws[:, k, col:col + 128], xbt[:, k, :], start=(k == 0), stop=(k == 7)),
                    reads=["ws", xk], writes=[pbk])
            pck = ("pc", ch)
            P.I(ACT, 'activation', dict(out=pc[:, ch, 3:3 + G], in_=pb[:, :G], func=AF.Copy), reads=[pbk, "pc"], writes=[pck])
            ca = cacc[ch % 2]
            cak = f"cacc{ch % 2}"
            P.I(DVE, 'tensor_scalar', dict(out=ca[:], in0=pc[:, ch, 0:G], scalar1=cws[:, ch, 0:1], scalar2=None, op0=ALU.mult),
                reads=[pck, "cws"], writes=[cak])
            for j in range(1, 4):
                P.I(DVE, 'scalar_tensor_tensor', dict(out=ca[:], in0=pc[:, ch, j:j + G], scalar=cws[:, ch, j:j + 1], in1=ca[:],
                                                      op0=ALU.mult, op1=ALU.add), reads=[pck, cak], writes=[cak])
            P.I(POOL, 'tensor_copy', dict(out=pc[:, ch, 0:3], in_=pc[:, ch, G:G + 3]), reads=[pck, cak], writes=[pck])
            if kind == 2:
                P.I(ACT, 'activation', dict(out=vTf[:, h, :], in_=ca[:], func=AF.Silu), reads=[cak], writes=[("vTf", h)])
            else:
                dst, dk_ = (qT, ("qT", h)) if kind == 0 else (kT, ("kT", h))
                P.I(ACT, 'activation', dict(out=qkf[:], in_=ca[:], func=AF.Silu), reads=[cak], writes=["qkf"])
                P.I(POOL, 'tensor_tensor', dict(out=sqt[:], in0=qkf[:], in1=qkf[:], op=ALU.mult), reads=["qkf"], writes=["sqt"])
                pq, pqk = proj_bank()
                P.I(PE, 'matmul', MM(pq[:, :G], ones[:], sqt[:]), reads=["ones", "sqt"], writes=[pqk])
                P.I(DVE, 'tensor_scalar', dict(out=rin[:], in0=pq[:, :G], scalar1=NORM_EPS, scalar2=None, op0=ALU.add),
                    reads=[pqk], writes=["rin"])
                P.I(ACT, 'activation', dict(out=rin[:], in_=rin[:], func=AF.Sqrt), reads=["rin"], writes=["rin"])
                P.I(DVE, 'reciprocal', dict(out=rin[:], in_=rin[:]), reads=["rin"], writes=["rin"])
                scl = 128 ** -0.5 if kind == 0 else 1.0
                P.I(DVE, 'scalar_tensor_tensor', dict(out=dst[:, h, :], in0=qkf[:], scalar=scl, in1=rin[:], op0=ALU.mult, op1=ALU.mult),
                    reads=["qkf", "rin"], writes=[dk_])
        for ti in range(NT if STAGE[0] >= 2 else 0):
            tcount[0] += 1
            u = tcount[0] % 2
            tsl = slice(ti * 128, (ti + 1) * 128)
            smt, smk = sm[u], f"sm{u}"
            pb, pbk = proj_bank()
            for k in range(8):
                P.I(PE, 'matmul', MM(pb[:, :], xbt[:, k, tsl], ws[:, k, 1536:2048], start=(k == 0), stop=(k == 7)),
                    reads=["ws", xk], writes=[pbk])
            P.I(ACT, 'activation', dict(out=zs[u][:], in_=pb[:, :], func=AF.Silu), reads=[pbk], writes=[f"zs{u}"])
            P.I(POOL, 'tensor_tensor', dict(out=zs[u][:], in0=zs[u][:], in1=ngs[:], op=ALU.mult), reads=[f"zs{u}", "ngs"], writes=[f"zs{u}"])
            for k in range(8):
                P.I(PE, 'matmul', MM(PMISC[:, 256:264], xbt[:, k, tsl], wbas[:, k, :], start=(k == 0), stop=(k == 7)),
                    reads=["wbas", xk], writes=["B6"])
            P.I(ACT, 'activation', dict(out=smt[:, 0:4], in_=PMISC[:, 256:260], func=AF.Sigmoid), reads=["B6"], writes=[(smk, "beta")])
            P.I(DVE, 'tensor_tensor', dict(out=smt[:, 32:36], in0=PMISC[:, 260:264], in1=hps[:, 4:8], op=ALU.add),
                reads=["B6", "hps"], writes=[(smk, "tmp")])
            P.I(ACT, 'activation', dict(out=smt[:, 32:36], in_=smt[:, 32:36], func=AF.Exp), reads=[(smk, "tmp")], writes=[(smk, "tmp")])
            P.I(ACT, 'activation', dict(out=smt[:, 32:36], in_=smt[:, 32:36], func=AF.Ln, bias=ones[:, 0:1]), reads=[(smk, "tmp"), "ones"], writes=[(smk, "tmp")])
            P.I(DVE, 'tensor_tensor', dict(out=smt[:, 4:8], in0=smt[:, 32:36], in1=negA[:], op=ALU.mult),
                reads=[(smk, "tmp"), "negA"], writes=[(smk, "g")])
            P.I(PE, 'matmul', MM(PMISC[:, 264:268], triu[:], smt[:, 4:8]), reads=["triu", (smk, "g")], writes=["B6"])
            P.I(PE, 'matmul', MM(PMISC[:, 268:272], ones[:], smt[:, 4:8]), reads=["ones", (smk, "g")], writes=["B6"])
            P.I(DVE, 'tensor_copy', dict(out=smt[:, 8:12], in_=PMISC[:, 264:268]), reads=["B6"], writes=[(smk, "gc")])
            P.I(DVE, 'tensor_scalar', dict(out=smt[:, 12:16], in0=PMISC[:, 264:268], scalar1=-1.0, scalar2=None, op0=ALU.mult),
                reads=["B6"], writes=[(smk, "negc")])
            P.I(ACT, 'activation', dict(out=smt[:, 16:20], in_=PMISC[:, 264:268], func=AF.Exp), reads=["B6"], writes=[(smk, "egc")])
            P.I(DVE, 'tensor_scalar', dict(out=smt[:, 20:24], in0=smt[:, 16:20], scalar1=-1.0, scalar2=None, op0=ALU.mult),
                reads=[(smk, "egc")], writes=[(smk, "negegc")])
            P.I(DVE, 'tensor_tensor', dict(out=smt[:, 24:28], in0=PMISC[:, 268:272], in1=smt[:, 8:12], op=ALU.subtract),
                reads=["B6", (smk, "gc")], writes=[(smk, "kdecs")])
            P.I(ACT, 'activation', dict(out=smt[:, 24:28], in_=smt[:, 24:28], func=AF.Exp), reads=[(smk, "kdecs")], writes=[(smk, "kdecs")])
            P.I(ACT, 'activation', dict(out=smt[:, 28:32], in_=PMISC[:, 268:272], func=AF.Exp), reads=["B6"], writes=[(smk, "etot")])

            if STAGE[0] < 3:
                continue
            B2, B3, B4, B5, B7 = psum[2], psum[3], psum[4], psum[5], psb
            v4 = lambda bank: bank[:, :].rearrange("p (h t) -> p h t", h=NH)
            bin_ = lambda ap: ap.unsqueeze(2).broadcast_to([128, NH, 128])
            bmid = lambda ap: ap.unsqueeze(1).broadcast_to([128, NH, 128])
            kks = [("kT", h) for h in range(NH)]
            qks = [("qT", h) for h in range(NH)]
            P.I(POOL, 'tensor_tensor', dict(out=gTri4[:], in0=bmid(triu[:]), in1=bin_(smt[:, 4:8]), op=ALU.mult),
                reads=["triu", (smk, "g")], writes=["gTri4"])
            P.I(POOL, 'tensor_scalar', dict(out=ngTri4[:], in0=gTri4[:], scalar1=-1.0, scalar2=None, op0=ALU.mult),
                reads=["gTri4"], writes=["ngTri4"])
            for h in range(NH):
                r = B2[:, h * 128:(h + 1) * 128]
                P.I(PE, 'matmul', MM(r, ones[:], gTri4[:, h, :], start=True, stop=False), reads=["ones", "gTri4"], writes=["B2"])
                P.I(PE, 'matmul', MM(r, ngTri4[:, h, :], ones[:], start=False, stop=False), reads=["ones", "ngTri4"], writes=["B2"])
                P.I(PE, 'matmul', MM(r, ident[:], negm[:], start=False, stop=True), reads=["ident", "negm"], writes=["B2"])
            P.I(ACT, 'activation', dict(out=ET4[:], in_=v4(B2), func=AF.Exp), reads=["B2"], writes=["ET4"])
            P.I(POOL, 'tensor_tensor', dict(out=ETs4[:], in0=ET4[:], in1=bmid(strict[:]), op=ALU.mult), reads=["ET4", "strict"], writes=["ETs4"])
            for h in range(NH):
                P.I(PE, 'matmul', MM(B3[:, h * 128:(h + 1) * 128], kT[:, h, tsl], kT[:, h, tsl]), reads=[kks[h]], writes=["B3"])
            P.I(DVE, 'tensor_tensor', dict(out=TT4[:], in0=v4(B3), in1=bin_(smt[:, 0:4]), op=ALU.mult), reads=["B3", (smk, "beta")], writes=["TT4"])
            P.I(DVE, 'tensor_tensor', dict(out=TT4[:], in0=TT4[:], in1=ETs4[:], op=ALU.mult), reads=["TT4", "ETs4"], writes=["TT4"])
            for h in range(NH):
                P.I(PE, 'matmul', MM(B2[:, h * 128:(h + 1) * 128], kT[:, h, tsl], qT[:, h, tsl]), reads=[kks[h], qks[h]], writes=["B2"])
            P.I(DVE, 'tensor_tensor', dict(out=attnT4[u][:], in0=v4(B2), in1=ET4[:], op=ALU.mult), reads=["B2", "ET4"], writes=[("attnT4", u)])
            for h in range(NH):
                P.I(PE, 'transpose', dict(out=B3[:, h * 128:(h + 1) * 128], in_=TT4[:, h, :], identity=ident[:]), reads=["TT4", "ident"], writes=["B3"])
            P.I(DVE, 'tensor_tensor', dict(out=TL4[:], in0=v4(B3), in1=bmid(ident[:]), op=ALU.add), reads=["B3", "ident"], writes=["TL4"])
            P.I(POOL, 'tensor_tensor', dict(out=TT4[:], in0=TT4[:], in1=bmid(ident[:]), op=ALU.add), reads=["TT4", "ident"], writes=["TT4"])
            P.I(DVE, 'scalar_tensor_tensor', dict(out=Nn4[0][:], in0=TT4[:], scalar=-1.0, in1=bmid(i2[:]), op0=ALU.mult, op1=ALU.add),
                reads=["TT4", "i2"], writes=[("Nn4", 0)])
            P.I(POOL, 'tensor_tensor', dict(out=Nn4[0][:], in0=Nn4[0][:], in1=bmid(bd[:, 0, :]), op=ALU.mult), reads=[("Nn4", 0), "bd"], writes=[("Nn4", 0)])
            P.I(DVE, 'scalar_tensor_tensor', dict(out=Mm4[0][:], in0=TL4[:], scalar=-1.0, in1=bmid(i2[:]), op0=ALU.mult, op1=ALU.add),
                reads=["TL4", "i2"], writes=[("Mm4", 0)])
            P.I(POOL, 'tensor_tensor', dict(out=Mm4[0][:], in0=Mm4[0][:], in1=bmid(bd[:, 0, :]), op=ALU.mult), reads=[("Mm4", 0), "bd"], writes=[("Mm4", 0)])
            for h in range(NH):
                P.I(PE, 'transpose', dict(out=B7[:, h * 128:(h + 1) * 128], in_=kT[:, h, tsl], identity=identb[:]), reads=[kks[h], "identb"], writes=["B7"])
            P.I(DVE, 'tensor_tensor', dict(out=Kdec4[u][:], in0=B7[:, 0:512].rearrange("p (h t) -> p h t", h=NH), in1=bin_(smt[:, 24:28]), op=ALU.mult),
                reads=["B7", (smk, "kdecs")], writes=[("Kdec4", u)])
            for h in range(NH):
                P.I(PE, 'transpose', dict(out=B4[:, h * 128:(h + 1) * 128], in_=vTf[:, h, tsl], identity=ident[:]), reads=[("vTf", h), "ident"], writes=["B4"])
            P.I(ACT, 'activation', dict(out=Vtok4[u][:], in_=v4(B4), func=AF.Copy), reads=["B4"], writes=[("Vtok4", u)])
            for l in range(1, 7 if STAGE[0] >= 4 else 1):
                cur, nxt = (l - 1) % 2, l % 2
                last = (l == 6)
                for h in range(NH):
                    P.I(PE, 'matmul', MM(B2[:, h * 128:(h + 1) * 128], TL4[:, h, :], Nn4[cur][:, h, :]), reads=["TL4", ("Nn4", cur)], writes=["B2"])
                P.I(DVE, 'scalar_tensor_tensor', dict(out=Pn4[:], in0=v4(B2), scalar=-1.0, in1=bmid(i2[:]), op0=ALU.mult, op1=ALU.add),
                    reads=["B2", "i2"], writes=["Pn4"])
                if not last:
                    for h in range(NH):
                        P.I(PE, 'matmul', MM(B3[:, h * 128:(h + 1) * 128], TT4[:, h, :], Mm4[cur][:, h, :]), reads=["TT4", ("Mm4", cur)], writes=["B3"])
                    P.I(ACT, 'activation', dict(out=Pm4[:], in_=v4(B3), func=AF.Copy, scale=-1.0), reads=["B3"], writes=["Pm4"])
                    P.I(POOL, 'tensor_tensor', dict(out=Pm4[:], in0=Pm4[:], in1=bmid(i2[:]), op=ALU.add), reads=["Pm4", "i2"], writes=["Pm4"])
                for h in range(NH):
                    P.I(PE, 'matmul', MM(B2[:, h * 128:(h + 1) * 128], Mm4[cur][:, h, :], Pn4[:, h, :]), reads=[("Mm4", cur), "Pn4"], writes=["B2"])
                dstN, dkN = (Nfin4[u], ("Nfin4", u)) if last else (Nn4[nxt], ("Nn4", nxt))
                P.I(DVE, 'tensor_tensor', dict(out=dstN[:], in0=v4(B2), in1=bmid(bd[:, l, :]), op=ALU.mult), reads=["B2", "bd"], writes=[dkN])
                if not last:
                    for h in range(NH):
                        P.I(PE, 'matmul', MM(B3[:, h * 128:(h + 1) * 128], Nn4[cur][:, h, :], Pm4[:, h, :]), reads=[("Nn4", cur), "Pm4"], writes=["B3"])
                    P.I(DVE, 'tensor_tensor', dict(out=Mm4[nxt][:], in0=v4(B3), in1=bmid(bd[:, l, :]), op=ALU.mult), reads=["B3", "bd"], writes=[("Mm4", nxt)])
            if STAGE[0] < 5:
                continue
            for h in range(NH):
                P.I(PE, 'matmul', MM(B4[:, h * 128:(h + 1) * 128], kT[:, h, tsl], Sb[:, h, :]), reads=[kks[h], "Sb"], writes=["B4"])
            for h in range(NH):
                P.I(PE, 'matmul', MM(B5[:, h * 128:(h + 1) * 128], qT[:, h, tsl], Sb[:, h, :]), reads=[qks[h], "Sb"], writes=["B5"])
            P.I(DVE, 'tensor_tensor', dict(out=Vres4[:], in0=v4(B4), in1=bin_(smt[:, 20:24]), op=ALU.mult), reads=["B4", (smk, "negegc")], writes=["Vres4"])
            P.I(POOL, 'tensor_tensor', dict(out=Vres4[:], in0=Vres4[:], in1=Vtok4[u][:], op=ALU.add), reads=["Vres4", ("Vtok4", u)], writes=["Vres4"])
            P.I(DVE, 'tensor_tensor', dict(out=tQS4[:], in0=v4(B5), in1=bin_(smt[:, 16:20]), op=ALU.mult), reads=["B5", (smk, "egc")], writes=["tQS4"])
            for h in range(NH):
                P.I(PE, 'matmul', MM(B4[:, h * 128:(h + 1) * 128], Nfin4[u][:, h, :], Vres4[:, h, :]), reads=[("Nfin4", u), "Vres4"], writes=["B4"])
            P.I(DVE, 'tensor_tensor', dict(out=Vn4[:], in0=v4(B4), in1=bin_(smt[:, 0:4]), op=ALU.mult), reads=["B4", (smk, "beta")], writes=["Vn4"])
            for h in range(NH):
                P.I(PE, 'matmul', MM(B5[:, h * 128:(h + 1) * 128], attnT4[u][:, h, :], Vn4[:, h, :]), reads=[("attnT4", u), "Vn4"], writes=["B5"])
            for h in range(NH):
                P.I(PE, 'matmul', MM(B4[:, h * 128:(h + 1) * 128], Kdec4[u][:, h, :], Vn4[:, h, :]), reads=[("Kdec4", u), "Vn4"], writes=["B4"])
            P.I(DVE, 'tensor_tensor', dict(out=osb4[:], in0=v4(B5), in1=tQS4[:], op=ALU.add), reads=["B5", "tQS4"], writes=["osb4"])
            P.I(POOL, 'tensor_tensor', dict(out=S32[:], in0=S32[:], in1=bin_(smt[:, 28:32]), op=ALU.mult), reads=["S32", (smk, "etot")], writes=["S32"])
            P.I(DVE, 'tensor_tensor', dict(out=S32[:], in0=S32[:], in1=v4(B4), op=ALU.add), reads=["S32", "B4"], writes=["S32"])
            P.I(ACT, 'activation', dict(out=Sb[:], in_=S32[:], func=AF.Copy), reads=["S32"], writes=["Sb"])
            P.I(POOL, 'tensor_tensor', dict(out=tmp4[:], in0=osb4[:], in1=osb4[:], op=ALU.mult), reads=["osb4"], writes=["tmp4"])
            P.I(DVE, 'tensor_reduce', dict(out=smt[:, 36:40], in_=tmp4[:], axis=AX.X, op=ALU.add), reads=["tmp4"], writes=[(smk, "ss")])
            P.I(DVE, 'tensor_scalar', dict(out=smt[:, 40:44], in0=smt[:, 36:40], scalar1=1.0 / 128, scalar2=NORM_EPS, op0=ALU.mult, op1=ALU.add),
                reads=[(smk, "ss")], writes=[(smk, "ms")])
            P.I(ACT, 'activation', dict(out=smt[:, 40:44], in_=smt[:, 40:44], func=AF.Sqrt), reads=[(smk, "ms")], writes=[(smk, "ms")])
            P.I(DVE, 'reciprocal', dict(out=smt[:, 44:48], in_=smt[:, 40:44]), reads=[(smk, "ms")], writes=[(smk, "rs")])
            P.I(DVE, 'tensor_tensor', dict(out=osb4[:], in0=osb4[:], in1=bin_(smt[:, 44:48]), op=ALU.mult), reads=["osb4", (smk, "rs")], writes=["osb4"])
            P.I(POOL, 'tensor_tensor', dict(out=ofin[u][:].rearrange("p (h t) -> p h t", h=NH), in0=osb4[:], in1=zs[u][:].rearrange("p (h t) -> p h t", h=NH), op=ALU.mult),
                reads=["osb4", f"zs{u}"], writes=[("ofin", u)])
            if STAGE[0] < 6:
                continue
            for h in range(NH):
                P.I(PE, 'transpose', dict(out=B7[:, 512 + h * 128:512 + (h + 1) * 128], in_=ofin[u][:, h * 128:(h + 1) * 128], identity=identb[:]),
                    reads=[("ofin", u), "identb"], writes=["B7"])
            P.I(ACT, 'activation', dict(out=oTt[:, :, tsl], in_=B7[:, 512:1024].rearrange("p (c t) -> p c t", c=4), func=AF.Copy),
                reads=["B7"], writes=[(oTk, ti)])
        okeys = [(oTk, ti) for ti in range(NT)]
        for c in range(8 if STAGE[0] >= 6 else 0):
            pw, pwk = proj_bank()
            for k in range(4):
                P.I(PE, 'matmul', MM(pw[:, :G], wos[:, k, c * 128:(c + 1) * 128], oTt[:, k, :], start=(k == 0), stop=(k == 3)),
                    reads=["wos"] + okeys, writes=[pwk])
            m, mk = mo[c % 2], f"mo{c % 2}"
            P.I(ACT, 'activation', dict(out=m[:], in_=pw[:, :G], func=AF.Copy), reads=[pwk], writes=[mk])
            P.dma(SP, mdst(c, t0, G), m[:], reads=[mk], writes=[("mix", g, c)], slot=f"st{c % 2}")
    print("gdn ops:", P.stats())
    if standalone:
        return P.finish()
    P.end_phase()


def gdn_inputs(xT, a_w_in, a_conv, a_a_log, a_dt_bias, a_norm_g, a_w_out, hh):
    hs = slice(hh * 512, (hh + 1) * 512)
    secs = [a_w_in[:, s * 1024:(s + 1) * 1024][:, hs] for s in range(4)]
    wqkvz = np.ascontiguousarray(np.concatenate(secs, 1))
    wba = np.ascontiguousarray(np.concatenate([a_w_in[:, 4096 + hh * 4:4096 + hh * 4 + 4], a_w_in[:, 4104 + hh * 4:4104 + hh * 4 + 4]], 1))
    cwl = []
    for kind in range(3):
        for h in range(4):
            c0 = kind * 1024 + hh * 512 + h * 128
            cwl.append(a_conv[:, c0:c0 + 128].T)
    cw = np.ascontiguousarray(np.stack(cwl, 1))
    hp = np.concatenate([a_a_log[hh * 4:hh * 4 + 4], a_dt_bias[hh * 4:hh * 4 + 4]])
    hp = np.ascontiguousarray(np.broadcast_to(hp[None, :], (128, 8)))
    ngt = np.ascontiguousarray(np.broadcast_to(np.tile(a_norm_g, 4)[None, :], (128, 512)))
    wo = np.ascontiguousarray(a_w_out[hs, :])
    d = {"xT": xT, "wqkvz": wqkvz, "wba": wba, "cw": cw, "hp": hp, "ngt": ngt, "wo": wo}
    d.update({"c_" + k: v for k, v in gdn_consts().items()})
    return d


D = 1024
MB = 256
ROPE_THETA = 500000.0
NEG = -30000.0


def moba_consts(T):
    half = 16
    inv_freq = np.power(np.float32(ROPE_THETA), -np.arange(half, dtype=np.float32) / half).astype(np.float32)
    ang = (np.arange(T, dtype=np.float32)[:, None] * inv_freq[None, :]).astype(np.float32)
    cos, sin = np.cos(ang).astype(np.float32).T, np.sin(ang).astype(np.float32).T
    C = np.ones((128, T), np.float32)
    S = np.zeros((128, T), np.float32)
    C[0:16], C[16:32] = cos, cos
    S[0:16], S[16:32] = -sin, sin
    scale = np.float32(128 ** -0.5)
    tabs = np.ascontiguousarray(np.stack([C * scale, S * scale, C, S]))
    nb = T // MB
    own = np.arange(nb)[:, None]
    n = np.arange(nb)[None, :]
    gm = np.where(n < own, 0.0, -1e30).astype(np.float32).reshape(1, nb * nb)
    gm = np.ascontiguousarray(np.broadcast_to(gm, (128, nb * nb)))
    oh = np.zeros((nb, nb, 128), np.float32)
    oh[np.arange(nb), np.arange(nb), :] = 1.0
    oh = np.ascontiguousarray(oh.reshape(nb, nb * 128))
    k = np.arange(128)[:, None, None]
    j = np.arange(2)[None, :, None]
    q = np.arange(256)[None, None, :]
    caus = np.where(j * 128 + k <= q, 0.0, NEG).astype(np.float32)
    return {"tabs": tabs, "c_gm": gm, "c_oh": oh, "c_caus": np.ascontiguousarray(caus), "c_ident": np.eye(128, dtype=np.float32)}


def build_moba(T, G=512, P=None, xsrc=None, mdst=None):
    standalone = P is None
    if standalone:
        P = Prog("moba")
        P.begin_phase("moba")
    NH = 4
    NB = T // MB
    NG = T // G
    NTILE = T // 128
    xT = P.dram_in("xT", [D, T]) if xsrc is None else None
    wq = P.dram_in("wq", [D, 512])
    wqs = P.dram_in("wqs", [D, 512])
    wk = P.dram_in("wk", [D, 512])
    wks = P.dram_in("wks", [D, 512])
    wv = P.dram_in("wv", [D, 512])
    wo = P.dram_in("wo", [512, D])
    tabs = P.dram_in("tabs", [4, 128, T])
    gmd = P.dram_in("c_gm", [128, NB * NB])
    ohd = P.dram_in("c_oh", [NB, NB * 128])
    causd = P.dram_in("c_caus", [128, 2, 256])
    identd = P.dram_in("c_ident", [128, 128])
    mixT = P.dram_out("mixT", [D, T]) if mdst is None else None

    wsb = [P.sb([128, 8, 128], BF16, f"w{i}") for i in range(5)]
    wos = P.sb([128, 4, D], BF16, "wos")
    qT = P.sb([128, T], BF16, "qT")
    kT = P.sb([128, T], BF16, "kT")
    vtok = P.sb([128, NTILE, 128], BF16, "vtok")
    oTall = P.sb([128, NH, T], BF16, "oTall")
    xb = [P.sb([128, 8, G], BF16, f"xb{i}") for i in range(2)]
    tb = [P.sb([128, 4, G], F32, f"tb{i}") for i in range(2)]
    t1 = P.sb([128, G], F32, "t1")
    t2 = P.sb([128, G], F32, "t2")
    kf = P.sb([128, G], F32, "kf")
    ks = P.sb([128, 2], F32, "ks")
    kmb = P.sb([128, NB], BF16, "kmb")
    gms = P.sb([128, NB * NB], F32, "gms")
    ohs = P.sb([NB, NB * 128], BF16, "ohs")
    caus = P.sb([128, 2, 256], BF16, "caus")
    ident = P.sb([128, 128], F32, "ident")
    identb = P.sb([128, 128], BF16, "identb")
    onesb = P.sb([128, 128], BF16, "onesb")
    gmt = P.sb([128, 2, NB], F32, "gmt")
    mx = P.sb([128, 16], F32, "mx")
    pen = P.sb([128, 2, NB], F32, "pen")
    penT = P.sb([NB, 256], BF16, "penT")
    PT = [P.sb([128, 512], BF16, f"PT{i}") for i in range(3)]
    rec = P.sb([128, 256], F32, "rec")
    mo = [P.sb([128, G], F32, f"mo{i}") for i in range(2)]
    psum = [P.ps([128, 512], F32, f"psum{i}") for i in range(8)]
    PA, PB, BG = psum[0], psum[1], psum[1]
    SB = psum[2:4]
    OB = psum[4:6]
    DN = psum[6:8]

    P.dma(SP, gms[:], gmd[:, :], writes=["gms"], slot="c_gm")
    P.dma(SP, ident[:], identd[:, :], writes=["ident"], slot="c_id")
    P.dma(POOL, ohs[:], ohd[:, :], writes=["ohs"], slot="c_oh")
    P.dma(POOL, caus[:], causd[:, :, :], writes=["caus"], slot="c_caus")
    P.dma(POOL, identb[:], identd[:, :], writes=["identb"], slot="c_idb")
    P.I(POOL, 'memset', dict(ap=onesb[:], constant=1.0), writes=["onesb"])
    for k in range(4):
        P.dma(POOL, wos[:, k, :], wo[k * 128:(k + 1) * 128, :], writes=["wos"], slot="w_o")

    if xsrc is None:
        xTv = xT.rearrange("(c p) t -> p c t", p=128)
        xsrc = lambda t0, n: xTv[:, :, t0:t0 + n]
    tabv = tabs.rearrange("n p t -> p n t")
    if mdst is None:
        mixv = mixT.rearrange("(c p) t -> p c t", p=128)
        mdst = lambda c, t0, n: mixv[:, c, t0:t0 + n]
    wd = [wq, wqs, wk, wks, wv]
    sbi = [0]
    odi = [0]
    gi = [0]
    for h in range(NH):
        for i in range(5):
            for k in range(8):
                P.dma(POOL, wsb[i][:, k, :], wd[i][k * 128:(k + 1) * 128, h * 128:(h + 1) * 128], writes=[f"w{i}"], slot=f"w{i}")
        for g in range(NG):
            gi[0] += 1
            s = gi[0] % 2
            t0 = g * G
            xbt, xk, tbt, tk = xb[s], f"xb{s}", tb[s], f"tb{s}"
            P.dma(POOL, xbt[:], xsrc(t0, G), writes=[xk], slot=f"ldx{s}")
            P.dma(SP, tbt[:], tabv[:, :, t0:t0 + G], writes=[tk], slot=f"ldt{s}")
            for w in range(2):
                for k in range(8):
                    P.I(PE, 'matmul', MM(PA[:, :G], wsb[2 * w][:, k, :], xbt[:, k, :], start=(k == 0), stop=(k == 7)),
                        reads=[f"w{2 * w}", xk], writes=["PA"])
                for k in range(8):
                    P.I(PE, 'matmul', MM(PB[:, :G], wsb[2 * w + 1][:, k, :], xbt[:, k, :], start=(k == 0), stop=(k == 7)),
                        reads=[f"w{2 * w + 1}", xk], writes=["PB"])
                P.I(DVE, 'tensor_tensor', dict(out=t1[:], in0=PA[:, :G], in1=tbt[:, 2 * w, :], op=ALU.mult), reads=["PA", tk], writes=["t1"])
                P.I(DVE, 'tensor_tensor', dict(out=t2[:], in0=PB[:, :G], in1=tbt[:, 2 * w + 1, :], op=ALU.mult), reads=["PB", tk], writes=["t2"])
                if w == 0:
                    P.I(POOL, 'tensor_tensor', dict(out=qT[:, t0:t0 + G], in0=t1[:], in1=t2[:], op=ALU.add), reads=["t1", "t2"], writes=[("qT", g)])
                else:
                    P.I(POOL, 'tensor_tensor', dict(out=kf[:], in0=t1[:], in1=t2[:], op=ALU.add), reads=["t1", "t2"], writes=["kf"])
                    P.I(ACT, 'activation', dict(out=kT[:, t0:t0 + G], in_=kf[:], func=AF.Copy), reads=["kf"], writes=[("kT", g)])
                    P.I(DVE, 'tensor_reduce', dict(out=ks[:], in_=kf[:].rearrange("p (b t) -> p b t", b=2), axis=AX.X, op=ALU.add),
                        reads=["kf"], writes=["ks"])
                    P.I(ACT, 'activation', dict(out=kmb[:, 2 * g:2 * g + 2], in_=ks[:], func=AF.Copy), reads=["ks"], writes=[("kmb", g)])
            for ti in range(4):
                for k in range(8):
                    P.I(PE, 'matmul', MM(PA[:, ti * 128:(ti + 1) * 128], xbt[:, k, ti * 128:(ti + 1) * 128], wsb[4][:, k, :],
                                         start=(k == 0), stop=(k == 7)), reads=["w4", xk], writes=["PA"])
            P.I(ACT, 'activation', dict(out=vtok[:, 4 * g:4 * g + 4, :], in_=PA[:, :].rearrange("p (a d) -> p a d", a=4), func=AF.Copy),
                reads=["PA"], writes=[("vtok", g)])
        qkeys = [("qT", g) for g in range(NG)]
        kkeys = [("kT", g) for g in range(NG)]
        vkeys = [("vtok", g) for g in range(NG)]
        mkeys = [("kmb", g) for g in range(NG)]
        for qb in range(NB):
            own = qb
            q0 = qb * 256
            qsl = slice(q0, q0 + 256)
            qk_ = [("qT", q0 // G)]
            if own > 0:
                for j in range(2):
                    P.I(PE, 'matmul', MM(BG[:, j * NB:(j + 1) * NB], qT[:, q0 + j * 128:q0 + (j + 1) * 128], kmb[:, :]),
                        reads=qk_ + mkeys, writes=["PB"])
                P.I(DVE, 'tensor_tensor', dict(out=gmt[:], in0=BG[:, 0:2 * NB].rearrange("p (j n) -> p j n", j=2),
                                               in1=gms[:, own * NB:(own + 1) * NB].unsqueeze(1).broadcast_to([128, 2, NB]), op=ALU.add),
                    reads=["PB", "gms"], writes=["gmt"])
                for j in range(2):
                    P.I(DVE, 'max', dict(out=mx[:, j * 8:(j + 1) * 8], in_=gmt[:, j, :]), reads=["gmt"], writes=[("mx", j)])
                for j in range(2):
                    P.I(DVE, 'tensor_scalar', dict(out=pen[:, j, :], in0=gmt[:, j, :], scalar1=mx[:, j * 8 + 2:j * 8 + 3], scalar2=None, op0=ALU.is_ge),
                        reads=["gmt", ("mx", j)], writes=[("pen", j)])
                P.I(DVE, 'tensor_scalar', dict(out=pen[:], in0=pen[:], scalar1=-1.0, scalar2=-NEG, op0=ALU.add, op1=ALU.mult),
                    reads=[("pen", 0), ("pen", 1)], writes=["pen"])
                for j in range(2):
                    P.I(PE, 'transpose', dict(out=BG[0:NB, 128 + j * 128:128 + (j + 1) * 128], in_=pen[:, j, :], identity=ident[:]),
                        reads=["pen", "ident"], writes=["PB"])
                P.I(ACT, 'activation', dict(out=penT[:], in_=BG[0:NB, 128:384], func=AF.Copy), reads=["PB"], writes=["penT"])
            odi[0] += 1
            ob, dn = OB[odi[0] % 2], DN[odi[0] % 2]
            obk, dnk = f"OB{odi[0] % 2}", f"DN{odi[0] % 2}"
            nchunks = 2 * (own + 1)
            ci = 0
            for n in range(own + 1):
                sbi[0] += 1
                x = sbi[0] % 2
                sbk, ptk = f"SB{x}", f"PT{x}"
                for c in range(2):
                    kc = 2 * n + c
                    reg = SB[x][:, c * 256:(c + 1) * 256]
                    P.I(PE, 'matmul', MM(reg, kT[:, kc * 128:(kc + 1) * 128], qT[:, qsl], start=True, stop=False),
                        reads=[("kT", kc * 128 // G)] + qk_, writes=[sbk])
                    if n < own:
                        P.I(PE, 'matmul', MM(reg, ohs[:, n * 128:(n + 1) * 128], penT[:], start=False, stop=True),
                            reads=["ohs", "penT"], writes=[sbk])
                    else:
                        P.I(PE, 'matmul', MM(reg, identb[:], caus[:, c, :], start=False, stop=True), reads=["identb", "caus"], writes=[sbk])
                P.I(ACT, 'activation', dict(out=PT[x][:], in_=SB[x][:, :], func=AF.Exp), reads=[sbk], writes=[ptk])
                for c in range(2):
                    kc = 2 * n + c
                    first, lastc = (ci == 0), (ci == nchunks - 1)
                    P.I(PE, 'matmul', MM(ob[:, 0:256], vtok[:, kc, :], PT[x][:, c * 256:(c + 1) * 256], start=first, stop=lastc),
                        reads=[("vtok", kc // 4), ptk], writes=[obk])
                    P.I(PE, 'matmul', MM(dn[:, 0:256], onesb[:], PT[x][:, c * 256:(c + 1) * 256], start=first, stop=lastc),
                        reads=["onesb", ptk], writes=[dnk])
                    ci += 1
            P.I(DVE, 'reciprocal', dict(out=rec[:], in_=dn[:, 0:256]), reads=[dnk], writes=["rec"])
            P.I(DVE, 'tensor_tensor', dict(out=oTall[:, h, qsl], in0=ob[:, 0:256], in1=rec[:], op=ALU.mult), reads=[obk, "rec"],
                writes=[("oT", h, q0 // G, (q0 // 256) % 2)])
    for g in range(NG):
        t0 = g * G
        okeys = [("oT", h, g, j) for h in range(NH) for j in range(2)]
        for c in range(8):
            pw, pwk = (PA, "PA") if c % 2 == 0 else (PB, "PB")
            for k in range(4):
                P.I(PE, 'matmul', MM(pw[:, :G], wos[:, k, c * 128:(c + 1) * 128], oTall[:, k, t0:t0 + G], start=(k == 0), stop=(k == 3)),
                    reads=["wos"] + okeys, writes=[pwk])
            m, mk = mo[c % 2], f"mo{c % 2}"
            P.I(ACT, 'activation', dict(out=m[:], in_=pw[:, :G], func=AF.Copy), reads=[pwk], writes=[mk])
            P.dma(SP, mdst(c, t0, G), m[:], reads=[mk], writes=[("mix", g, c)], slot=f"st{c % 2}")
    print("moba ops:", P.stats())
    if standalone:
        return P.finish()
    P.end_phase()


def moba_inputs(xT, b_w_qkv, b_w_out, hh, T):
    hs = slice(hh * 512, (hh + 1) * 512)
    wq = b_w_qkv[:, 0:1024][:, hs]
    wk = b_w_qkv[:, 1024:2048][:, hs]
    wv = b_w_qkv[:, 2048:3072][:, hs]
    perm = np.arange(512)
    for h in range(4):
        perm[h * 128:h * 128 + 16] = np.arange(h * 128 + 16, h * 128 + 32)
        perm[h * 128 + 16:h * 128 + 32] = np.arange(h * 128, h * 128 + 16)
    d = {"xT": xT, "wq": np.ascontiguousarray(wq), "wqs": np.ascontiguousarray(wq[:, perm]),
         "wk": np.ascontiguousarray(wk), "wks": np.ascontiguousarray(wk[:, perm]), "wv": np.ascontiguousarray(wv),
         "wo": np.ascontiguousarray(b_w_out[hs, :])}
    d.update(moba_consts(T))
    return d


B_, T_FULL = 4, 8192


NLAYERS = [4]
KINDS = [0, 1, 2, 0]
NOAG = [False]
AGUNUSED = [False]


def build_fused(T):
    H = T // 2
    P = Prog("fused")
    x0T = P.dram_in("x0T", [D, T])
    x0h = P.dram_in("x0h", [D, H])
    outT = P.dram_out("outT", [D, H])
    mixp = P.dram_tmp("mixp", [2 * D, H])
    msum = P.dram_tmp("msum", [D, H])
    hout = P.dram_tmp("hout", [D, H])
    hfull = P.dram_tmp("hfull", [2 * D, H])
    x0v = x0T.rearrange("(c p) t -> p c t", p=128)
    x0hv = x0h.rearrange("(c p) t -> p c t", p=128)
    outv = outT.rearrange("(c p) t -> p c t", p=128)
    msv = msum.rearrange("(c p) t -> p c t", p=128)
    hov = hout.rearrange("(c p) t -> p c t", p=128)
    hf4 = hfull.rearrange("(c r p) t -> r p c t", c=8, r=2, p=128)
    mp4 = mixp.rearrange("(c r p) t -> r p c t", c=8, r=2, p=128)
    hfv = [hf4[r] for r in range(2)]
    mpv = [mp4[r] for r in range(2)]
    mdst = lambda c, t0, n: mpv[t0 // H][:, c, t0 % H:t0 % H + n]
    NL = NLAYERS[0]
    for i in range(NL):
        kind = KINDS[i]
        P.dram_prefix = f"L{i}_"
        if i == 0 or NOAG[0] or AGUNUSED[0]:
            xsrc = lambda t0, n: x0v[:, :, t0:t0 + n]
        else:
            xsrc = lambda t0, n: hfv[t0 // H][:, :, t0 % H:t0 % H + n]
        P.begin_phase(f"m{i}")
        if kind == 0:
            build_gdn(T, P=P, xsrc=xsrc, mdst=mdst)
        elif kind == 1:
            build_moba(T, P=P, xsrc=xsrc, mdst=mdst)
        else:
            build_ret(T, P=P, xsrc=xsrc, mdst=mdst)
        P.begin_phase(f"rs{i}")
        for c in range(8):
            P.cc("ReduceScatter", ALU.add, mixp[c * 256:(c + 1) * 256, :], msum[c * 128:(c + 1) * 128, :], slot=f"cc_rs{i}")
        P.end_phase()
        P.begin_phase(f"p{i}")
        hs = x0hv if i == 0 else hov
        od = outv if i == NL - 1 else hov
        build_P(H, P=P, hsrc=lambda t0, n, hs=hs: hs[:, :, t0:t0 + n], masrc=lambda t0, n: msv[:, :, t0:t0 + n],
                odst=lambda t0, n, od=od: od[:, :, t0:t0 + n], single_mix=True)
        if i < NL - 1 and not NOAG[0]:
            P.begin_phase(f"ag{i}")
            for c in range(8):
                P.cc("AllGather", ALU.bypass, hout[c * 128:(c + 1) * 128, :], hfull[c * 256:(c + 1) * 256, :], slot=f"cc_ag{i}")
            P.end_phase()
    print("fused ops:", P.stats())
    return P.finish()


def _lnp(g1, b1, g2, b2):
    lay = lambda v: v.reshape(8, 128).T
    return np.ascontiguousarray(np.concatenate([lay(g1), lay(b1), lay(g2), lay(b2)], axis=1).astype(np.float32))


def _ret_inputs(c_w_in, c_gn_g, c_w_out, hh, T):
    heads = [2 * hh, 2 * hh + 1]
    hq = slice(hh * 512, (hh + 1) * 512)
    hv = slice(hh * 1024, (hh + 1) * 1024)
    tabs, gam = ret_tables(heads, T)
    return {"wq": np.ascontiguousarray(c_w_in[:, 0:1024][:, hq]), "wk": np.ascontiguousarray(c_w_in[:, 1024:2048][:, hq]),
            "wv": np.ascontiguousarray(c_w_in[:, 2048:4096][:, hv]), "wg": np.ascontiguousarray(c_w_in[:, 4096:6144][:, hv]),
            "wo": np.ascontiguousarray(c_w_out[hv, :]),
            "gng": np.ascontiguousarray(np.broadcast_to(c_gn_g[hv][None, :], (128, 1024))), "tabs": tabs,
            "cmask": np.triu(np.ones((128, 128), np.float32)), "ident": np.eye(128, dtype=np.float32),
            "gam": np.ascontiguousarray(np.broadcast_to(np.array(gam, np.float32)[None, :], (128, 2)))}


def _core_inputs(c, T, x, a_w_in, a_conv, a_a_log, a_dt_bias, a_norm_g, a_w_out, b_w_qkv, b_w_out,
                 c_w_in, c_gn_g, c_w_out, f_w13, f_w2, ln1_g, ln1_b, ln2_g, ln2_b, cache):
    b, r = c // 2, c % 2
    H = T // 2
    xT = cache.setdefault(("xT", b), np.ascontiguousarray(x[b].T))
    d = {"x0T": xT, "x0h": np.ascontiguousarray(xT[:, r * H:(r + 1) * H])}
    for i in range(NLAYERS[0]):
        kind = KINDS[i]
        j = sum(1 for q in range(i) if KINDS[q] == kind) % {0: 2, 1: 1, 2: 1}[kind]
        key = ("layer", i, r)
        if key not in cache:
            if kind == 0:
                li = gdn_inputs(None, a_w_in[j], a_conv[j], a_a_log[j], a_dt_bias[j], a_norm_g[j], a_w_out[j], hh=r)
            elif kind == 1:
                li = moba_inputs(None, b_w_qkv[j], b_w_out[j], hh=r, T=T)
            else:
                li = _ret_inputs(c_w_in[j], c_gn_g[j], c_w_out[j], hh=r, T=T)
            li.pop("xT", None)
            li["w13"] = f_w13[i]
            li["w2"] = f_w2[i]
            li["lnp"] = _lnp(ln1_g[i], ln1_b[i], ln2_g[i], ln2_b[i])
            cache[key] = {f"L{i}_{k}": v for k, v in li.items()}
        d.update(cache[key])
    return d


def kernel(x, a_w_in, a_conv, a_a_log, a_dt_bias, a_norm_g, a_w_out, b_w_qkv, b_w_out,
           c_w_in, c_gn_g, c_w_out, f_w13, f_w2, ln1_g, ln1_b, ln2_g, ln2_b):
    f32 = lambda a: np.ascontiguousarray(np.asarray(a, dtype=np.float32))
    args = [f32(a) for a in (x, a_w_in, a_conv, a_a_log, a_dt_bias, a_norm_g, a_w_out, b_w_qkv, b_w_out,
                             c_w_in, c_gn_g, c_w_out, f_w13, f_w2, ln1_g, ln1_b, ln2_g, ln2_b)]
    T = args[0].shape[1]
    cores = list(range(8))
    cache = {}
    ims = [_core_inputs(c, T, *args, cache=cache) for c in cores]
    nc = build_fused(T)
    res = run_bass_kernel_spmd(nc, ims, core_ids=cores).results
    out = np.empty((B_, T, D), np.float32)
    H = T // 2
    for c in cores:
        out[c // 2, (c % 2) * H:(c % 2 + 1) * H, :] = res[c]["outT"].T
    return out
```

```python
import numpy as np
from contextlib import ExitStack
import concourse.bass as bass
import concourse.mybir as mybir
from concourse.bass_utils import run_bass_kernel_spmd

F32 = mybir.dt.float32
BF16 = mybir.dt.bfloat16
AF = mybir.ActivationFunctionType
ALU = mybir.AluOpType
AX = mybir.AxisListType

PE, ACT, DVE, POOL, SP = "PE", "ACT", "DVE", "POOL", "SP"


def MM(out, lhsT, rhs, start=True, stop=True):
    return dict(out=out, lhsT=lhsT, rhs=rhs, start=start, stop=stop)


class Prog:
    ENGS = (PE, ACT, DVE, POOL, SP)
    PAIRS = [[0, 1], [2, 3], [4, 5], [6, 7]]

    def __init__(self, name="k"):
        self.nc = bass.Bass("TRN2", target_bir_lowering=False)
        self.stack = ExitStack()
        self.pstack = None
        self.dma_sems = {}
        self.esem = None
        self.base = {e: 0 for e in self.ENGS}
        self.dram_prefix = ""
        self.tag = ""
        self.nphase = 0
        self.total_ops = {e: 0 for e in self.ENGS}
        self._reset()

    def _reset(self):
        self.ops = {e: [] for e in self.ENGS}
        self.last_write = {}
        self.readers = {}
        self.seen = {e: {} for e in self.ENGS}
        self.marked = {e: set() for e in self.ENGS}

    def begin_phase(self, tag):
        self.tag = tag
        self.pstack = ExitStack()
        self._reset()

    def sb(self, shape, dtype, name):
        return self.pstack.enter_context(self.nc.sbuf_tensor(f"{self.tag}_{name}", list(shape), dtype))

    def ps(self, shape, dtype, name):
        return self.pstack.enter_context(self.nc.psum_tensor(f"{self.tag}_{name}", list(shape), dtype))

    def dram_in(self, name, shape, dtype=F32):
        return self.nc.dram_tensor(self.dram_prefix + name, list(shape), dtype, kind="ExternalInput").ap()

    def dram_out(self, name, shape, dtype=F32):
        return self.nc.dram_tensor(name, list(shape), dtype, kind="ExternalOutput").ap()

    def dram_tmp(self, name, shape, dtype=F32):
        return self.nc.dram_tensor(name, list(shape), dtype).ap()

    def _deps(self, eng, reads, writes):
        deps = []
        for k in reads:
            if k in self.last_write:
                deps.append(self.last_write[k])
        for k in writes:
            if k in self.last_write:
                deps.append(self.last_write[k])
            for r in self.readers.get(k, ()):
                deps.append(r)
        out = {}
        for (prod, val) in deps:
            if prod == eng and (eng == PE):
                continue
            if self.seen[eng].get(prod, -1) >= val:
                continue
            if out.get(prod, -1) < val:
                out[prod] = val
        for prod, val in out.items():
            self.seen[eng][prod] = val
            if prod in self.ops:
                self.marked[prod].add(val)
        return list(out.items())

    def op(self, eng, fn, reads=(), writes=()):
        deps = self._deps(eng, reads, writes)
        idx = len(self.ops[eng])
        self.ops[eng].append(("op", deps, fn, idx))
        tag = (eng, idx)
        for k in writes:
            self.last_write[k] = tag
            self.readers[k] = []
        for k in reads:
            if k not in writes:
                self.readers.setdefault(k, []).append(tag)
        return idx

    def I(self, eng, name, kw, reads=(), writes=()):
        return self.op(eng, lambda e, name=name, kw=kw: getattr(e, name)(**kw), reads, writes)

    def _slot(self, slot):
        if slot not in self.dma_sems:
            sem = self.stack.enter_context(self.nc.semaphore(f"d{len(self.dma_sems)}"))
            self.dma_sems[slot] = [sem, 0]
        return self.dma_sems[slot]

    def dma(self, queue, out, in_, reads=(), writes=(), slot=None):
        assert slot is not None
        deps = self._deps(queue, reads, writes)
        ent = self._slot(slot)
        ent[1] += 16
        val = ent[1]
        self.ops[queue].append(("dma", deps, (out, in_, ent[0]), None))
        tag = (("dma", slot), val)
        for k in writes:
            self.last_write[k] = tag
            self.readers[k] = []
        for k in reads:
            self.readers.setdefault(k, []).append(tag)

    def cc(self, kind, alu, in_, out, slot):
        ent = self._slot(slot)
        ent[1] += 1
        self.ops[POOL].append(("cc", [], (kind, alu, in_, out, ent[0]), None))

    def end_phase(self):
        nc = self.nc
        if self.esem is None:
            self.esem = {e: self.stack.enter_context(nc.semaphore(f"e{e}")) for e in self.ENGS}
        esem = self.esem
        for e in self.ENGS:
            last = [idx for kind, _, _, idx in self.ops[e] if kind == "op"]
            if last:
                self.marked[e].add(last[-1])
        ranks = {}
        for e in self.ENGS:
            m = sorted(self.marked[e])
            ranks[e] = {idx: self.base[e] + i + 1 for i, idx in enumerate(m)}
        pre_eng = [(esem[e], self.base[e]) for e in self.ENGS if self.base[e] > 0]
        pre_dma = [(ent[0], ent[1]) for ent in self.prev_dma] if self.nphase > 0 else []

        def emit(ename, e):
            if not self.ops[ename]:
                return
            for sem, val in pre_eng + pre_dma:
                e.wait_ge(sem, val)
            for kind, deps, payload, idx in self.ops[ename]:
                for prod, val in deps:
                    if isinstance(prod, tuple):
                        e.wait_ge(self.dma_sems[prod[1]][0], val)
                    else:
                        e.wait_ge(esem[prod], ranks[prod][val])
                if kind == "op":
                    ins = payload(e)
                    if idx in ranks[ename]:
                        ins.then_inc(esem[ename], 1)
                elif kind == "dma":
                    out, in_, sem = payload
                    e.dma_start(out=out, in_=in_).then_inc(sem, 16)
                else:
                    ckind, alu, in_, out, sem = payload
                    e.collective_compute(ckind, alu, replica_groups=self.PAIRS, ins=[in_], outs=[out]).then_inc(sem)

        with nc.Block() as block:
            @block.tensor
            def _(e):
                emit(PE, e)

            @block.scalar
            def _(e):
                emit(ACT, e)

            @block.vector
            def _(e):
                emit(DVE, e)

            @block.gpsimd
            def _(e):
                emit(POOL, e)

            @block.sync
            def _(e):
                emit(SP, e)
        for e in self.ENGS:
            self.base[e] += len(self.marked[e])
            self.total_ops[e] += len(self.ops[e])
        self.prev_dma = [[ent[0], ent[1]] for ent in self.dma_sems.values()]
        self.nphase += 1
        self.pstack.close()
        self.pstack = None
        self._reset()

    def finish(self):
        if self.pstack is not None:
            self.end_phase()
        nc = self.nc
        finals = [(ent[0], ent[1]) for ent in self.dma_sems.values()] + [(self.esem[e], self.base[e]) for e in self.ENGS if self.base[e] > 0]
        with nc.Block() as block:
            @block.sync
            def _(e):
                for sem, val in finals:
                    e.wait_ge(sem, val)
        self.stack.close()
        return nc

    def stats(self):
        return {e: self.total_ops[e] + len(v) for e, v in self.ops.items()}


D = 1024
DFF = 2816
NFF = DFF // 128
ALPHA = 8 ** 0.25
LN_EPS = 1e-5


def bc_mid(ap2d, n):
    return ap2d.unsqueeze(1).broadcast_to([ap2d.shape[0], n, ap2d.shape[1]])


def emit_ln(P, y, out_f32, out_bf, g_ap, b_ap, ones, sq, st, psA, psB, keyp, N, out_keys, ykey):
    nc = P.nc
    mean, msq, var, rstd = st
    P.I(ACT, 'activation', dict(out=sq[:], in_=y[:], func=AF.Square), reads=[ykey], writes=[keyp + "sq"])
    for c in range(8):
        P.I(PE, 'matmul', MM(psA[:, :N], ones[:], y[:, c, :], start=(c == 0), stop=(c == 7)),
             reads=[ykey, "ones"], writes=[keyp + "psA"])
    for c in range(8):
        P.I(PE, 'matmul', MM(psB[:, :N], ones[:], sq[:, c, :], start=(c == 0), stop=(c == 7)),
             reads=[keyp + "sq", "ones"], writes=[keyp + "psB"])
    P.I(DVE, 'tensor_scalar', dict(out=mean[:], in0=psA[:, :N], scalar1=1.0 / D, scalar2=None, op0=ALU.mult),
         reads=[keyp + "psA"], writes=[keyp + "mean"])
    P.I(DVE, 'tensor_tensor', dict(out=msq[:], in0=mean[:], in1=mean[:], op=ALU.mult),
         reads=[keyp + "mean"], writes=[keyp + "msq"])
    P.I(DVE, 'scalar_tensor_tensor', dict(out=var[:], in0=psB[:, :N], scalar=1.0 / D, in1=msq[:],
                                               op0=ALU.mult, op1=ALU.subtract),
         reads=[keyp + "psB", keyp + "msq"], writes=[keyp + "var"])
    P.I(DVE, 'tensor_scalar', dict(out=var[:], in0=var[:], scalar1=LN_EPS, scalar2=None, op0=ALU.add),
         reads=[keyp + "var"], writes=[keyp + "var"])
    P.I(ACT, 'activation', dict(out=var[:], in_=var[:], func=AF.Sqrt),
         reads=[keyp + "var"], writes=[keyp + "var"])
    P.I(DVE, 'reciprocal', dict(out=rstd[:], in_=var[:]),
         reads=[keyp + "var"], writes=[keyp + "rstd"])
    P.I(DVE, 'tensor_tensor', dict(out=y[:], in0=y[:], in1=bc_mid(mean[:], 8), op=ALU.subtract),
         reads=[ykey, keyp + "mean"], writes=[ykey])
    P.I(DVE, 'tensor_tensor', dict(out=y[:], in0=y[:], in1=bc_mid(rstd[:], 8), op=ALU.mult),
         reads=[ykey, keyp + "rstd"], writes=[ykey])
    for c in range(8):
        P.I(ACT, 'activation', dict(out=out_f32[:, c, :], in_=y[:, c, :], func=AF.Identity,
                                              bias=b_ap[:, c:c + 1], scale=g_ap[:, c:c + 1]),
             reads=[ykey, "lnp"], writes=[out_keys[0] + str(c)])
    if out_bf is not None:
        P.I(POOL, 'tensor_copy', dict(out=out_bf[:], in_=out_f32[:]),
             reads=[out_keys[0] + str(c) for c in range(8)], writes=[out_keys[1]])


def build_P(ntok, N=256, P=None, hsrc=None, masrc=None, mbsrc=None, odst=None, single_mix=False):
    standalone = P is None
    if standalone:
        P = Prog("P")
        P.begin_phase("P")
    nc = P.nc
    if standalone:
        hT = P.dram_in("hT", [D, ntok])
        mA = P.dram_in("mA", [D, ntok])
        mB = P.dram_in("mB", [D, ntok])
    w13 = P.dram_in("w13", [D, 2 * DFF])
    w2 = P.dram_in("w2", [DFF, D])
    lnp = P.dram_in("lnp", [128, 32])
    if standalone:
        outT = P.dram_out("outT", [D, ntok])

    w13s = P.sb([128, 8, 2 * DFF], BF16, "w13s")
    w2s = P.sb([128, NFF, D], BF16, "w2s")
    lnps = P.sb([128, 32], F32, "lnps")
    ones = P.sb([128, 128], F32, "ones")
    hbuf = [P.sb([128, 8, N], F32, f"hb{i}") for i in range(2)]
    mAt = P.sb([128, 8, N], F32, "mAt")
    mBt = P.sb([128, 8, N], F32, "mBt")
    sq = P.sb([128, 8, N], F32, "sq")
    h1 = P.sb([128, 8, N], F32, "h1")
    h1b = P.sb([128, 8, N], BF16, "h1b")
    act = P.sb([128, NFF, N], BF16, "act")
    sg = [P.sb([128, N], F32, f"sg{i}") for i in range(2)]
    st = [P.sb([128, N], F32, f"st{i}") for i in range(4)]
    psum = [P.ps([128, 512], F32, f"psum{i}") for i in range(8)]

    P.I(POOL, 'memset', dict(ap=ones[:], constant=1.0), writes=["ones"])
    P.dma(SP, lnps[:], lnp[:, :], writes=["lnp"], slot="lnp")
    for k in range(8):
        P.dma(POOL, w13s[:, k, :], w13[k * 128:(k + 1) * 128, :], writes=["w13"], slot="w13")
    for j in range(NFF):
        P.dma(POOL, w2s[:, j, :], w2[j * 128:(j + 1) * 128, :], writes=["w2"], slot="w2")
    w13keys = [("w13", k) for k in range(8)]
    w2keys = [("w2", j) for j in range(NFF)]

    if standalone:
        hTv = hT.rearrange("(c p) t -> p c t", p=128)
        mAv = mA.rearrange("(c p) t -> p c t", p=128)
        mBv = mB.rearrange("(c p) t -> p c t", p=128)
        outv = outT.rearrange("(c p) t -> p c t", p=128)
        hsrc = lambda t0, n: hTv[:, :, t0:t0 + n]
        masrc = lambda t0, n: mAv[:, :, t0:t0 + n]
        mbsrc = lambda t0, n: mBv[:, :, t0:t0 + n]
        odst = lambda t0, n: outv[:, :, t0:t0 + n]

    ng = ntok // N
    for g in range(ng):
        s = g % 2
        t0 = g * N
        hb, ma, mb, o, y2 = hbuf[s], mAt, mBt, mBt, mAt
        hk, mak, mbk = f"h{s}", "y2", "mB"
        P.dma(SP, hb[:], hsrc(t0, N), writes=[hk], slot=f"ldh{s}")
        P.dma(SP, ma[:], masrc(t0, N), writes=[mak], slot=f"ldA{s}")
        if not single_mix:
            P.dma(SP, mb[:], mbsrc(t0, N), writes=[mbk] + ["mB_" + str(c) for c in range(8)], slot=f"ldB{s}")
        P.I(DVE, 'scalar_tensor_tensor', dict(out=hb[:], in0=hb[:], scalar=ALPHA, in1=ma[:],
                                                                 op0=ALU.mult, op1=ALU.add),
             reads=[hk, mak], writes=[hk])
        if not single_mix:
            P.I(POOL, 'tensor_tensor', dict(out=hb[:], in0=hb[:], in1=mb[:], op=ALU.add),
                reads=[hk, mbk], writes=[hk])
        emit_ln(P, hb, h1, h1b, lnps[:, 0:8], lnps[:, 8:16], ones, sq, st, psum[0], psum[1], "ln", N,
                ("h1_", "h1b"), hk)
        for j in range(NFF):
            pg = psum[2 + (j % 2)]
            pu = psum[4 + (j % 2)]
            pgk, puk = f"pg{j % 2}", f"pu{j % 2}"
            for k in range(8):
                P.I(PE, 'matmul', MM(pg[:, :N], w13s[:, k, j * 128:(j + 1) * 128], h1b[:, k, :],
                                                            start=(k == 0), stop=(k == 7)),
                     reads=["w13", "h1b"], writes=[pgk])
            for k in range(8):
                P.I(PE, 'matmul', MM(pu[:, :N], w13s[:, k, DFF + j * 128:DFF + (j + 1) * 128],
                                                            h1b[:, k, :], start=(k == 0), stop=(k == 7)),
                     reads=["w13", "h1b"], writes=[puk])
            sgt = sg[j % 2]
            P.I(ACT, 'activation', dict(out=sgt[:], in_=pg[:, :N], func=AF.Silu),
                 reads=[pgk], writes=[f"sg{j % 2}"])
            P.I(DVE, 'tensor_tensor', dict(out=act[:, j, :], in0=pu[:, :N], in1=sgt[:], op=ALU.mult),
                 reads=[puk, f"sg{j % 2}"], writes=[("act", j)])
        for c in range(8):
            pd = psum[6 + (c % 2)]
            pdk = f"pd{c % 2}"
            for j in range(NFF):
                P.I(PE, 'matmul', MM(pd[:, :N], w2s[:, j, c * 128:(c + 1) * 128], act[:, j, :],
                                                            start=(j == 0), stop=(j == NFF - 1)),
                     reads=["w2", ("act", j)], writes=[pdk])
            P.I(DVE, 'scalar_tensor_tensor', dict(out=y2[:, c, :], in0=h1[:, c, :], scalar=ALPHA,
                                                                   in1=pd[:, :N], op0=ALU.mult, op1=ALU.add),
                 reads=[pdk, "h1_" + str(c)], writes=["y2"])
        emit_ln(P, y2, o, None, lnps[:, 16:24], lnps[:, 24:32], ones, sq, st, psum[0], psum[1], "ln", N,
                ("mB_", None), "y2")
        P.dma(SP, odst(t0, N), o[:], reads=["mB_" + str(c) for c in range(8)] + ["mB"], writes=[("out", g)],
              slot=f"st{s}")
    if standalone:
        print("P ops:", P.stats())
        return P.finish()
    P.end_phase()


def ref_P(hT, mA, mB, w13, w2, g1, b1, g2, b2):
    import ml_dtypes
    bf = lambda a: a.astype(ml_dtypes.bfloat16).astype(np.float32)

    def ln(x, g, b):
        mu = x.mean(-1, keepdims=True)
        var = ((x - mu) ** 2).mean(-1, keepdims=True)
        return (x - mu) / np.sqrt(var + LN_EPS) * g + b
    h = hT.T
    y = ALPHA * h + mA.T + mB.T
    h1 = ln(y, g1, b1)
    gu = bf(h1) @ bf(w13)
    gg, uu = gu[:, :DFF], gu[:, DFF:]
    a = gg / (1 + np.exp(-gg)) * uu
    f = bf(a) @ bf(w2)
    return ln(ALPHA * h1 + f, g2, b2).T


D = 1024
T = 8192
RET_DK, RET_DV, RET_HEADS, RET_CHUNK = 256, 512, 4, 128
LN_EPS = 1e-5
XPOS_BASE = 10000.0


def ret_tables(nh_local_ids, T):
    inv_freq = np.power(np.float32(XPOS_BASE), -np.linspace(0.0, 1.0, RET_DK // 2, dtype=np.float32)).astype(np.float32)
    t = np.arange(T, dtype=np.float32)
    ang = (t[:, None] * inv_freq[None, :]).astype(np.float32)
    cos = np.cos(ang).astype(np.float32).T
    sin = np.sin(ang).astype(np.float32).T
    pos = (np.arange(T) % RET_CHUNK).astype(np.float64)
    tabs = []
    gam = []
    for h in nh_local_ids:
        lg = np.log1p(-np.exp2(-5.0 - h))
        dq = np.exp(lg * (pos + 1.0))
        dk = np.exp(-lg * (pos + 1.0)) * RET_DK ** -0.5
        tabs += [cos * dq, sin * dq, cos * dk, sin * dk]
        gam.append(float(np.exp(lg * RET_CHUNK)))
    return np.ascontiguousarray(np.stack(tabs).astype(np.float32)), gam


def build_ret(T, heads=None, G=512, P=None, xsrc=None, mdst=None):
    standalone = P is None
    if standalone:
        P = Prog("ret")
        P.begin_phase("ret")
    nc = P.nc
    NT = G // 128
    xT = P.dram_in("xT", [D, T]) if xsrc is None else None
    wq = P.dram_in("wq", [D, 512])
    wk = P.dram_in("wk", [D, 512])
    wv = P.dram_in("wv", [D, 1024])
    wg = P.dram_in("wg", [D, 1024])
    wo = P.dram_in("wo", [1024, D])
    gng = P.dram_in("gng", [128, 1024])
    tabs = P.dram_in("tabs", [8, 128, T])
    cmask = P.dram_in("cmask", [128, 128])
    identd = P.dram_in("ident", [128, 128])
    mixT = P.dram_out("mixT", [D, T]) if mdst is None else None
    gamd = P.dram_in("gam", [128, 2])

    wqs = P.sb([128, 8, 512], BF16, "wqs")
    wks = P.sb([128, 8, 512], BF16, "wks")
    wvs = P.sb([128, 8, 1024], BF16, "wvs")
    wgs = P.sb([128, 8, 1024], BF16, "wgs")
    wos = P.sb([128, 8, D], BF16, "wos")
    gns = P.sb([128, 1024], F32, "gns")
    msk = P.sb([128, 128], F32, "msk")
    gams = P.sb([128, 2], F32, "gams")
    idb = P.sb([128, 128], BF16, "idb")
    xb = [P.sb([128, 8, G], BF16, f"xb{i}") for i in range(2)]
    tb = [P.sb([128, 8, G], F32, f"tb{i}") for i in range(2)]
    qkT = [[P.sb([128, 2, G], BF16, f"qkT{h}{w}") for w in range(2)] for h in range(2)]
    rt = [P.sb([128, G], F32, f"rt{i}") for i in range(4)]
    ktok = [P.sb([128, 256], BF16, f"ktok{i}") for i in range(2)]
    vt = [P.sb([128, 512], BF16, f"vt{i}") for i in range(2)]
    gt = [P.sb([128, 512], F32, f"gt{i}") for i in range(2)]
    sTm = [P.sb([128, 128], BF16, f"sTm{i}") for i in range(2)]
    S32 = [P.sb([128, 2, 512], F32, f"S32_{h}") for h in range(2)]
    Sb = [P.sb([128, 2, 512], BF16, f"Sb_{h}") for h in range(2)]
    on = [P.sb([128, 512], F32, f"on{i}") for i in range(2)]
    ofin = [P.sb([128, 512], BF16, f"ofin{i}") for i in range(2)]
    junk = P.sb([128, 512], F32, "junk")
    stat = [P.sb([128, 8], F32, f"stat{i}") for i in range(2)]
    oT = [P.sb([128, 8, G], BF16, f"oT{i}") for i in range(2)]
    mo = [P.sb([128, G], F32, f"mo{i}") for i in range(2)]
    psum = [P.ps([128, 512], F32, f"psum{i}") for i in range(6)]
    pst = [P.ps([128, 1024], BF16, f"pst{i}") for i in range(2)]

    P.dma(SP, gns[:], gng[:, :], writes=["gns"], slot="c0")
    P.dma(SP, msk[:], cmask[:, :], writes=["msk"], slot="c1")
    P.dma(SP, gams[:], gamd[:, :], writes=["gams"], slot="c3")
    P.dma(POOL, idb[:], identd[:, :], writes=["idb"], slot="c2")
    for k in range(8):
        r = slice(k * 128, (k + 1) * 128)
        P.dma(POOL, wqs[:, k, :], wq[r, :], writes=["wq"], slot="wq")
        P.dma(POOL, wks[:, k, :], wk[r, :], writes=["wk"], slot="wk")
        P.dma(POOL, wvs[:, k, :], wv[r, :], writes=["wv"], slot="wv")
        P.dma(POOL, wgs[:, k, :], wg[r, :], writes=["wg"], slot="wg")
        P.dma(POOL, wos[:, k, :], wo[r, :], writes=["wo"], slot="wo")
    for h in range(2):
        P.I(POOL, 'memset', dict(ap=S32[h][:], constant=0.0), writes=[f"S32_{h}"])
        P.I(POOL, 'memset', dict(ap=Sb[h][:], constant=0.0), writes=[f"Sb_{h}"])

    if xsrc is None:
        xTv = xT.rearrange("(c p) t -> p c t", p=128)
        xsrc = lambda t0, n: xTv[:, :, t0:t0 + n]
    tabv = tabs.rearrange("n p t -> p n t")
    if mdst is None:
        mixv = mixT.rearrange("(c p) t -> p c t", p=128)
        mdst = lambda c, t0, n: mixv[:, c, t0:t0 + n]

    pp = [0]

    def proj_bank():
        pp[0] ^= 1
        return psum[pp[0]], f"pp{pp[0]}"

    cnt = [0]
    ng = T // G
    for g in range(ng):
        s = g % 2
        t0 = g * G
        xbt, tbt, oTt = xb[s], tb[s], oT[s]
        xk, tk, oTk = f"xb{s}", f"tb{s}", f"oT{s}"
        P.dma(POOL, xbt[:], xsrc(t0, G), writes=[xk], slot=f"ldx{s}")
        P.dma(SP, tbt[:], tabv[:, :, t0:t0 + G], writes=[tk], slot=f"ldt{s}")
        for h in range(2):
            for w, (ws, wkey) in enumerate(((wqs, "wq"), (wks, "wk"))):
                banks = []
                for dc in range(2):
                    pb, pbk = proj_bank()
                    col = h * 256 + dc * 128
                    for k in range(8):
                        P.I(PE, 'matmul', MM(
                            pb[:, :G], ws[:, k, col:col + 128], xbt[:, k, :], start=(k == 0), stop=(k == 7)),
                            reads=[wkey, xk], writes=[pbk])
                    banks.append((pb, pbk))
                (p1, p1k), (p2, p2k) = banks
                ct = tbt[:, h * 4 + w * 2 + 0, :]
                sn = tbt[:, h * 4 + w * 2 + 1, :]
                dst = qkT[h][w]
                dk_ = f"qkT{h}{w}"
                a, b, c_, d_ = rt
                P.I(DVE, 'tensor_tensor', dict(out=a[:], in0=p1[:, :G], in1=ct, op=ALU.mult),
                     reads=[p1k, tk], writes=["rt0"])
                P.I(DVE, 'tensor_tensor', dict(out=b[:], in0=p2[:, :G], in1=sn, op=ALU.mult),
                     reads=[p2k, tk], writes=["rt1"])
                P.I(DVE, 'tensor_tensor', dict(out=c_[:], in0=p1[:, :G], in1=sn, op=ALU.mult),
                     reads=[p1k, tk], writes=["rt2"])
                P.I(DVE, 'tensor_tensor', dict(out=d_[:], in0=p2[:, :G], in1=ct, op=ALU.mult),
                     reads=[p2k, tk], writes=["rt3"])
                P.I(POOL, 'tensor_tensor', dict(out=dst[:, 0, :], in0=a[:], in1=b[:], op=ALU.subtract),
                     reads=["rt0", "rt1"], writes=[dk_])
                P.I(POOL, 'tensor_tensor', dict(out=dst[:, 1, :], in0=c_[:], in1=d_[:], op=ALU.add),
                     reads=["rt2", "rt3"], writes=[dk_])
            qT, kT = qkT[h]
            qk_, kk_ = f"qkT{h}0", f"qkT{h}1"
            for ti in range(NT):
                cnt[0] += 1
                u = cnt[0] % 2
                tsl = slice(ti * 128, (ti + 1) * 128)
                pb, pbk = proj_bank()
                for k in range(8):
                    P.I(PE, 'matmul', MM(pb[:, :], xbt[:, k, tsl], wvs[:, k, h * 512:(h + 1) * 512],
                                                            start=(k == 0), stop=(k == 7)),
                         reads=["wv", xk], writes=[pbk])
                P.I(ACT, 'activation', dict(out=vt[u][:], in_=pb[:, :], func=AF.Copy),
                     reads=[pbk], writes=[f"vt{u}"])
                pb, pbk = proj_bank()
                for k in range(8):
                    P.I(PE, 'matmul', MM(pb[:, :], xbt[:, k, tsl], wgs[:, k, h * 512:(h + 1) * 512],
                                                            start=(k == 0), stop=(k == 7)),
                         reads=["wg", xk], writes=[pbk])
                P.I(ACT, 'activation', dict(out=gt[u][:], in_=pb[:, :], func=AF.Silu),
                     reads=[pbk], writes=[f"gt{u}"])
                ptr, ptrk = pst[0], "pst0"
                for dc in range(2):
                    P.I(PE, 'transpose', dict(out=ptr[:, dc * 128:(dc + 1) * 128], in_=kT[:, dc, tsl], identity=idb[:]),
                         reads=[kk_, "idb"], writes=[ptrk])
                P.I(DVE, 'tensor_copy', dict(out=ktok[u][:], in_=ptr[:, 0:256]),
                     reads=[ptrk], writes=[f"ktok{u}"])
                psc, psck = psum[2], "psc"
                for dc in range(2):
                    P.I(PE, 'matmul', MM(psc[:, :128], kT[:, dc, tsl], qT[:, dc, tsl], start=(dc == 0), stop=(dc == 1)),
                         reads=[kk_, qk_], writes=[psck])
                P.I(DVE, 'tensor_tensor', dict(out=sTm[u][:], in0=psc[:, :128], in1=msk[:], op=ALU.mult),
                     reads=[psck, "msk"], writes=[f"sTm{u}"])
                po, pok = psum[3], "po"
                P.I(PE, 'matmul', MM(po[:, :], sTm[u][:], vt[u][:], start=True, stop=False),
                     reads=[f"sTm{u}", f"vt{u}"], writes=[pok])
                for dc in range(2):
                    P.I(PE, 'matmul', MM(po[:, :], qT[:, dc, tsl], Sb[h][:, dc, :], start=False, stop=(dc == 1)),
                         reads=[qk_, f"Sb_{h}"], writes=[pok])
                for dc in range(2):
                    pS, pSk = psum[4 + dc], f"pS{dc}"
                    P.I(PE, 'matmul', MM(pS[:, :], ktok[u][:, dc * 128:(dc + 1) * 128], vt[u][:], start=True, stop=True),
                         reads=[f"ktok{u}", f"vt{u}"], writes=[pSk])
                    P.I(DVE, 'tensor_tensor', dict(out=S32[h][:, dc, :], in0=S32[h][:, dc, :], in1=pS[:, :], op=ALU.add),
                         reads=[pSk, f"S32_{h}"], writes=[f"S32_{h}"])
                    P.I(ACT, 'activation', dict(out=S32[h][:, dc, :], in_=S32[h][:, dc, :], func=AF.Copy, scale=gams[:, h:h + 1]),
                         reads=[f"S32_{h}", "gams"], writes=[f"S32_{h}"])
                    P.I(POOL, 'tensor_copy', dict(out=Sb[h][:, dc, :], in_=S32[h][:, dc, :]),
                         reads=[f"S32_{h}"], writes=[f"Sb_{h}"])
                stt, stk = stat[u], f"stat{u}"
                P.I(ACT, 'activation', dict(out=junk[:], in_=po[:, :], func=AF.Copy, accum_out=stt[:, 0:1]),
                     reads=[pok], writes=[stk + "a"])
                P.I(ACT, 'activation', dict(out=junk[:], in_=po[:, :], func=AF.Square, accum_out=stt[:, 1:2]),
                     reads=[pok], writes=[stk + "b"])
                P.I(DVE, 'tensor_scalar', dict(out=stt[:, 2:3], in0=stt[:, 0:1], scalar1=1.0 / 512, scalar2=None, op0=ALU.mult),
                     reads=[stk + "a"], writes=[stk + "c"])
                P.I(DVE, 'tensor_tensor', dict(out=stt[:, 3:4], in0=stt[:, 2:3], in1=stt[:, 2:3], op=ALU.mult),
                     reads=[stk + "c"], writes=[stk + "d"])
                P.I(DVE, 'scalar_tensor_tensor', dict(out=stt[:, 4:5], in0=stt[:, 1:2], scalar=1.0 / 512, in1=stt[:, 3:4],
                                                                     op0=ALU.mult, op1=ALU.subtract),
                     reads=[stk + "b", stk + "d"], writes=[stk + "e"])
                P.I(DVE, 'tensor_scalar', dict(out=stt[:, 4:5], in0=stt[:, 4:5], scalar1=LN_EPS, scalar2=None, op0=ALU.add),
                     reads=[stk + "e"], writes=[stk + "e"])
                P.I(ACT, 'activation', dict(out=stt[:, 5:6], in_=stt[:, 4:5], func=AF.Sqrt),
                     reads=[stk + "e"], writes=[stk + "f"])
                P.I(DVE, 'reciprocal', dict(out=stt[:, 6:7], in_=stt[:, 5:6]),
                     reads=[stk + "f"], writes=[stk + "g"])
                P.I(DVE, 'scalar_tensor_tensor', dict(out=stt[:, 7:8], in0=stt[:, 2:3], scalar=-1.0, in1=stt[:, 6:7],
                                                                     op0=ALU.mult, op1=ALU.mult),
                     reads=[stk + "c", stk + "g"], writes=[stk + "h"])
                P.I(ACT, 'activation', dict(out=on[u][:], in_=po[:, :], func=AF.Identity,
                                                              bias=stt[:, 7:8], scale=stt[:, 6:7]),
                     reads=[pok, stk + "g", stk + "h"], writes=[f"on{u}"])
                P.I(POOL, 'tensor_tensor', dict(out=on[u][:], in0=on[u][:], in1=gns[:, h * 512:(h + 1) * 512], op=ALU.mult),
                     reads=[f"on{u}", "gns"], writes=[f"on{u}"])
                P.I(DVE, 'tensor_tensor', dict(out=ofin[u][:], in0=on[u][:], in1=gt[u][:], op=ALU.mult),
                     reads=[f"on{u}", f"gt{u}"], writes=[f"ofin{u}"])
                ptr2, ptr2k = pst[1], "pst1"
                for fc in range(4):
                    P.I(PE, 'transpose', dict(out=ptr2[:, fc * 128:(fc + 1) * 128], in_=ofin[u][:, fc * 128:(fc + 1) * 128],
                                                              identity=idb[:]),
                         reads=[f"ofin{u}", "idb"], writes=[ptr2k])
                P.I(ACT, 'activation', dict(out=oTt[:, h * 4:(h + 1) * 4, tsl],
                                                 in_=ptr2[:, 0:512].rearrange("p (c t) -> p c t", c=4), func=AF.Copy),
                     reads=[ptr2k], writes=[(oTk, h, ti)])
        okeys = [(oTk, h, ti) for h in range(2) for ti in range(NT)]
        for c in range(8):
            pw, pwk = proj_bank()
            for k in range(8):
                P.I(PE, 'matmul', MM(pw[:, :G], wos[:, k, c * 128:(c + 1) * 128], oTt[:, k, :],
                                                            start=(k == 0), stop=(k == 7)),
                     reads=["wo"] + okeys, writes=[pwk])
            m = mo[c % 2]
            mk = f"mo{c % 2}"
            P.I(ACT, 'activation', dict(out=m[:], in_=pw[:, :G], func=AF.Copy),
                 reads=[pwk], writes=[mk])
            P.dma(SP, mdst(c, t0, G), m[:], reads=[mk], writes=[("mix", g, c)], slot=f"st{c % 2}")
    print("ret ops:", P.stats())
    if standalone:
        return P.finish()
    P.end_phase()


def ref_ret(x, w_in_h, gn_g_h, w_out_h, heads):
    T = x.shape[0]
    nh = len(heads)
    x = x.astype(np.float64)
    proj = x @ w_in_h.astype(np.float64)
    q = proj[:, :nh * 256].reshape(T, nh, 256)
    k = proj[:, nh * 256:2 * nh * 256].reshape(T, nh, 256)
    v = proj[:, 2 * nh * 256:2 * nh * 256 + nh * 512].reshape(T, nh, 512)
    gate = proj[:, 2 * nh * 256 + nh * 512:].reshape(T, nh, 512)
    inv_freq = np.power(XPOS_BASE, -np.linspace(0.0, 1.0, 128))
    ang = np.arange(T)[:, None] * inv_freq[None, :]
    cos, sin = np.cos(ang)[:, None, :], np.sin(ang)[:, None, :]

    def rot(a):
        a1, a2 = a[..., :128], a[..., 128:]
        return np.concatenate([a1 * cos - a2 * sin, a1 * sin + a2 * cos], -1)
    q = rot(q)
    k = rot(k) * 256 ** -0.5
    outs = []
    for i, h in enumerate(heads):
        lg = np.log1p(-np.exp2(-5.0 - h))
        gamma = np.exp(lg)
        S = np.zeros((256, 512))
        o = np.zeros((T, 512))
        for t in range(T):
            S = gamma * S + np.outer(k[t, i], v[t, i])
            o[t] = q[t, i] @ S
        mu = o.mean(-1, keepdims=True)
        var = ((o - mu) ** 2).mean(-1, keepdims=True)
        o = (o - mu) / np.sqrt(var + LN_EPS) * gn_g_h[i * 512:(i + 1) * 512]
        g = gate[:, i]
        outs.append(o * (g / (1 + np.exp(-g))))
    o = np.concatenate(outs, -1)
    return o @ w_out_h.astype(np.float64)


D = 1024
NORM_EPS = 1e-6
NEG = -30000.0


def gdn_consts():
    r = np.arange(128)
    c = {}
    c["ident"] = np.eye(128, dtype=np.float32)
    c["i2"] = (2.0 * np.eye(128)).astype(np.float32)
    c["triu"] = (r[:, None] <= r[None, :]).astype(np.float32)
    c["negm"] = np.where(r[:, None] <= r[None, :], 0.0, NEG).astype(np.float32)
    c["strict"] = (r[:, None] < r[None, :]).astype(np.float32)
    bd = []
    for l in range(1, 8):
        b = 1 << l
        bd.append(((r[:, None] // b) == (r[None, :] // b)).astype(np.float32))
    c["bd"] = np.ascontiguousarray(np.stack(bd, 1))
    return c


STAGE = [9]


def build_gdn(T, G=512, P=None, xsrc=None, mdst=None):
    standalone = P is None
    if standalone:
        P = Prog("gdn")
        P.begin_phase("gdn")
    NT = G // 128
    NH = 4
    xT = P.dram_in("xT", [D, T]) if xsrc is None else None
    wqkvz = P.dram_in("wqkvz", [D, 2048])
    wba = P.dram_in("wba", [D, 8])
    cw = P.dram_in("cw", [128, 12, 4])
    hp = P.dram_in("hp", [128, 8])
    ngt = P.dram_in("ngt", [128, 512])
    wo = P.dram_in("wo", [512, D])
    cd = {k: P.dram_in("c_" + k, list(v.shape)) for k, v in gdn_consts().items()}
    mixT = P.dram_out("mixT", [D, T]) if mdst is None else None

    ws = P.sb([128, 8, 2048], BF16, "ws")
    wbas = P.sb([128, 8, 8], BF16, "wbas")
    wos = P.sb([128, 4, D], BF16, "wos")
    cws = P.sb([128, 12, 4], F32, "cws")
    hps = P.sb([128, 8], F32, "hps")
    negA = P.sb([128, 4], F32, "negA")
    ngs = P.sb([128, 512], F32, "ngs")
    ident = P.sb([128, 128], F32, "ident")
    identb = P.sb([128, 128], BF16, "identb")
    i2 = P.sb([128, 128], F32, "i2")
    ones = P.sb([128, 128], F32, "ones")
    triu = P.sb([128, 128], F32, "triu")
    negm = P.sb([128, 128], F32, "negm")
    strict = P.sb([128, 128], F32, "strict")
    bd = P.sb([128, 7, 128], F32, "bd")
    xb = [P.sb([128, 8, G], BF16, f"xb{i}") for i in range(2)]
    pc = P.sb([128, 12, G + 3], F32, "pc")
    cacc = [P.sb([128, G], F32, f"cacc{i}") for i in range(2)]
    qkf = P.sb([128, G], F32, "qkf")
    sqt = P.sb([128, G], F32, "sqt")
    rin = P.sb([128, G], F32, "rin")
    qT = P.sb([128, NH, G], BF16, "qT")
    kT = P.sb([128, NH, G], BF16, "kT")
    vTf = P.sb([128, NH, G], F32, "vTf")
    zs = [P.sb([128, 512], F32, f"zs{i}") for i in range(2)]
    sm = [P.sb([128, 48], F32, f"sm{i}") for i in range(2)]
    def t4(name, dt=F32, n=1):
        return [P.sb([128, NH, 128], dt, f"{name}_{i}") for i in range(n)]
    gTri4 = t4("gTri")[0]; ngTri4 = t4("ngTri")[0]; ET4 = t4("ET")[0]; ETs4 = t4("ETs")[0]; TT4 = t4("TT")[0]; TL4 = t4("TL")[0]
    attnT4 = t4("attnT", BF16, 2)
    Nn4 = t4("Nn", F32, 2); Mm4 = t4("Mm", F32, 2); Pn4 = t4("Pn")[0]; Pm4 = t4("Pm")[0]
    Nfin4 = t4("Nfin", F32, 2)
    Vtok4 = t4("Vtok", F32, 2); Vres4 = t4("Vres")[0]; Vn4 = t4("Vn", BF16)[0]; tQS4 = t4("tQS")[0]; osb4 = t4("osb")[0]
    Kdec4 = t4("Kdec", BF16, 2); tmp4 = t4("tmp")[0]
    S32 = P.sb([128, NH, 128], F32, "S32")
    Sb = P.sb([128, NH, 128], BF16, "Sb")
    junk = P.sb([128, 128], F32, "junk")
    ofin = [P.sb([128, 512], BF16, f"ofin{i}") for i in range(2)]
    oT = [P.sb([128, NH, G], BF16, f"oT{i}") for i in range(2)]
    mo = [P.sb([128, G], F32, f"mo{i}") for i in range(2)]
    psum = [P.ps([128, 512], F32, f"psum{i}") for i in range(7)]
    psb = P.ps([128, 1024], BF16, "psb")

    P.dma(SP, cws[:], cw[:, :, :], writes=["cws"], slot="c0_1")
    P.dma(SP, hps[:], hp[:, :], writes=["hps"], slot="c0_2")
    P.dma(SP, ngs[:], ngt[:, :], writes=["ngs"], slot="c0_3")
    for nm, t in (("ident", ident), ("i2", i2), ("triu", triu), ("negm", negm), ("strict", strict)):
        P.dma(SP, t[:], cd[nm][:, :], writes=[nm], slot="c_" + nm)
    P.dma(SP, bd[:], cd["bd"][:, :, :], writes=["bd"], slot="c0_5")
    P.dma(POOL, identb[:], cd["ident"][:, :], writes=["identb"], slot="c1")
    P.I(POOL, 'memset', dict(ap=ones[:], constant=1.0), writes=["ones"])
    P.I(POOL, 'memset', dict(ap=pc[:], constant=0.0), writes=["pc"])
    P.I(POOL, 'memset', dict(ap=S32[:], constant=0.0), writes=["S32"])
    P.I(POOL, 'memset', dict(ap=Sb[:], constant=0.0), writes=["Sb"])
    for k in range(8):
        r = slice(k * 128, (k + 1) * 128)
        P.dma(POOL, ws[:, k, :], wqkvz[r, :], writes=["ws"], slot="w0")
        P.dma(POOL, wbas[:, k, :], wba[r, :], writes=["wbas"], slot="w1")
    for k in range(4):
        P.dma(POOL, wos[:, k, :], wo[k * 128:(k + 1) * 128, :], writes=["wos"], slot="w2")
    P.I(ACT, 'activation', dict(out=negA[:], in_=hps[:, 0:4], func=AF.Exp), reads=["hps"], writes=["negA"])
    P.I(DVE, 'tensor_scalar', dict(out=negA[:], in0=negA[:], scalar1=-1.0, scalar2=None, op0=ALU.mult),
        reads=["negA"], writes=["negA"])

    if xsrc is None:
        xTv = xT.rearrange("(c p) t -> p c t", p=128)
        xsrc = lambda t0, n: xTv[:, :, t0:t0 + n]
    if mdst is None:
        mixv = mixT.rearrange("(c p) t -> p c t", p=128)
        mdst = lambda c, t0, n: mixv[:, c, t0:t0 + n]
    pp = [0]

    def proj_bank():
        pp[0] ^= 1
        return psum[pp[0]], f"pp{pp[0]}"

    PN = psum[2:4]
    PSET = psum[4]
    PREC = psum[5]
    PMISC = psum[6]

    tcount = [0]
    ng = T // G
    for g in range(ng):
        s = g % 2
        t0 = g * G
        xbt, xk = xb[s], f"xb{s}"
        oTt, oTk = oT[s], f"oT{s}"
        P.dma(POOL, xbt[:], xsrc(t0, G), writes=[xk], slot=f"ldx{s}")
        for ch in range(12):
            kind, h = ch // 4, ch % 4
            pb, pbk = proj_bank()
            col = kind * 512 + h * 128
            for k in range(8):
                P.I(PE, 'matmul', MM(pb[:, :G], ws[:, k, col:col + 128], xbt[:, k, :], start=(k == 0), stop=(k == 7)),
                    reads=["ws", xk], writes=[pbk])
            pck = ("pc", ch)
            P.I(ACT, 'activation', dict(out=pc[:, ch, 3:3 + G], in_=pb[:, :G], func=AF.Copy), reads=[pbk, "pc"], writes=[pck])
            ca = cacc[ch % 2]
            cak = f"cacc{ch % 2}"
            P.I(DVE, 'tensor_scalar', dict(out=ca[:], in0=pc[:, ch, 0:G], scalar1=cws[:, ch, 0:1], scalar2=None, op0=ALU.mult),
                reads=[pck, "cws"], writes=[cak])
            for j in range(1, 4):
                P.I(DVE, 'scalar_tensor_tensor', dict(out=ca[:], in0=pc[:, ch, j:j + G], scalar=cws[:, ch, j:j + 1], in1=ca[:],
                                                      op0=ALU.mult, op1=ALU.add), reads=[pck, cak], writes=[cak])
            P.I(POOL, 'tensor_copy', dict(out=pc[:, ch, 0:3], in_=pc[:, ch, G:G + 3]), reads=[pck, cak], writes=[pck])
            if kind == 2:
                P.I(ACT, 'activation', dict(out=vTf[:, h, :], in_=ca[:], func=AF.Silu), reads=[cak], writes=[("vTf", h)])
            else:
                dst, dk_ = (qT, ("qT", h)) if kind == 0 else (kT, ("kT", h))
                P.I(ACT, 'activation', dict(out=qkf[:], in_=ca[:], func=AF.Silu), reads=[cak], writes=["qkf"])
                P.I(POOL, 'tensor_tensor', dict(out=sqt[:], in0=qkf[:], in1=qkf[:], op=ALU.mult), reads=["qkf"], writes=["sqt"])
                pq, pqk = proj_bank()
                P.I(PE, 'matmul', MM(pq[:, :G], ones[:], sqt[:]), reads=["ones", "sqt"], writes=[pqk])
                P.I(DVE, 'tensor_scalar', dict(out=rin[:], in0=pq[:, :G], scalar1=NORM_EPS, scalar2=None, op0=ALU.add),
                    reads=[pqk], writes=["rin"])
                P.I(ACT, 'activation', dict(out=rin[:], in_=rin[:], func=AF.Sqrt), reads=["rin"], writes=["rin"])
                P.I(DVE, 'reciprocal', dict(out=rin[:], in_=rin[:]), reads=["rin"], writes=["rin"])
                scl = 128 ** -0.5 if kind == 0 else 1.0
                P.I(DVE, 'scalar_tensor_tensor', dict(out=dst[:, h, :], in0=qkf[:], scalar=scl, in1=rin[:], op0=ALU.mult, op1=ALU.mult),
                    reads=["qkf", "rin"], writes=[dk_])
        B2, B3, B4, B5, B7 = psum[2], psum[3], psum[4], psum[5], psb
        v4 = lambda bank: bank[:, :].rearrange("p (h t) -> p h t", h=NH)
        bin_ = lambda ap: ap.unsqueeze(2).broadcast_to([128, NH, 128])
        bmid = lambda ap: ap.unsqueeze(1).broadcast_to([128, NH, 128])
        kks = [("kT", h) for h in range(NH)]
        qks = [("qT", h) for h in range(NH)]

        def chain_a(ti, u, I):
            tsl = slice(ti * 128, (ti + 1) * 128)
            smt, smk = sm[u], f"sm{u}"
            pvt, pvtk = None, None

            pb, pbk = proj_bank()
            for k in range(8):
                I(PE, 'matmul', MM(pb[:, :], xbt[:, k, tsl], ws[:, k, 1536:2048], start=(k == 0), stop=(k == 7)),
                    reads=["ws", xk], writes=[pbk])
            I(ACT, 'activation', dict(out=zs[u][:], in_=pb[:, :], func=AF.Silu), reads=[pbk], writes=[f"zs{u}"])
            I(POOL, 'tensor_tensor', dict(out=zs[u][:], in0=zs[u][:], in1=ngs[:], op=ALU.mult), reads=[f"zs{u}", "ngs"], writes=[f"zs{u}"])
            for k in range(8):
                I(PE, 'matmul', MM(PMISC[:, 256:264], xbt[:, k, tsl], wbas[:, k, :], start=(k == 0), stop=(k == 7)),
                    reads=["wbas", xk], writes=["B6"])
            I(ACT, 'activation', dict(out=smt[:, 0:4], in_=PMISC[:, 256:260], func=AF.Sigmoid), reads=["B6"], writes=[(smk, "beta")])
            I(DVE, 'tensor_tensor', dict(out=smt[:, 32:36], in0=PMISC[:, 260:264], in1=hps[:, 4:8], op=ALU.add),
                reads=["B6", "hps"], writes=[(smk, "tmp")])
            I(ACT, 'activation', dict(out=smt[:, 32:36], in_=smt[:, 32:36], func=AF.Exp), reads=[(smk, "tmp")], writes=[(smk, "tmp")])
            I(ACT, 'activation', dict(out=smt[:, 32:36], in_=smt[:, 32:36], func=AF.Ln, bias=ones[:, 0:1]), reads=[(smk, "tmp"), "ones"], writes=[(smk, "tmp")])
            I(DVE, 'tensor_tensor', dict(out=smt[:, 4:8], in0=smt[:, 32:36], in1=negA[:], op=ALU.mult),
                reads=[(smk, "tmp"), "negA"], writes=[(smk, "g")])
            I(PE, 'matmul', MM(PMISC[:, 264:268], triu[:], smt[:, 4:8]), reads=["triu", (smk, "g")], writes=["B6"])
            I(PE, 'matmul', MM(PMISC[:, 268:272], ones[:], smt[:, 4:8]), reads=["ones", (smk, "g")], writes=["B6"])
            I(DVE, 'tensor_copy', dict(out=smt[:, 8:12], in_=PMISC[:, 264:268]), reads=["B6"], writes=[(smk, "gc")])
            I(DVE, 'tensor_scalar', dict(out=smt[:, 12:16], in0=PMISC[:, 264:268], scalar1=-1.0, scalar2=None, op0=ALU.mult),
                reads=["B6"], writes=[(smk, "negc")])
            I(ACT, 'activation', dict(out=smt[:, 16:20], in_=PMISC[:, 264:268], func=AF.Exp), reads=["B6"], writes=[(smk, "egc")])
            I(DVE, 'tensor_scalar', dict(out=smt[:, 20:24], in0=smt[:, 16:20], scalar1=-1.0, scalar2=None, op0=ALU.mult),
                reads=[(smk, "egc")], writes=[(smk, "negegc")])
            I(DVE, 'tensor_tensor', dict(out=smt[:, 24:28], in0=PMISC[:, 268:272], in1=smt[:, 8:12], op=ALU.subtract),
                reads=["B6", (smk, "gc")], writes=[(smk, "kdecs")])
            I(ACT, 'activation', dict(out=smt[:, 24:28], in_=smt[:, 24:28], func=AF.Exp), reads=[(smk, "kdecs")], writes=[(smk, "kdecs")])
            I(ACT, 'activation', dict(out=smt[:, 28:32], in_=PMISC[:, 268:272], func=AF.Exp), reads=["B6"], writes=[(smk, "etot")])

            I(POOL, 'tensor_tensor', dict(out=gTri4[:], in0=bmid(triu[:]), in1=bin_(smt[:, 4:8]), op=ALU.mult),
                reads=["triu", (smk, "g")], writes=["gTri4"])
            I(POOL, 'tensor_scalar', dict(out=ngTri4[:], in0=gTri4[:], scalar1=-1.0, scalar2=None, op0=ALU.mult),
                reads=["gTri4"], writes=["ngTri4"])
            for h in range(NH):
                r = B2[:, h * 128:(h + 1) * 128]
                I(PE, 'matmul', MM(r, ones[:], gTri4[:, h, :], start=True, stop=False), reads=["ones", "gTri4"], writes=["B2"])
                I(PE, 'matmul', MM(r, ngTri4[:, h, :], ones[:], start=False, stop=False), reads=["ones", "ngTri4"], writes=["B2"])
                I(PE, 'matmul', MM(r, ident[:], negm[:], start=False, stop=True), reads=["ident", "negm"], writes=["B2"])
            I(ACT, 'activation', dict(out=ET4[:], in_=v4(B2), func=AF.Exp), reads=["B2"], writes=["ET4"])
            I(POOL, 'tensor_tensor', dict(out=ETs4[:], in0=ET4[:], in1=bmid(strict[:]), op=ALU.mult), reads=["ET4", "strict"], writes=["ETs4"])
            for h in range(NH):
                I(PE, 'matmul', MM(B3[:, h * 128:(h + 1) * 128], kT[:, h, tsl], kT[:, h, tsl]), reads=[kks[h]], writes=["B3"])
            I(DVE, 'tensor_tensor', dict(out=TT4[:], in0=v4(B3), in1=bin_(smt[:, 0:4]), op=ALU.mult), reads=["B3", (smk, "beta")], writes=["TT4"])
            I(DVE, 'tensor_tensor', dict(out=TT4[:], in0=TT4[:], in1=ETs4[:], op=ALU.mult), reads=["TT4", "ETs4"], writes=["TT4"])
            for h in range(NH):
                I(PE, 'matmul', MM(B2[:, h * 128:(h + 1) * 128], kT[:, h, tsl], qT[:, h, tsl]), reads=[kks[h], qks[h]], writes=["B2"])
            I(DVE, 'tensor_tensor', dict(out=attnT4[u][:], in0=v4(B2), in1=ET4[:], op=ALU.mult), reads=["B2", "ET4"], writes=[("attnT4", u)])
            for h in range(NH):
                I(PE, 'transpose', dict(out=B3[:, h * 128:(h + 1) * 128], in_=TT4[:, h, :], identity=ident[:]), reads=["TT4", "ident"], writes=["B3"])
            I(DVE, 'tensor_tensor', dict(out=TL4[:], in0=v4(B3), in1=bmid(ident[:]), op=ALU.add), reads=["B3", "ident"], writes=["TL4"])
            I(POOL, 'tensor_tensor', dict(out=TT4[:], in0=TT4[:], in1=bmid(ident[:]), op=ALU.add), reads=["TT4", "ident"], writes=["TT4"])
            I(DVE, 'scalar_tensor_tensor', dict(out=Nn4[0][:], in0=TT4[:], scalar=-1.0, in1=bmid(i2[:]), op0=ALU.mult, op1=ALU.add),
                reads=["TT4", "i2"], writes=[("Nn4", 0)])
            I(POOL, 'tensor_tensor', dict(out=Nn4[0][:], in0=Nn4[0][:], in1=bmid(bd[:, 0, :]), op=ALU.mult), reads=[("Nn4", 0), "bd"], writes=[("Nn4", 0)])
            I(DVE, 'scalar_tensor_tensor', dict(out=Mm4[0][:], in0=TL4[:], scalar=-1.0, in1=bmid(i2[:]), op0=ALU.mult, op1=ALU.add),
                reads=["TL4", "i2"], writes=[("Mm4", 0)])
            I(POOL, 'tensor_tensor', dict(out=Mm4[0][:], in0=Mm4[0][:], in1=bmid(bd[:, 0, :]), op=ALU.mult), reads=[("Mm4", 0), "bd"], writes=[("Mm4", 0)])
            for h in range(NH):
                I(PE, 'transpose', dict(out=B7[:, h * 128:(h + 1) * 128], in_=kT[:, h, tsl], identity=identb[:]), reads=[kks[h], "identb"], writes=["B7"])
            I(DVE, 'tensor_tensor', dict(out=Kdec4[u][:], in0=B7[:, 0:512].rearrange("p (h t) -> p h t", h=NH), in1=bin_(smt[:, 24:28]), op=ALU.mult),
                reads=["B7", (smk, "kdecs")], writes=[("Kdec4", u)])
            pvt, pvtk = proj_bank()
            for h in range(NH):
                I(PE, 'transpose', dict(out=pvt[:, h * 128:(h + 1) * 128], in_=vTf[:, h, tsl], identity=ident[:]), reads=[("vTf", h), "ident"], writes=[pvtk])
            I(ACT, 'activation', dict(out=Vtok4[u][:], in_=v4(pvt), func=AF.Copy), reads=[pvtk], writes=[("Vtok4", u)])
            for l in range(1, 7):
                cur, nxt = (l - 1) % 2, l % 2
                last = (l == 6)
                for h in range(NH):
                    I(PE, 'matmul', MM(B2[:, h * 128:(h + 1) * 128], TL4[:, h, :], Nn4[cur][:, h, :]), reads=["TL4", ("Nn4", cur)], writes=["B2"])
                I(DVE, 'scalar_tensor_tensor', dict(out=Pn4[:], in0=v4(B2), scalar=-1.0, in1=bmid(i2[:]), op0=ALU.mult, op1=ALU.add),
                    reads=["B2", "i2"], writes=["Pn4"])
                if not last:
                    for h in range(NH):
                        I(PE, 'matmul', MM(B3[:, h * 128:(h + 1) * 128], TT4[:, h, :], Mm4[cur][:, h, :]), reads=["TT4", ("Mm4", cur)], writes=["B3"])
                    I(ACT, 'activation', dict(out=Pm4[:], in_=v4(B3), func=AF.Copy, scale=-1.0), reads=["B3"], writes=["Pm4"])
                    I(POOL, 'tensor_tensor', dict(out=Pm4[:], in0=Pm4[:], in1=bmid(i2[:]), op=ALU.add), reads=["Pm4", "i2"], writes=["Pm4"])
                for h in range(NH):
                    I(PE, 'matmul', MM(B2[:, h * 128:(h + 1) * 128], Mm4[cur][:, h, :], Pn4[:, h, :]), reads=[("Mm4", cur), "Pn4"], writes=["B2"])
                dstN, dkN = (Nfin4[u], ("Nfin4", u)) if last else (Nn4[nxt], ("Nn4", nxt))
                I(DVE, 'tensor_tensor', dict(out=dstN[:], in0=v4(B2), in1=bmid(bd[:, l, :]), op=ALU.mult), reads=["B2", "bd"], writes=[dkN])
                if not last:
                    for h in range(NH):
                        I(PE, 'matmul', MM(B3[:, h * 128:(h + 1) * 128], Nn4[cur][:, h, :], Pm4[:, h, :]), reads=[("Nn4", cur), "Pm4"], writes=["B3"])
                    I(DVE, 'tensor_tensor', dict(out=Mm4[nxt][:], in0=v4(B3), in1=bmid(bd[:, l, :]), op=ALU.mult), reads=["B3", "bd"], writes=[("Mm4", nxt)])

        def chain_b(ti, u, I):
            tsl = slice(ti * 128, (ti + 1) * 128)
            smt, smk = sm[u], f"sm{u}"

            for h in range(NH):
                I(PE, 'matmul', MM(B4[:, h * 128:(h + 1) * 128], kT[:, h, tsl], Sb[:, h, :]), reads=[kks[h], "Sb"], writes=["B4"])
            for h in range(NH):
                I(PE, 'matmul', MM(B5[:, h * 128:(h + 1) * 128], qT[:, h, tsl], Sb[:, h, :]), reads=[qks[h], "Sb"], writes=["B5"])
            I(DVE, 'tensor_tensor', dict(out=Vres4[:], in0=v4(B4), in1=bin_(smt[:, 20:24]), op=ALU.mult), reads=["B4", (smk, "negegc")], writes=["Vres4"])
            I(POOL, 'tensor_tensor', dict(out=Vres4[:], in0=Vres4[:], in1=Vtok4[u][:], op=ALU.add), reads=["Vres4", ("Vtok4", u)], writes=["Vres4"])
            I(DVE, 'tensor_tensor', dict(out=tQS4[:], in0=v4(B5), in1=bin_(smt[:, 16:20]), op=ALU.mult), reads=["B5", (smk, "egc")], writes=["tQS4"])
            for h in range(NH):
                I(PE, 'matmul', MM(B4[:, h * 128:(h + 1) * 128], Nfin4[u][:, h, :], Vres4[:, h, :]), reads=[("Nfin4", u), "Vres4"], writes=["B4"])
            I(DVE, 'tensor_tensor', dict(out=Vn4[:], in0=v4(B4), in1=bin_(smt[:, 0:4]), op=ALU.mult), reads=["B4", (smk, "beta")], writes=["Vn4"])
            for h in range(NH):
                I(PE, 'matmul', MM(B5[:, h * 128:(h + 1) * 128], attnT4[u][:, h, :], Vn4[:, h, :]), reads=[("attnT4", u), "Vn4"], writes=["B5"])
            for h in range(NH):
                I(PE, 'matmul', MM(B4[:, h * 128:(h + 1) * 128], Kdec4[u][:, h, :], Vn4[:, h, :]), reads=[("Kdec4", u), "Vn4"], writes=["B4"])
            I(DVE, 'tensor_tensor', dict(out=osb4[:], in0=v4(B5), in1=tQS4[:], op=ALU.add), reads=["B5", "tQS4"], writes=["osb4"])
            I(POOL, 'tensor_tensor', dict(out=S32[:], in0=S32[:], in1=bin_(smt[:, 28:32]), op=ALU.mult), reads=["S32", (smk, "etot")], writes=["S32"])
            I(DVE, 'tensor_tensor', dict(out=S32[:], in0=S32[:], in1=v4(B4), op=ALU.add), reads=["S32", "B4"], writes=["S32"])
            I(ACT, 'activation', dict(out=Sb[:], in_=S32[:], func=AF.Copy), reads=["S32"], writes=["Sb"])
            I(POOL, 'tensor_tensor', dict(out=tmp4[:], in0=osb4[:], in1=osb4[:], op=ALU.mult), reads=["osb4"], writes=["tmp4"])
            I(DVE, 'tensor_reduce', dict(out=smt[:, 36:40], in_=tmp4[:], axis=AX.X, op=ALU.add), reads=["tmp4"], writes=[(smk, "ss")])
            I(DVE, 'tensor_scalar', dict(out=smt[:, 40:44], in0=smt[:, 36:40], scalar1=1.0 / 128, scalar2=NORM_EPS, op0=ALU.mult, op1=ALU.add),
                reads=[(smk, "ss")], writes=[(smk, "ms")])
            I(ACT, 'activation', dict(out=smt[:, 40:44], in_=smt[:, 40:44], func=AF.Sqrt), reads=[(smk, "ms")], writes=[(smk, "ms")])
            I(DVE, 'reciprocal', dict(out=smt[:, 44:48], in_=smt[:, 40:44]), reads=[(smk, "ms")], writes=[(smk, "rs")])
            I(DVE, 'tensor_tensor', dict(out=osb4[:], in0=osb4[:], in1=bin_(smt[:, 44:48]), op=ALU.mult), reads=["osb4", (smk, "rs")], writes=["osb4"])
            I(POOL, 'tensor_tensor', dict(out=ofin[u][:].rearrange("p (h t) -> p h t", h=NH), in0=osb4[:], in1=zs[u][:].rearrange("p (h t) -> p h t", h=NH), op=ALU.mult),
                reads=["osb4", f"zs{u}"], writes=[("ofin", u)])
            for h in range(NH):
                I(PE, 'transpose', dict(out=B7[:, 512 + h * 128:512 + (h + 1) * 128], in_=ofin[u][:, h * 128:(h + 1) * 128], identity=identb[:]),
                    reads=[("ofin", u), "identb"], writes=["B7"])
            I(ACT, 'activation', dict(out=oTt[:, :, tsl], in_=B7[:, 512:1024].rearrange("p (c t) -> p c t", c=4), func=AF.Copy),
                reads=["B7"], writes=[(oTk, ti)])

        def run_interleaved(fa, fb):
            la, lb = [], []
            if fa is not None:
                fa(lambda *a, **k: la.append((a, k)))
            if fb is not None:
                fb(lambda *a, **k: lb.append((a, k)))
            na, nb = len(la), len(lb)
            ia = ib = 0
            while ia < na or ib < nb:
                if ib >= nb or (ia < na and ia * max(nb, 1) <= ib * max(na, 1)):
                    a, k = la[ia]; ia += 1
                else:
                    a, k = lb[ib]; ib += 1
                P.I(*a, **k)

        us = []
        for ti in range(NT):
            tcount[0] += 1
            us.append(tcount[0] % 2)
        for ti in range(NT + 1):
            fa = (lambda I, ti=ti: chain_a(ti, us[ti], I)) if ti < NT else None
            fb = (lambda I, ti=ti: chain_b(ti - 1, us[ti - 1], I)) if ti >= 1 else None
            run_interleaved(fa, fb)

        okeys = [(oTk, ti) for ti in range(NT)]
        for c in range(8 if STAGE[0] >= 6 else 0):
            pw, pwk = proj_bank()
            for k in range(4):
                P.I(PE, 'matmul', MM(pw[:, :G], wos[:, k, c * 128:(c + 1) * 128], oTt[:, k, :], start=(k == 0), stop=(k == 3)),
                    reads=["wos"] + okeys, writes=[pwk])
            m, mk = mo[c % 2], f"mo{c % 2}"
            P.I(ACT, 'activation', dict(out=m[:], in_=pw[:, :G], func=AF.Copy), reads=[pwk], writes=[mk])
            P.dma(SP, mdst(c, t0, G), m[:], reads=[mk], writes=[("mix", g, c)], slot=f"st{c % 2}")
    print("gdn ops:", P.stats())
    if standalone:
        return P.finish()
    P.end_phase()


def gdn_inputs(xT, a_w_in, a_conv, a_a_log, a_dt_bias, a_norm_g, a_w_out, hh):
    hs = slice(hh * 512, (hh + 1) * 512)
    secs = [a_w_in[:, s * 1024:(s + 1) * 1024][:, hs] for s in range(4)]
    wqkvz = np.ascontiguousarray(np.concatenate(secs, 1))
    wba = np.ascontiguousarray(np.concatenate([a_w_in[:, 4096 + hh * 4:4096 + hh * 4 + 4], a_w_in[:, 4104 + hh * 4:4104 + hh * 4 + 4]], 1))
    cwl = []
    for kind in range(3):
        for h in range(4):
            c0 = kind * 1024 + hh * 512 + h * 128
            cwl.append(a_conv[:, c0:c0 + 128].T)
    cw = np.ascontiguousarray(np.stack(cwl, 1))
    hp = np.concatenate([a_a_log[hh * 4:hh * 4 + 4], a_dt_bias[hh * 4:hh * 4 + 4]])
    hp = np.ascontiguousarray(np.broadcast_to(hp[None, :], (128, 8)))
    ngt = np.ascontiguousarray(np.broadcast_to(np.tile(a_norm_g, 4)[None, :], (128, 512)))
    wo = np.ascontiguousarray(a_w_out[hs, :])
    d = {"xT": xT, "wqkvz": wqkvz, "wba": wba, "cw": cw, "hp": hp, "ngt": ngt, "wo": wo}
    d.update({"c_" + k: v for k, v in gdn_consts().items()})
    return d


D = 1024
MB = 256
ROPE_THETA = 500000.0
NEG = -30000.0


def moba_consts(T):
    half = 16
    inv_freq = np.power(np.float32(ROPE_THETA), -np.arange(half, dtype=np.float32) / half).astype(np.float32)
    ang = (np.arange(T, dtype=np.float32)[:, None] * inv_freq[None, :]).astype(np.float32)
    cos, sin = np.cos(ang).astype(np.float32).T, np.sin(ang).astype(np.float32).T
    C = np.ones((128, T), np.float32)
    S = np.zeros((128, T), np.float32)
    C[0:16], C[16:32] = cos, cos
    S[0:16], S[16:32] = -sin, sin
    scale = np.float32(128 ** -0.5)
    tabs = np.ascontiguousarray(np.stack([C * scale, S * scale, C, S]))
    nb = T // MB
    own = np.arange(nb)[:, None]
    n = np.arange(nb)[None, :]
    gm = np.where(n < own, 0.0, -1e30).astype(np.float32).reshape(1, nb * nb)
    gm = np.ascontiguousarray(np.broadcast_to(gm, (128, nb * nb)))
    oh = np.zeros((nb, nb, 128), np.float32)
    oh[np.arange(nb), np.arange(nb), :] = 1.0
    oh = np.ascontiguousarray(oh.reshape(nb, nb * 128))
    k = np.arange(128)[:, None, None]
    j = np.arange(2)[None, :, None]
    q = np.arange(256)[None, None, :]
    caus = np.where(j * 128 + k <= q, 0.0, NEG).astype(np.float32)
    return {"tabs": tabs, "c_gm": gm, "c_oh": oh, "c_caus": np.ascontiguousarray(caus), "c_ident": np.eye(128, dtype=np.float32)}


def build_moba(T, G=512, P=None, xsrc=None, mdst=None):
    standalone = P is None
    if standalone:
        P = Prog("moba")
        P.begin_phase("moba")
    NH = 4
    NB = T // MB
    NG = T // G
    NTILE = T // 128
    xT = P.dram_in("xT", [D, T]) if xsrc is None else None
    wq = P.dram_in("wq", [D, 512])
    wqs = P.dram_in("wqs", [D, 512])
    wk = P.dram_in("wk", [D, 512])
    wks = P.dram_in("wks", [D, 512])
    wv = P.dram_in("wv", [D, 512])
    wo = P.dram_in("wo", [512, D])
    tabs = P.dram_in("tabs", [4, 128, T])
    gmd = P.dram_in("c_gm", [128, NB * NB])
    ohd = P.dram_in("c_oh", [NB, NB * 128])
    causd = P.dram_in("c_caus", [128, 2, 256])
    identd = P.dram_in("c_ident", [128, 128])
    mixT = P.dram_out("mixT", [D, T]) if mdst is None else None

    wsb = [P.sb([128, 8, 128], BF16, f"w{i}") for i in range(5)]
    wos = P.sb([128, 4, D], BF16, "wos")
    qT = P.sb([128, T], BF16, "qT")
    kT = P.sb([128, T], BF16, "kT")
    vtok = P.sb([128, NTILE, 128], BF16, "vtok")
    oTall = P.sb([128, NH, T], BF16, "oTall")
    xb = [P.sb([128, 8, G], BF16, f"xb{i}") for i in range(2)]
    tb = [P.sb([128, 4, G], F32, f"tb{i}") for i in range(2)]
    t1 = P.sb([128, G], F32, "t1")
    t2 = P.sb([128, G], F32, "t2")
    kf = P.sb([128, G], F32, "kf")
    ks = P.sb([128, 2], F32, "ks")
    kmb = P.sb([128, NB], BF16, "kmb")
    gms = P.sb([128, NB * NB], F32, "gms")
    ohs = P.sb([NB, NB * 128], BF16, "ohs")
    caus = P.sb([128, 2, 256], BF16, "caus")
    ident = P.sb([128, 128], F32, "ident")
    identb = P.sb([128, 128], BF16, "identb")
    onesb = P.sb([128, 128], BF16, "onesb")
    gmt = P.sb([128, 2, NB], F32, "gmt")
    mx = P.sb([128, 16], F32, "mx")
    pen = P.sb([128, 2, NB], F32, "pen")
    penT = [P.sb([NB, 256], BF16, f"penT{i}") for i in range(2)]
    PT = [P.sb([128, 512], BF16, f"PT{i}") for i in range(3)]
    rec = P.sb([128, 256], F32, "rec")
    mo = [P.sb([128, G], F32, f"mo{i}") for i in range(2)]
    psum = [P.ps([128, 512], F32, f"psum{i}") for i in range(8)]
    PA, PB, BG = psum[0], psum[1], psum[1]
    SB = psum[2:4]
    OB = psum[4:6]
    DN = psum[6:8]

    P.dma(SP, gms[:], gmd[:, :], writes=["gms"], slot="c_gm")
    P.dma(SP, ident[:], identd[:, :], writes=["ident"], slot="c_id")
    P.dma(POOL, ohs[:], ohd[:, :], writes=["ohs"], slot="c_oh")
    P.dma(POOL, caus[:], causd[:, :, :], writes=["caus"], slot="c_caus")
    P.dma(POOL, identb[:], identd[:, :], writes=["identb"], slot="c_idb")
    P.I(POOL, 'memset', dict(ap=onesb[:], constant=1.0), writes=["onesb"])
    for k in range(4):
        P.dma(POOL, wos[:, k, :], wo[k * 128:(k + 1) * 128, :], writes=["wos"], slot="w_o")

    if xsrc is None:
        xTv = xT.rearrange("(c p) t -> p c t", p=128)
        xsrc = lambda t0, n: xTv[:, :, t0:t0 + n]
    tabv = tabs.rearrange("n p t -> p n t")
    if mdst is None:
        mixv = mixT.rearrange("(c p) t -> p c t", p=128)
        mdst = lambda c, t0, n: mixv[:, c, t0:t0 + n]
    wd = [wq, wqs, wk, wks, wv]
    sbi = [0]
    odi = [0]
    gi = [0]
    for h in range(NH):
        for i in range(5):
            for k in range(8):
                P.dma(POOL, wsb[i][:, k, :], wd[i][k * 128:(k + 1) * 128, h * 128:(h + 1) * 128], writes=[f"w{i}"], slot=f"w{i}")
        for g in range(NG):
            gi[0] += 1
            s = gi[0] % 2
            t0 = g * G
            xbt, xk, tbt, tk = xb[s], f"xb{s}", tb[s], f"tb{s}"
            P.dma(POOL, xbt[:], xsrc(t0, G), writes=[xk], slot=f"ldx{s}")
            P.dma(SP, tbt[:], tabv[:, :, t0:t0 + G], writes=[tk], slot=f"ldt{s}")
            for w in range(2):
                for k in range(8):
                    P.I(PE, 'matmul', MM(PA[:, :G], wsb[2 * w][:, k, :], xbt[:, k, :], start=(k == 0), stop=(k == 7)),
                        reads=[f"w{2 * w}", xk], writes=["PA"])
                for k in range(8):
                    P.I(PE, 'matmul', MM(PB[:, :G], wsb[2 * w + 1][:, k, :], xbt[:, k, :], start=(k == 0), stop=(k == 7)),
                        reads=[f"w{2 * w + 1}", xk], writes=["PB"])
                P.I(DVE, 'tensor_tensor', dict(out=t1[:], in0=PA[:, :G], in1=tbt[:, 2 * w, :], op=ALU.mult), reads=["PA", tk], writes=["t1"])
                P.I(DVE, 'tensor_tensor', dict(out=t2[:], in0=PB[:, :G], in1=tbt[:, 2 * w + 1, :], op=ALU.mult), reads=["PB", tk], writes=["t2"])
                if w == 0:
                    P.I(POOL, 'tensor_tensor', dict(out=qT[:, t0:t0 + G], in0=t1[:], in1=t2[:], op=ALU.add), reads=["t1", "t2"], writes=[("qT", g)])
                else:
                    P.I(POOL, 'tensor_tensor', dict(out=kf[:], in0=t1[:], in1=t2[:], op=ALU.add), reads=["t1", "t2"], writes=["kf"])
                    P.I(ACT, 'activation', dict(out=kT[:, t0:t0 + G], in_=kf[:], func=AF.Copy), reads=["kf"], writes=[("kT", g)])
                    P.I(DVE, 'tensor_reduce', dict(out=ks[:], in_=kf[:].rearrange("p (b t) -> p b t", b=2), axis=AX.X, op=ALU.add),
                        reads=["kf"], writes=["ks"])
                    P.I(ACT, 'activation', dict(out=kmb[:, 2 * g:2 * g + 2], in_=ks[:], func=AF.Copy), reads=["ks"], writes=[("kmb", g)])
            for ti in range(4):
                for k in range(8):
                    P.I(PE, 'matmul', MM(PA[:, ti * 128:(ti + 1) * 128], xbt[:, k, ti * 128:(ti + 1) * 128], wsb[4][:, k, :],
                                         start=(k == 0), stop=(k == 7)), reads=["w4", xk], writes=["PA"])
            P.I(ACT, 'activation', dict(out=vtok[:, 4 * g:4 * g + 4, :], in_=PA[:, :].rearrange("p (a d) -> p a d", a=4), func=AF.Copy),
                reads=["PA"], writes=[("vtok", g)])
        qkeys = [("qT", g) for g in range(NG)]
        kkeys = [("kT", g) for g in range(NG)]
        vkeys = [("vtok", g) for g in range(NG)]
        mkeys = [("kmb", g) for g in range(NG)]
        def gate_ops(qb):
            own, q0 = qb, qb * 256
            pT, pTk = penT[qb % 2], f"penT{qb % 2}"
            qk_ = [("qT", q0 // G)]
            for j in range(2):
                P.I(PE, 'matmul', MM(BG[:, j * NB:(j + 1) * NB], qT[:, q0 + j * 128:q0 + (j + 1) * 128], kmb[:, :]),
                    reads=qk_ + mkeys, writes=["PB"])
            P.I(DVE, 'tensor_tensor', dict(out=gmt[:], in0=BG[:, 0:2 * NB].rearrange("p (j n) -> p j n", j=2),
                                           in1=gms[:, own * NB:(own + 1) * NB].unsqueeze(1).broadcast_to([128, 2, NB]), op=ALU.add),
                reads=["PB", "gms"], writes=["gmt"])
            for j in range(2):
                P.I(DVE, 'max', dict(out=mx[:, j * 8:(j + 1) * 8], in_=gmt[:, j, :]), reads=["gmt"], writes=[("mx", j)])
            for j in range(2):
                P.I(DVE, 'tensor_scalar', dict(out=pen[:, j, :], in0=gmt[:, j, :], scalar1=mx[:, j * 8 + 2:j * 8 + 3], scalar2=None, op0=ALU.is_ge),
                    reads=["gmt", ("mx", j)], writes=[("pen", j)])
            P.I(DVE, 'tensor_scalar', dict(out=pen[:], in0=pen[:], scalar1=-1.0, scalar2=-NEG, op0=ALU.add, op1=ALU.mult),
                reads=[("pen", 0), ("pen", 1)], writes=["pen"])
            for j in range(2):
                P.I(PE, 'transpose', dict(out=BG[0:NB, 128 + j * 128:128 + (j + 1) * 128], in_=pen[:, j, :], identity=ident[:]),
                    reads=["pen", "ident"], writes=["PB"])
            P.I(ACT, 'activation', dict(out=pT[:], in_=BG[0:NB, 128:384], func=AF.Copy), reads=["PB"], writes=[pTk])

        def score_ops(qb, n):
            own, q0 = qb, qb * 256
            qsl = slice(q0, q0 + 256)
            qk_ = [("qT", q0 // G)]
            x, y = n % 2, n % 3
            sbk, ptk = f"SB{x}", f"PT{y}"
            for c in range(2):
                kc = 2 * n + c
                reg = SB[x][:, c * 256:(c + 1) * 256]
                P.I(PE, 'matmul', MM(reg, kT[:, kc * 128:(kc + 1) * 128], qT[:, qsl], start=True, stop=False),
                    reads=[("kT", kc * 128 // G)] + qk_, writes=[sbk])
                if n < own:
                    P.I(PE, 'matmul', MM(reg, ohs[:, n * 128:(n + 1) * 128], penT[qb % 2][:], start=False, stop=True),
                        reads=["ohs", f"penT{qb % 2}"], writes=[sbk])
                else:
                    P.I(PE, 'matmul', MM(reg, identb[:], caus[:, c, :], start=False, stop=True), reads=["identb", "caus"], writes=[sbk])
            P.I(ACT, 'activation', dict(out=PT[y][:], in_=SB[x][:, :], func=AF.Exp), reads=[sbk], writes=[ptk])

        def pv_ops(qb, n, ob, dn, obk, dnk):
            own = qb
            y = n % 3
            ptk = f"PT{y}"
            for c in range(2):
                kc = 2 * n + c
                first, lastc = (n == 0 and c == 0), (n == own and c == 1)
                P.I(PE, 'matmul', MM(ob[:, 0:256], vtok[:, kc, :], PT[y][:, c * 256:(c + 1) * 256], start=first, stop=lastc),
                    reads=[("vtok", kc // 4), ptk], writes=[obk])
                P.I(PE, 'matmul', MM(dn[:, 0:256], onesb[:], PT[y][:, c * 256:(c + 1) * 256], start=first, stop=lastc),
                    reads=["onesb", ptk], writes=[dnk])

        for qb in range(NB):
            own = qb
            q0 = qb * 256
            qsl = slice(q0, q0 + 256)
            ob, dn = OB[qb % 2], DN[qb % 2]
            obk, dnk = f"OB{qb % 2}", f"DN{qb % 2}"
            score_ops(qb, 0)
            if qb + 1 < NB:
                gate_ops(qb + 1)
            for n in range(1, own + 1):
                score_ops(qb, n)
                pv_ops(qb, n - 1, ob, dn, obk, dnk)
            pv_ops(qb, own, ob, dn, obk, dnk)
            P.I(DVE, 'reciprocal', dict(out=rec[:], in_=dn[:, 0:256]), reads=[dnk], writes=["rec"])
            P.I(DVE, 'tensor_tensor', dict(out=oTall[:, h, qsl], in0=ob[:, 0:256], in1=rec[:], op=ALU.mult), reads=[obk, "rec"],
                writes=[("oT", h, q0 // G, (q0 // 256) % 2)])
    for g in range(NG):
        t0 = g * G
        okeys = [("oT", h, g, j) for h in range(NH) for j in range(2)]
        for c in range(8):
            pw, pwk = (PA, "PA") if c % 2 == 0 else (PB, "PB")
            for k in range(4):
                P.I(PE, 'matmul', MM(pw[:, :G], wos[:, k, c * 128:(c + 1) * 128], oTall[:, k, t0:t0 + G], start=(k == 0), stop=(k == 3)),
                    reads=["wos"] + okeys, writes=[pwk])
            m, mk = mo[c % 2], f"mo{c % 2}"
            P.I(ACT, 'activation', dict(out=m[:], in_=pw[:, :G], func=AF.Copy), reads=[pwk], writes=[mk])
            P.dma(SP, mdst(c, t0, G), m[:], reads=[mk], writes=[("mix", g, c)], slot=f"st{c % 2}")
    print("moba ops:", P.stats())
    if standalone:
        return P.finish()
    P.end_phase()


def moba_inputs(xT, b_w_qkv, b_w_out, hh, T):
    hs = slice(hh * 512, (hh + 1) * 512)
    wq = b_w_qkv[:, 0:1024][:, hs]
    wk = b_w_qkv[:, 1024:2048][:, hs]
    wv = b_w_qkv[:, 2048:3072][:, hs]
    perm = np.arange(512)
    for h in range(4):
        perm[h * 128:h * 128 + 16] = np.arange(h * 128 + 16, h * 128 + 32)
        perm[h * 128 + 16:h * 128 + 32] = np.arange(h * 128, h * 128 + 16)
    d = {"xT": xT, "wq": np.ascontiguousarray(wq), "wqs": np.ascontiguousarray(wq[:, perm]),
         "wk": np.ascontiguousarray(wk), "wks": np.ascontiguousarray(wk[:, perm]), "wv": np.ascontiguousarray(wv),
         "wo": np.ascontiguousarray(b_w_out[hs, :])}
    d.update(moba_consts(T))
    return d


B_, T_FULL = 4, 8192


NLAYERS = [4]
KINDS = [0, 1, 2, 0]
NOAG = [False]
AGUNUSED = [False]


def build_fused(T):
    H = T // 2
    P = Prog("fused")
    x0T = P.dram_in("x0T", [D, T])
    x0h = P.dram_in("x0h", [D, H])
    outT = P.dram_out("outT", [D, H])
    mixp = P.dram_tmp("mixp", [2 * D, H])
    msum = P.dram_tmp("msum", [D, H])
    hout = P.dram_tmp("hout", [D, H])
    hfull = P.dram_tmp("hfull", [2 * D, H])
    x0v = x0T.rearrange("(c p) t -> p c t", p=128)
    x0hv = x0h.rearrange("(c p) t -> p c t", p=128)
    outv = outT.rearrange("(c p) t -> p c t", p=128)
    msv = msum.rearrange("(c p) t -> p c t", p=128)
    hov = hout.rearrange("(c p) t -> p c t", p=128)
    hf4 = hfull.rearrange("(c r p) t -> r p c t", c=8, r=2, p=128)
    mp4 = mixp.rearrange("(c r p) t -> r p c t", c=8, r=2, p=128)
    hfv = [hf4[r] for r in range(2)]
    mpv = [mp4[r] for r in range(2)]
    mdst = lambda c, t0, n: mpv[t0 // H][:, c, t0 % H:t0 % H + n]
    NL = NLAYERS[0]
    for i in range(NL):
        kind = KINDS[i]
        P.dram_prefix = f"L{i}_"
        if i == 0 or NOAG[0] or AGUNUSED[0]:
            xsrc = lambda t0, n: x0v[:, :, t0:t0 + n]
        else:
            xsrc = lambda t0, n: hfv[t0 // H][:, :, t0 % H:t0 % H + n]
        P.begin_phase(f"m{i}")
        if kind == 0:
            build_gdn(T, P=P, xsrc=xsrc, mdst=mdst)
        elif kind == 1:
            build_moba(T, P=P, xsrc=xsrc, mdst=mdst)
        else:
            build_ret(T, P=P, xsrc=xsrc, mdst=mdst)
        P.begin_phase(f"rs{i}")
        for c in range(8):
            P.cc("ReduceScatter", ALU.add, mixp[c * 256:(c + 1) * 256, :], msum[c * 128:(c + 1) * 128, :], slot=f"cc_rs{i}")
        P.end_phase()
        P.begin_phase(f"p{i}")
        hs = x0hv if i == 0 else hov
        od = outv if i == NL - 1 else hov
        build_P(H, P=P, hsrc=lambda t0, n, hs=hs: hs[:, :, t0:t0 + n], masrc=lambda t0, n: msv[:, :, t0:t0 + n],
                odst=lambda t0, n, od=od: od[:, :, t0:t0 + n], single_mix=True)
        if i < NL - 1 and not NOAG[0]:
            P.begin_phase(f"ag{i}")
            for c in range(8):
                P.cc("AllGather", ALU.bypass, hout[c * 128:(c + 1) * 128, :], hfull[c * 256:(c + 1) * 256, :], slot=f"cc_ag{i}")
            P.end_phase()
    print("fused ops:", P.stats())
    return P.finish()


def _lnp(g1, b1, g2, b2):
    lay = lambda v: v.reshape(8, 128).T
    return np.ascontiguousarray(np.concatenate([lay(g1), lay(b1), lay(g2), lay(b2)], axis=1).astype(np.float32))


def _ret_inputs(c_w_in, c_gn_g, c_w_out, hh, T):
    heads = [2 * hh, 2 * hh + 1]
    hq = slice(hh * 512, (hh + 1) * 512)
    hv = slice(hh * 1024, (hh + 1) * 1024)
    tabs, gam = ret_tables(heads, T)
    return {"wq": np.ascontiguousarray(c_w_in[:, 0:1024][:, hq]), "wk": np.ascontiguousarray(c_w_in[:, 1024:2048][:, hq]),
            "wv": np.ascontiguousarray(c_w_in[:, 2048:4096][:, hv]), "wg": np.ascontiguousarray(c_w_in[:, 4096:6144][:, hv]),
            "wo": np.ascontiguousarray(c_w_out[hv, :]),
            "gng": np.ascontiguousarray(np.broadcast_to(c_gn_g[hv][None, :], (128, 1024))), "tabs": tabs,
            "cmask": np.triu(np.ones((128, 128), np.float32)), "ident": np.eye(128, dtype=np.float32),
            "gam": np.ascontiguousarray(np.broadcast_to(np.array(gam, np.float32)[None, :], (128, 2)))}


def _core_inputs(c, T, x, a_w_in, a_conv, a_a_log, a_dt_bias, a_norm_g, a_w_out, b_w_qkv, b_w_out,
                 c_w_in, c_gn_g, c_w_out, f_w13, f_w2, ln1_g, ln1_b, ln2_g, ln2_b, cache):
    b, r = c // 2, c % 2
    H = T // 2
    xT = cache.setdefault(("xT", b), np.ascontiguousarray(x[b].T))
    d = {"x0T": xT, "x0h": np.ascontiguousarray(xT[:, r * H:(r + 1) * H])}
    for i in range(NLAYERS[0]):
        kind = KINDS[i]
        j = sum(1 for q in range(i) if KINDS[q] == kind) % {0: 2, 1: 1, 2: 1}[kind]
        key = ("layer", i, r)
        if key not in cache:
            if kind == 0:
                li = gdn_inputs(None, a_w_in[j], a_conv[j], a_a_log[j], a_dt_bias[j], a_norm_g[j], a_w_out[j], hh=r)
            elif kind == 1:
                li = moba_inputs(None, b_w_qkv[j], b_w_out[j], hh=r, T=T)
            else:
                li = _ret_inputs(c_w_in[j], c_gn_g[j], c_w_out[j], hh=r, T=T)
            li.pop("xT", None)
            li["w13"] = f_w13[i]
            li["w2"] = f_w2[i]
            li["lnp"] = _lnp(ln1_g[i], ln1_b[i], ln2_g[i], ln2_b[i])
            cache[key] = {f"L{i}_{k}": v for k, v in li.items()}
        d.update(cache[key])
    return d


def kernel(x, a_w_in, a_conv, a_a_log, a_dt_bias, a_norm_g, a_w_out, b_w_qkv, b_w_out,
           c_w_in, c_gn_g, c_w_out, f_w13, f_w2, ln1_g, ln1_b, ln2_g, ln2_b):
    f32 = lambda a: np.ascontiguousarray(np.asarray(a, dtype=np.float32))
    args = [f32(a) for a in (x, a_w_in, a_conv, a_a_log, a_dt_bias, a_norm_g, a_w_out, b_w_qkv, b_w_out,
                             c_w_in, c_gn_g, c_w_out, f_w13, f_w2, ln1_g, ln1_b, ln2_g, ln2_b)]
    T = args[0].shape[1]
    cores = list(range(8))
    cache = {}
    ims = [_core_inputs(c, T, *args, cache=cache) for c in cores]
    nc = build_fused(T)
    res = run_bass_kernel_spmd(nc, ims, core_ids=cores).results
    out = np.empty((B_, T, D), np.float32)
    H = T // 2
    for c in cores:
        out[c // 2, (c % 2) * H:(c % 2 + 1) * H, :] = res[c]["outT"].T
    return out
```

```python
import numpy as np
from contextlib import ExitStack
import concourse.bass as bass
import concourse.mybir as mybir
from concourse.bass_utils import run_bass_kernel_spmd

F32 = mybir.dt.float32
BF16 = mybir.dt.bfloat16
AF = mybir.ActivationFunctionType
ALU = mybir.AluOpType
AX = mybir.AxisListType

PE, ACT, DVE, POOL, SP = "PE", "ACT", "DVE", "POOL", "SP"


def MM(out, lhsT, rhs, start=True, stop=True):
    return dict(out=out, lhsT=lhsT, rhs=rhs, start=start, stop=stop)


class Prog:
    ENGS = (PE, ACT, DVE, POOL, SP)
    PAIRS = [[0, 1], [2, 3], [4, 5], [6, 7]]

    def __init__(self, name="k"):
        self.nc = bass.Bass("TRN2", target_bir_lowering=False)
        self.stack = ExitStack()
        self.pstack = None
        self.dma_sems = {}
        self.esem = None
        self.base = {e: 0 for e in self.ENGS}
        self.dram_prefix = ""
        self.tag = ""
        self.nphase = 0
        self.total_ops = {e: 0 for e in self.ENGS}
        self._reset()

    def _reset(self):
        self.ops = {e: [] for e in self.ENGS}
        self.last_write = {}
        self.readers = {}
        self.seen = {e: {} for e in self.ENGS}
        self.marked = {e: set() for e in self.ENGS}

    def begin_phase(self, tag):
        self.tag = tag
        self.pstack = ExitStack()
        self._reset()

    def sb(self, shape, dtype, name):
        return self.pstack.enter_context(self.nc.sbuf_tensor(f"{self.tag}_{name}", list(shape), dtype))

    def ps(self, shape, dtype, name):
        return self.pstack.enter_context(self.nc.psum_tensor(f"{self.tag}_{name}", list(shape), dtype))

    def dram_in(self, name, shape, dtype=F32):
        return self.nc.dram_tensor(self.dram_prefix + name, list(shape), dtype, kind="ExternalInput").ap()

    def dram_out(self, name, shape, dtype=F32):
        return self.nc.dram_tensor(name, list(shape), dtype, kind="ExternalOutput").ap()

    def dram_tmp(self, name, shape, dtype=F32):
        return self.nc.dram_tensor(name, list(shape), dtype).ap()

    def _deps(self, eng, reads, writes):
        deps = []
        for k in reads:
            if k in self.last_write:
                deps.append(self.last_write[k])
        for k in writes:
            if k in self.last_write:
                deps.append(self.last_write[k])
            for r in self.readers.get(k, ()):
                deps.append(r)
        out = {}
        for (prod, val) in deps:
            if prod == eng and (eng == PE):
                continue
            if self.seen[eng].get(prod, -1) >= val:
                continue
            if out.get(prod, -1) < val:
                out[prod] = val
        for prod, val in out.items():
            self.seen[eng][prod] = val
            if prod in self.ops:
                self.marked[prod].add(val)
        return list(out.items())

    def op(self, eng, fn, reads=(), writes=()):
        deps = self._deps(eng, reads, writes)
        idx = len(self.ops[eng])
        self.ops[eng].append(("op", deps, fn, idx))
        tag = (eng, idx)
        for k in writes:
            self.last_write[k] = tag
            self.readers[k] = []
        for k in reads:
            if k not in writes:
                self.readers.setdefault(k, []).append(tag)
        return idx

    def I(self, eng, name, kw, reads=(), writes=()):
        return self.op(eng, lambda e, name=name, kw=kw: getattr(e, name)(**kw), reads, writes)

    def _slot(self, slot):
        if slot not in self.dma_sems:
            sem = self.stack.enter_context(self.nc.semaphore(f"d{len(self.dma_sems)}"))
            self.dma_sems[slot] = [sem, 0]
        return self.dma_sems[slot]

    def dma(self, queue, out, in_, reads=(), writes=(), slot=None):
        assert slot is not None
        deps = self._deps(queue, reads, writes)
        ent = self._slot(slot)
        ent[1] += 16
        val = ent[1]
        self.ops[queue].append(("dma", deps, (out, in_, ent[0]), None))
        tag = (("dma", slot), val)
        for k in writes:
            self.last_write[k] = tag
            self.readers[k] = []
        for k in reads:
            self.readers.setdefault(k, []).append(tag)

    def cc(self, kind, alu, in_, out, slot):
        ent = self._slot(slot)
        ent[1] += 1
        self.ops[POOL].append(("cc", [], (kind, alu, in_, out, ent[0]), None))

    def end_phase(self):
        nc = self.nc
        if self.esem is None:
            self.esem = {e: self.stack.enter_context(nc.semaphore(f"e{e}")) for e in self.ENGS}
        esem = self.esem
        for e in self.ENGS:
            last = [idx for kind, _, _, idx in self.ops[e] if kind == "op"]
            if last:
                self.marked[e].add(last[-1])
        ranks = {}
        for e in self.ENGS:
            m = sorted(self.marked[e])
            ranks[e] = {idx: self.base[e] + i + 1 for i, idx in enumerate(m)}
        pre_eng = [(esem[e], self.base[e]) for e in self.ENGS if self.base[e] > 0]
        pre_dma = [(ent[0], ent[1]) for ent in self.prev_dma] if self.nphase > 0 else []

        def emit(ename, e):
            if not self.ops[ename]:
                return
            for sem, val in pre_eng + pre_dma:
                e.wait_ge(sem, val)
            for kind, deps, payload, idx in self.ops[ename]:
                for prod, val in deps:
                    if isinstance(prod, tuple):
                        e.wait_ge(self.dma_sems[prod[1]][0], val)
                    else:
                        e.wait_ge(esem[prod], ranks[prod][val])
                if kind == "op":
                    ins = payload(e)
                    if idx in ranks[ename]:
                        ins.then_inc(esem[ename], 1)
                elif kind == "dma":
                    out, in_, sem = payload
                    e.dma_start(out=out, in_=in_).then_inc(sem, 16)
                else:
                    ckind, alu, in_, out, sem = payload
                    e.collective_compute(ckind, alu, replica_groups=self.PAIRS, ins=[in_], outs=[out]).then_inc(sem)

        with nc.Block() as block:
            @block.tensor
            def _(e):
                emit(PE, e)

            @block.scalar
            def _(e):
                emit(ACT, e)

            @block.vector
            def _(e):
                emit(DVE, e)

            @block.gpsimd
            def _(e):
                emit(POOL, e)

            @block.sync
            def _(e):
                emit(SP, e)
        for e in self.ENGS:
            self.base[e] += len(self.marked[e])
            self.total_ops[e] += len(self.ops[e])
        self.prev_dma = [[ent[0], ent[1]] for ent in self.dma_sems.values()]
        self.nphase += 1
        self.pstack.close()
        self.pstack = None
        self._reset()

    def finish(self):
        if self.pstack is not None:
            self.end_phase()
        nc = self.nc
        finals = [(ent[0], ent[1]) for ent in self.dma_sems.values()] + [(self.esem[e], self.base[e]) for e in self.ENGS if self.base[e] > 0]
        with nc.Block() as block:
            @block.sync
            def _(e):
                for sem, val in finals:
                    e.wait_ge(sem, val)
        self.stack.close()
        return nc

    def stats(self):
        return {e: self.total_ops[e] + len(v) for e, v in self.ops.items()}


D = 1024
DFF = 2816
NFF = DFF // 128
ALPHA = 8 ** 0.25
LN_EPS = 1e-5


def bc_mid(ap2d, n):
    return ap2d.unsqueeze(1).broadcast_to([ap2d.shape[0], n, ap2d.shape[1]])


def emit_ln(P, y, out_f32, out_bf, g_ap, b_ap, ones, sq, st, psA, psB, keyp, N, out_keys, ykey):
    nc = P.nc
    mean, msq, var, rstd = st
    P.I(ACT, 'activation', dict(out=sq[:], in_=y[:], func=AF.Square), reads=[ykey], writes=[keyp + "sq"])
    for c in range(8):
        P.I(PE, 'matmul', MM(psA[:, :N], ones[:], y[:, c, :], start=(c == 0), stop=(c == 7)),
             reads=[ykey, "ones"], writes=[keyp + "psA"])
    for c in range(8):
        P.I(PE, 'matmul', MM(psB[:, :N], ones[:], sq[:, c, :], start=(c == 0), stop=(c == 7)),
             reads=[keyp + "sq", "ones"], writes=[keyp + "psB"])
    P.I(DVE, 'tensor_scalar', dict(out=mean[:], in0=psA[:, :N], scalar1=1.0 / D, scalar2=None, op0=ALU.mult),
         reads=[keyp + "psA"], writes=[keyp + "mean"])
    P.I(DVE, 'tensor_tensor', dict(out=msq[:], in0=mean[:], in1=mean[:], op=ALU.mult),
         reads=[keyp + "mean"], writes=[keyp + "msq"])
    P.I(DVE, 'scalar_tensor_tensor', dict(out=var[:], in0=psB[:, :N], scalar=1.0 / D, in1=msq[:],
                                               op0=ALU.mult, op1=ALU.subtract),
         reads=[keyp + "psB", keyp + "msq"], writes=[keyp + "var"])
    P.I(DVE, 'tensor_scalar', dict(out=var[:], in0=var[:], scalar1=LN_EPS, scalar2=None, op0=ALU.add),
         reads=[keyp + "var"], writes=[keyp + "var"])
    P.I(ACT, 'activation', dict(out=var[:], in_=var[:], func=AF.Sqrt),
         reads=[keyp + "var"], writes=[keyp + "var"])
    P.I(DVE, 'reciprocal', dict(out=rstd[:], in_=var[:]),
         reads=[keyp + "var"], writes=[keyp + "rstd"])
    P.I(DVE, 'tensor_tensor', dict(out=y[:], in0=y[:], in1=bc_mid(mean[:], 8), op=ALU.subtract),
         reads=[ykey, keyp + "mean"], writes=[ykey])
    P.I(DVE, 'tensor_tensor', dict(out=y[:], in0=y[:], in1=bc_mid(rstd[:], 8), op=ALU.mult),
         reads=[ykey, keyp + "rstd"], writes=[ykey])
    for c in range(8):
        P.I(ACT, 'activation', dict(out=out_f32[:, c, :], in_=y[:, c, :], func=AF.Identity,
                                              bias=b_ap[:, c:c + 1], scale=g_ap[:, c:c + 1]),
             reads=[ykey, "lnp"], writes=[out_keys[0] + str(c)])
    if out_bf is not None:
        P.I(POOL, 'tensor_copy', dict(out=out_bf[:], in_=out_f32[:]),
             reads=[out_keys[0] + str(c) for c in range(8)], writes=[out_keys[1]])


def build_P(ntok, N=256, P=None, hsrc=None, masrc=None, mbsrc=None, odst=None, single_mix=False):
    standalone = P is None
    if standalone:
        P = Prog("P")
        P.begin_phase("P")
    nc = P.nc
    if standalone:
        hT = P.dram_in("hT", [D, ntok])
        mA = P.dram_in("mA", [D, ntok])
        mB = P.dram_in("mB", [D, ntok])
    w13 = P.dram_in("w13", [D, 2 * DFF])
    w2 = P.dram_in("w2", [DFF, D])
    lnp = P.dram_in("lnp", [128, 32])
    if standalone:
        outT = P.dram_out("outT", [D, ntok])

    w13s = P.sb([128, 8, 2 * DFF], BF16, "w13s")
    w2s = P.sb([128, NFF, D], BF16, "w2s")
    lnps = P.sb([128, 32], F32, "lnps")
    ones = P.sb([128, 128], F32, "ones")
    hbuf = [P.sb([128, 8, N], F32, f"hb{i}") for i in range(2)]
    mAt = P.sb([128, 8, N], F32, "mAt")
    mBt = P.sb([128, 8, N], F32, "mBt")
    sq = P.sb([128, 8, N], F32, "sq")
    h1 = P.sb([128, 8, N], F32, "h1")
    h1b = P.sb([128, 8, N], BF16, "h1b")
    act = P.sb([128, NFF, N], BF16, "act")
    sg = [P.sb([128, N], F32, f"sg{i}") for i in range(2)]
    st = [P.sb([128, N], F32, f"st{i}") for i in range(4)]
    psum = [P.ps([128, 512], F32, f"psum{i}") for i in range(8)]

    P.I(POOL, 'memset', dict(ap=ones[:], constant=1.0), writes=["ones"])
    P.dma(SP, lnps[:], lnp[:, :], writes=["lnp"], slot="lnp")
    for k in range(8):
        P.dma(POOL, w13s[:, k, :], w13[k * 128:(k + 1) * 128, :], writes=["w13"], slot="w13")
    for j in range(NFF):
        P.dma(POOL, w2s[:, j, :], w2[j * 128:(j + 1) * 128, :], writes=["w2"], slot="w2")
    w13keys = [("w13", k) for k in range(8)]
    w2keys = [("w2", j) for j in range(NFF)]

    if standalone:
        hTv = hT.rearrange("(c p) t -> p c t", p=128)
        mAv = mA.rearrange("(c p) t -> p c t", p=128)
        mBv = mB.rearrange("(c p) t -> p c t", p=128)
        outv = outT.rearrange("(c p) t -> p c t", p=128)
        hsrc = lambda t0, n: hTv[:, :, t0:t0 + n]
        masrc = lambda t0, n: mAv[:, :, t0:t0 + n]
        mbsrc = lambda t0, n: mBv[:, :, t0:t0 + n]
        odst = lambda t0, n: outv[:, :, t0:t0 + n]

    ng = ntok // N
    for g in range(ng):
        s = g % 2
        t0 = g * N
        hb, ma, mb, o, y2 = hbuf[s], mAt, mBt, mBt, mAt
        hk, mak, mbk = f"h{s}", "y2", "mB"
        P.dma(SP, hb[:], hsrc(t0, N), writes=[hk], slot=f"ldh{s}")
        P.dma(SP, ma[:], masrc(t0, N), writes=[mak], slot=f"ldA{s}")
        if not single_mix:
            P.dma(SP, mb[:], mbsrc(t0, N), writes=[mbk] + ["mB_" + str(c) for c in range(8)], slot=f"ldB{s}")
        P.I(DVE, 'scalar_tensor_tensor', dict(out=hb[:], in0=hb[:], scalar=ALPHA, in1=ma[:],
                                                                 op0=ALU.mult, op1=ALU.add),
             reads=[hk, mak], writes=[hk])
        if not single_mix:
            P.I(POOL, 'tensor_tensor', dict(out=hb[:], in0=hb[:], in1=mb[:], op=ALU.add),
                reads=[hk, mbk], writes=[hk])
        emit_ln(P, hb, h1, h1b, lnps[:, 0:8], lnps[:, 8:16], ones, sq, st, psum[0], psum[1], "ln", N,
                ("h1_", "h1b"), hk)
        for j in range(NFF):
            pg = psum[2 + (j % 2)]
            pu = psum[4 + (j % 2)]
            pgk, puk = f"pg{j % 2}", f"pu{j % 2}"
            for k in range(8):
                P.I(PE, 'matmul', MM(pg[:, :N], w13s[:, k, j * 128:(j + 1) * 128], h1b[:, k, :],
                                                            start=(k == 0), stop=(k == 7)),
                     reads=["w13", "h1b"], writes=[pgk])
            for k in range(8):
                P.I(PE, 'matmul', MM(pu[:, :N], w13s[:, k, DFF + j * 128:DFF + (j + 1) * 128],
                                                            h1b[:, k, :], start=(k == 0), stop=(k == 7)),
                     reads=["w13", "h1b"], writes=[puk])
            sgt = sg[j % 2]
            P.I(ACT, 'activation', dict(out=sgt[:], in_=pg[:, :N], func=AF.Silu),
                 reads=[pgk], writes=[f"sg{j % 2}"])
            P.I(DVE, 'tensor_tensor', dict(out=act[:, j, :], in0=pu[:, :N], in1=sgt[:], op=ALU.mult),
                 reads=[puk, f"sg{j % 2}"], writes=[("act", j)])
        for c in range(8):
            pd = psum[6 + (c % 2)]
            pdk = f"pd{c % 2}"
            for j in range(NFF):
                P.I(PE, 'matmul', MM(pd[:, :N], w2s[:, j, c * 128:(c + 1) * 128], act[:, j, :],
                                                            start=(j == 0), stop=(j == NFF - 1)),
                     reads=["w2", ("act", j)], writes=[pdk])
            P.I(DVE, 'scalar_tensor_tensor', dict(out=y2[:, c, :], in0=h1[:, c, :], scalar=ALPHA,
                                                                   in1=pd[:, :N], op0=ALU.mult, op1=ALU.add),
                 reads=[pdk, "h1_" + str(c)], writes=["y2"])
        emit_ln(P, y2, o, None, lnps[:, 16:24], lnps[:, 24:32], ones, sq, st, psum[0], psum[1], "ln", N,
                ("mB_", None), "y2")
        P.dma(SP, odst(t0, N), o[:], reads=["mB_" + str(c) for c in range(8)] + ["mB"], writes=[("out", g)],
              slot=f"st{s}")
    if standalone:
        print("P ops:", P.stats())
        return P.finish()
    P.end_phase()


def ref_P(hT, mA, mB, w13, w2, g1, b1, g2, b2):
    import ml_dtypes
    bf = lambda a: a.astype(ml_dtypes.bfloat16).astype(np.float32)

    def ln(x, g, b):
        mu = x.mean(-1, keepdims=True)
        var = ((x - mu) ** 2).mean(-1, keepdims=True)
        return (x - mu) / np.sqrt(var + LN_EPS) * g + b
    h = hT.T
    y = ALPHA * h + mA.T + mB.T
    h1 = ln(y, g1, b1)
    gu = bf(h1) @ bf(w13)
    gg, uu = gu[:, :DFF], gu[:, DFF:]
    a = gg / (1 + np.exp(-gg)) * uu
    f = bf(a) @ bf(w2)
    return ln(ALPHA * h1 + f, g2, b2).T


D = 1024
T = 8192
RET_DK, RET_DV, RET_HEADS, RET_CHUNK = 256, 512, 4, 128
LN_EPS = 1e-5
XPOS_BASE = 10000.0


def ret_tables(nh_local_ids, T):
    inv_freq = np.power(np.float32(XPOS_BASE), -np.linspace(0.0, 1.0, RET_DK // 2, dtype=np.float32)).astype(np.float32)
    t = np.arange(T, dtype=np.float32)
    ang = (t[:, None] * inv_freq[None, :]).astype(np.float32)
    cos = np.cos(ang).astype(np.float32).T
    sin = np.sin(ang).astype(np.float32).T
    pos = (np.arange(T) % RET_CHUNK).astype(np.float64)
    tabs = []
    gam = []
    for h in nh_local_ids:
        lg = np.log1p(-np.exp2(-5.0 - h))
        dq = np.exp(lg * (pos + 1.0))
        dk = np.exp(-lg * (pos + 1.0)) * RET_DK ** -0.5
        tabs += [cos * dq, sin * dq, cos * dk, sin * dk]
        gam.append(float(np.exp(lg * RET_CHUNK)))
    return np.ascontiguousarray(np.stack(tabs).astype(np.float32)), gam


def build_ret(T, heads=None, G=512, P=None, xsrc=None, mdst=None):
    standalone = P is None
    if standalone:
        P = Prog("ret")
        P.begin_phase("ret")
    nc = P.nc
    NT = G // 128
    xT = P.dram_in("xT", [D, T]) if xsrc is None else None
    wq = P.dram_in("wq", [D, 512])
    wk = P.dram_in("wk", [D, 512])
    wv = P.dram_in("wv", [D, 1024])
    wg = P.dram_in("wg", [D, 1024])
    wo = P.dram_in("wo", [1024, D])
    gng = P.dram_in("gng", [128, 1024])
    tabs = P.dram_in("tabs", [8, 128, T])
    cmask = P.dram_in("cmask", [128, 128])
    identd = P.dram_in("ident", [128, 128])
    mixT = P.dram_out("mixT", [D, T]) if mdst is None else None
    gamd = P.dram_in("gam", [128, 2])

    wqs = P.sb([128, 8, 512], BF16, "wqs")
    wks = P.sb([128, 8, 512], BF16, "wks")
    wvs = P.sb([128, 8, 1024], BF16, "wvs")
    wgs = P.sb([128, 8, 1024], BF16, "wgs")
    wos = P.sb([128, 8, D], BF16, "wos")
    gns = P.sb([128, 1024], F32, "gns")
    msk = P.sb([128, 128], F32, "msk")
    gams = P.sb([128, 2], F32, "gams")
    idb = P.sb([128, 128], BF16, "idb")
    xb = [P.sb([128, 8, G], BF16, f"xb{i}") for i in range(2)]
    tb = [P.sb([128, 8, G], F32, f"tb{i}") for i in range(2)]
    qkT = [[P.sb([128, 2, G], BF16, f"qkT{h}{w}") for w in range(2)] for h in range(2)]
    rt = [P.sb([128, G], F32, f"rt{i}") for i in range(4)]
    ktok = [P.sb([128, 256], BF16, f"ktok{i}") for i in range(2)]
    vt = [P.sb([128, 512], BF16, f"vt{i}") for i in range(2)]
    gt = [P.sb([128, 512], F32, f"gt{i}") for i in range(2)]
    sTm = [P.sb([128, 128], BF16, f"sTm{i}") for i in range(2)]
    S32 = [P.sb([128, 2, 512], F32, f"S32_{h}") for h in range(2)]
    Sb = [P.sb([128, 2, 512], BF16, f"Sb_{h}") for h in range(2)]
    on = [P.sb([128, 512], F32, f"on{i}") for i in range(2)]
    ofin = [P.sb([128, 512], BF16, f"ofin{i}") for i in range(2)]
    junk = P.sb([128, 512], F32, "junk")
    stat = [P.sb([128, 8], F32, f"stat{i}") for i in range(2)]
    oT = [P.sb([128, 8, G], BF16, f"oT{i}") for i in range(2)]
    mo = [P.sb([128, G], F32, f"mo{i}") for i in range(2)]
    psum = [P.ps([128, 512], F32, f"psum{i}") for i in range(6)]
    pst = [P.ps([128, 1024], BF16, f"pst{i}") for i in range(2)]

    P.dma(SP, gns[:], gng[:, :], writes=["gns"], slot="c0")
    P.dma(SP, msk[:], cmask[:, :], writes=["msk"], slot="c1")
    P.dma(SP, gams[:], gamd[:, :], writes=["gams"], slot="c3")
    P.dma(POOL, idb[:], identd[:, :], writes=["idb"], slot="c2")
    for k in range(8):
        r = slice(k * 128, (k + 1) * 128)
        P.dma(POOL, wqs[:, k, :], wq[r, :], writes=["wq"], slot="wq")
        P.dma(POOL, wks[:, k, :], wk[r, :], writes=["wk"], slot="wk")
        P.dma(POOL, wvs[:, k, :], wv[r, :], writes=["wv"], slot="wv")
        P.dma(POOL, wgs[:, k, :], wg[r, :], writes=["wg"], slot="wg")
        P.dma(POOL, wos[:, k, :], wo[r, :], writes=["wo"], slot="wo")
    for h in range(2):
        P.I(POOL, 'memset', dict(ap=S32[h][:], constant=0.0), writes=[f"S32_{h}"])
        P.I(POOL, 'memset', dict(ap=Sb[h][:], constant=0.0), writes=[f"Sb_{h}"])

    if xsrc is None:
        xTv = xT.rearrange("(c p) t -> p c t", p=128)
        xsrc = lambda t0, n: xTv[:, :, t0:t0 + n]
    tabv = tabs.rearrange("n p t -> p n t")
    if mdst is None:
        mixv = mixT.rearrange("(c p) t -> p c t", p=128)
        mdst = lambda c, t0, n: mixv[:, c, t0:t0 + n]

    pp = [0]

    def proj_bank():
        pp[0] ^= 1
        return psum[pp[0]], f"pp{pp[0]}"

    cnt = [0]
    ng = T // G
    for g in range(ng):
        s = g % 2
        t0 = g * G
        xbt, tbt, oTt = xb[s], tb[s], oT[s]
        xk, tk, oTk = f"xb{s}", f"tb{s}", f"oT{s}"
        P.dma(POOL, xbt[:], xsrc(t0, G), writes=[xk], slot=f"ldx{s}")
        P.dma(SP, tbt[:], tabv[:, :, t0:t0 + G], writes=[tk], slot=f"ldt{s}")
        for h in range(2):
            for w, (ws, wkey) in enumerate(((wqs, "wq"), (wks, "wk"))):
                banks = []
                for dc in range(2):
                    pb, pbk = proj_bank()
                    col = h * 256 + dc * 128
                    for k in range(8):
                        P.I(PE, 'matmul', MM(
                            pb[:, :G], ws[:, k, col:col + 128], xbt[:, k, :], start=(k == 0), stop=(k == 7)),
                            reads=[wkey, xk], writes=[pbk])
                    banks.append((pb, pbk))
                (p1, p1k), (p2, p2k) = banks
                ct = tbt[:, h * 4 + w * 2 + 0, :]
                sn = tbt[:, h * 4 + w * 2 + 1, :]
                dst = qkT[h][w]
                dk_ = f"qkT{h}{w}"
                a, b, c_, d_ = rt
                P.I(DVE, 'tensor_tensor', dict(out=a[:], in0=p1[:, :G], in1=ct, op=ALU.mult),
                     reads=[p1k, tk], writes=["rt0"])
                P.I(DVE, 'tensor_tensor', dict(out=b[:], in0=p2[:, :G], in1=sn, op=ALU.mult),
                     reads=[p2k, tk], writes=["rt1"])
                P.I(DVE, 'tensor_tensor', dict(out=c_[:], in0=p1[:, :G], in1=sn, op=ALU.mult),
                     reads=[p1k, tk], writes=["rt2"])
                P.I(DVE, 'tensor_tensor', dict(out=d_[:], in0=p2[:, :G], in1=ct, op=ALU.mult),
                     reads=[p2k, tk], writes=["rt3"])
                P.I(POOL, 'tensor_tensor', dict(out=dst[:, 0, :], in0=a[:], in1=b[:], op=ALU.subtract),
                     reads=["rt0", "rt1"], writes=[dk_])
                P.I(POOL, 'tensor_tensor', dict(out=dst[:, 1, :], in0=c_[:], in1=d_[:], op=ALU.add),
                     reads=["rt2", "rt3"], writes=[dk_])
            qT, kT = qkT[h]
            qk_, kk_ = f"qkT{h}0", f"qkT{h}1"
            for ti in range(NT):
                cnt[0] += 1
                u = cnt[0] % 2
                tsl = slice(ti * 128, (ti + 1) * 128)
                pb, pbk = proj_bank()
                for k in range(8):
                    P.I(PE, 'matmul', MM(pb[:, :], xbt[:, k, tsl], wvs[:, k, h * 512:(h + 1) * 512],
                                                            start=(k == 0), stop=(k == 7)),
                         reads=["wv", xk], writes=[pbk])
                P.I(ACT, 'activation', dict(out=vt[u][:], in_=pb[:, :], func=AF.Copy),
                     reads=[pbk], writes=[f"vt{u}"])
                pb, pbk = proj_bank()
                for k in range(8):
                    P.I(PE, 'matmul', MM(pb[:, :], xbt[:, k, tsl], wgs[:, k, h * 512:(h + 1) * 512],
                                                            start=(k == 0), stop=(k == 7)),
                         reads=["wg", xk], writes=[pbk])
                P.I(ACT, 'activation', dict(out=gt[u][:], in_=pb[:, :], func=AF.Silu),
                     reads=[pbk], writes=[f"gt{u}"])
                ptr, ptrk = pst[0], "pst0"
                for dc in range(2):
                    P.I(PE, 'transpose', dict(out=ptr[:, dc * 128:(dc + 1) * 128], in_=kT[:, dc, tsl], identity=idb[:]),
                         reads=[kk_, "idb"], writes=[ptrk])
                P.I(DVE, 'tensor_copy', dict(out=ktok[u][:], in_=ptr[:, 0:256]),
                     reads=[ptrk], writes=[f"ktok{u}"])
                psc, psck = psum[2], "psc"
                for dc in range(2):
                    P.I(PE, 'matmul', MM(psc[:, :128], kT[:, dc, tsl], qT[:, dc, tsl], start=(dc == 0), stop=(dc == 1)),
                         reads=[kk_, qk_], writes=[psck])
                P.I(DVE, 'tensor_tensor', dict(out=sTm[u][:], in0=psc[:, :128], in1=msk[:], op=ALU.mult),
                     reads=[psck, "msk"], writes=[f"sTm{u}"])
                po, pok = psum[3], "po"
                P.I(PE, 'matmul', MM(po[:, :], sTm[u][:], vt[u][:], start=True, stop=False),
                     reads=[f"sTm{u}", f"vt{u}"], writes=[pok])
                for dc in range(2):
                    P.I(PE, 'matmul', MM(po[:, :], qT[:, dc, tsl], Sb[h][:, dc, :], start=False, stop=(dc == 1)),
                         reads=[qk_, f"Sb_{h}"], writes=[pok])
                for dc in range(2):
                    pS, pSk = psum[4 + dc], f"pS{dc}"
                    P.I(PE, 'matmul', MM(pS[:, :], ktok[u][:, dc * 128:(dc + 1) * 128], vt[u][:], start=True, stop=True),
                         reads=[f"ktok{u}", f"vt{u}"], writes=[pSk])
                    P.I(DVE, 'tensor_tensor', dict(out=S32[h][:, dc, :], in0=S32[h][:, dc, :], in1=pS[:, :], op=ALU.add),
                         reads=[pSk, f"S32_{h}"], writes=[f"S32_{h}"])
                    P.I(ACT, 'activation', dict(out=S32[h][:, dc, :], in_=S32[h][:, dc, :], func=AF.Copy, scale=gams[:, h:h + 1]),
                         reads=[f"S32_{h}", "gams"], writes=[f"S32_{h}"])
                    P.I(POOL, 'tensor_copy', dict(out=Sb[h][:, dc, :], in_=S32[h][:, dc, :]),
                         reads=[f"S32_{h}"], writes=[f"Sb_{h}"])
                stt, stk = stat[u], f"stat{u}"
                P.I(ACT, 'activation', dict(out=junk[:], in_=po[:, :], func=AF.Copy, accum_out=stt[:, 0:1]),
                     reads=[pok], writes=[stk + "a"])
                P.I(ACT, 'activation', dict(out=junk[:], in_=po[:, :], func=AF.Square, accum_out=stt[:, 1:2]),
                     reads=[pok], writes=[stk + "b"])
                P.I(DVE, 'tensor_scalar', dict(out=stt[:, 2:3], in0=stt[:, 0:1], scalar1=1.0 / 512, scalar2=None, op0=ALU.mult),
                     reads=[stk + "a"], writes=[stk + "c"])
                P.I(DVE, 'tensor_tensor', dict(out=stt[:, 3:4], in0=stt[:, 2:3], in1=stt[:, 2:3], op=ALU.mult),
                     reads=[stk + "c"], writes=[stk + "d"])
                P.I(DVE, 'scalar_tensor_tensor', dict(out=stt[:, 4:5], in0=stt[:, 1:2], scalar=1.0 / 512, in1=stt[:, 3:4],
                                                                     op0=ALU.mult, op1=ALU.subtract),
                     reads=[stk + "b", stk + "d"], writes=[stk + "e"])
                P.I(DVE, 'tensor_scalar', dict(out=stt[:, 4:5], in0=stt[:, 4:5], scalar1=LN_EPS, scalar2=None, op0=ALU.add),
                     reads=[stk + "e"], writes=[stk + "e"])
                P.I(ACT, 'activation', dict(out=stt[:, 5:6], in_=stt[:, 4:5], func=AF.Sqrt),
                     reads=[stk + "e"], writes=[stk + "f"])
                P.I(DVE, 'reciprocal', dict(out=stt[:, 6:7], in_=stt[:, 5:6]),
                     reads=[stk + "f"], writes=[stk + "g"])
                P.I(DVE, 'scalar_tensor_tensor', dict(out=stt[:, 7:8], in0=stt[:, 2:3], scalar=-1.0, in1=stt[:, 6:7],
                                                                     op0=ALU.mult, op1=ALU.mult),
                     reads=[stk + "c", stk + "g"], writes=[stk + "h"])
                P.I(ACT, 'activation', dict(out=on[u][:], in_=po[:, :], func=AF.Identity,
                                                              bias=stt[:, 7:8], scale=stt[:, 6:7]),
                     reads=[pok, stk + "g", stk + "h"], writes=[f"on{u}"])
                P.I(POOL, 'tensor_tensor', dict(out=on[u][:], in0=on[u][:], in1=gns[:, h * 512:(h + 1) * 512], op=ALU.mult),
                     reads=[f"on{u}", "gns"], writes=[f"on{u}"])
                P.I(DVE, 'tensor_tensor', dict(out=ofin[u][:], in0=on[u][:], in1=gt[u][:], op=ALU.mult),
                     reads=[f"on{u}", f"gt{u}"], writes=[f"ofin{u}"])
                ptr2, ptr2k = pst[1], "pst1"
                for fc in range(4):
                    P.I(PE, 'transpose', dict(out=ptr2[:, fc * 128:(fc + 1) * 128], in_=ofin[u][:, fc * 128:(fc + 1) * 128],
                                                              identity=idb[:]),
                         reads=[f"ofin{u}", "idb"], writes=[ptr2k])
                P.I(ACT, 'activation', dict(out=oTt[:, h * 4:(h + 1) * 4, tsl],
                                                 in_=ptr2[:, 0:512].rearrange("p (c t) -> p c t", c=4), func=AF.Copy),
                     reads=[ptr2k], writes=[(oTk, h, ti)])
        okeys = [(oTk, h, ti) for h in range(2) for ti in range(NT)]
        for c in range(8):
            pw, pwk = proj_bank()
            for k in range(8):
                P.I(PE, 'matmul', MM(pw[:, :G], wos[:, k, c * 128:(c + 1) * 128], oTt[:, k, :],
                                                            start=(k == 0), stop=(k == 7)),
                     reads=["wo"] + okeys, writes=[pwk])
            m = mo[c % 2]
            mk = f"mo{c % 2}"
            P.I(ACT, 'activation', dict(out=m[:], in_=pw[:, :G], func=AF.Copy),
                 reads=[pwk], writes=[mk])
            P.dma(SP, mdst(c, t0, G), m[:], reads=[mk], writes=[("mix", g, c)], slot=f"st{c % 2}")
    print("ret ops:", P.stats())
    if standalone:
        return P.finish()
    P.end_phase()


def ref_ret(x, w_in_h, gn_g_h, w_out_h, heads):
    T = x.shape[0]
    nh = len(heads)
    x = x.astype(np.float64)
    proj = x @ w_in_h.astype(np.float64)
    q = proj[:, :nh * 256].reshape(T, nh, 256)
    k = proj[:, nh * 256:2 * nh * 256].reshape(T, nh, 256)
    v = proj[:, 2 * nh * 256:2 * nh * 256 + nh * 512].reshape(T, nh, 512)
    gate = proj[:, 2 * nh * 256 + nh * 512:].reshape(T, nh, 512)
    inv_freq = np.power(XPOS_BASE, -np.linspace(0.0, 1.0, 128))
    ang = np.arange(T)[:, None] * inv_freq[None, :]
    cos, sin = np.cos(ang)[:, None, :], np.sin(ang)[:, None, :]

    def rot(a):
        a1, a2 = a[..., :128], a[..., 128:]
        return np.concatenate([a1 * cos - a2 * sin, a1 * sin + a2 * cos], -1)
    q = rot(q)
    k = rot(k) * 256 ** -0.5
    outs = []
    for i, h in enumerate(heads):
        lg = np.log1p(-np.exp2(-5.0 - h))
        gamma = np.exp(lg)
        S = np.zeros((256, 512))
        o = np.zeros((T, 512))
        for t in range(T):
            S = gamma * S + np.outer(k[t, i], v[t, i])
            o[t] = q[t, i] @ S
        mu = o.mean(-1, keepdims=True)
        var = ((o - mu) ** 2).mean(-1, keepdims=True)
        o = (o - mu) / np.sqrt(var + LN_EPS) * gn_g_h[i * 512:(i + 1) * 512]
        g = gate[:, i]
        outs.append(o * (g / (1 + np.exp(-g))))
    o = np.concatenate(outs, -1)
    return o @ w_out_h.astype(np.float64)


D = 1024
NORM_EPS = 1e-6
NEG = -30000.0


def gdn_consts():
    r = np.arange(128)
    c = {}
    c["ident"] = np.eye(128, dtype=np.float32)
    c["i2"] = (2.0 * np.eye(128)).astype(np.float32)
    c["triu"] = (r[:, None] <= r[None, :]).astype(np.float32)
    c["negm"] = np.where(r[:, None] <= r[None, :], 0.0, NEG).astype(np.float32)
    c["strict"] = (r[:, None] < r[None, :]).astype(np.float32)
    bd = []
    for l in range(1, 8):
        b = 1 << l
        bd.append(((r[:, None] // b) == (r[None, :] // b)).astype(np.float32))
    c["bd"] = np.ascontiguousarray(np.stack(bd, 1))
    return c


STAGE = [9]


def build_gdn(T, G=512, P=None, xsrc=None, mdst=None):
    standalone = P is None
    if standalone:
        P = Prog("gdn")
        P.begin_phase("gdn")
    NT = G // 128
    NH = 4
    xT = P.dram_in("xT", [D, T]) if xsrc is None else None
    wqkvz = P.dram_in("wqkvz", [D, 2048])
    wba = P.dram_in("wba", [D, 8])
    cw = P.dram_in("cw", [128, 12, 4])
    hp = P.dram_in("hp", [128, 8])
    ngt = P.dram_in("ngt", [128, 512])
    wo = P.dram_in("wo", [512, D])
    cd = {k: P.dram_in("c_" + k, list(v.shape)) for k, v in gdn_consts().items()}
    mixT = P.dram_out("mixT", [D, T]) if mdst is None else None

    ws = P.sb([128, 8, 2048], BF16, "ws")
    wbas = P.sb([128, 8, 8], BF16, "wbas")
    wos = P.sb([128, 4, D], BF16, "wos")
    cws = P.sb([128, 12, 4], F32, "cws")
    hps = P.sb([128, 8], F32, "hps")
    negA = P.sb([128, 4], F32, "negA")
    ngs = P.sb([128, 512], F32, "ngs")
    ident = P.sb([128, 128], F32, "ident")
    identb = P.sb([128, 128], BF16, "identb")
    i2 = P.sb([128, 128], F32, "i2")
    ones = P.sb([128, 128], F32, "ones")
    triu = P.sb([128, 128], F32, "triu")
    negm = P.sb([128, 128], F32, "negm")
    strict = P.sb([128, 128], F32, "strict")
    bd = P.sb([128, 7, 128], F32, "bd")
    xb = [P.sb([128, 8, G], BF16, f"xb{i}") for i in range(2)]
    pc = P.sb([128, 12, G + 3], F32, "pc")
    cacc = [P.sb([128, G], F32, f"cacc{i}") for i in range(2)]
    qkf = P.sb([128, G], F32, "qkf")
    sqt = P.sb([128, G], F32, "sqt")
    rin = P.sb([128, G], F32, "rin")
    qT = P.sb([128, NH, G], BF16, "qT")
    kT = P.sb([128, NH, G], BF16, "kT")
    vTf = P.sb([128, NH, G], F32, "vTf")
    zs = [P.sb([128, 512], F32, f"zs{i}") for i in range(2)]
    sm = [P.sb([128, 48], F32, f"sm{i}") for i in range(2)]
    def t4(name, dt=F32, n=1):
        return [P.sb([128, NH, 128], dt, f"{name}_{i}") for i in range(n)]
    gTri4 = t4("gTri")[0]; ngTri4 = t4("ngTri")[0]; ET4 = t4("ET")[0]; ETs4 = t4("ETs")[0]; TT4 = t4("TT")[0]; TL4 = t4("TL")[0]
    attnT4 = t4("attnT", BF16, 2)
    Nn4 = t4("Nn", F32, 2); Mm4 = t4("Mm", F32, 2); Pn4 = t4("Pn")[0]; Pm4 = t4("Pm")[0]
    Nfin4 = t4("Nfin", F32, 2)
    Vtok4 = t4("Vtok", F32, 2); Vres4 = t4("Vres")[0]; Vn4 = t4("Vn", BF16)[0]; tQS4 = t4("tQS")[0]; osb4 = t4("osb")[0]
    Kdec4 = t4("Kdec", BF16, 2); tmp4 = t4("tmp")[0]
    S32 = P.sb([128, NH, 128], F32, "S32")
    Sb = P.sb([128, NH, 128], BF16, "Sb")
    junk = P.sb([128, 128], F32, "junk")
    ofin = [P.sb([128, 512], BF16, f"ofin{i}") for i in range(2)]
    oT = [P.sb([128, NH, G], BF16, f"oT{i}") for i in range(2)]
    mo = [P.sb([128, G], F32, f"mo{i}") for i in range(2)]
    psum = [P.ps([128, 512], F32, f"psum{i}") for i in range(7)]
    psb = P.ps([128, 1024], BF16, "psb")

    P.dma(SP, cws[:], cw[:, :, :], writes=["cws"], slot="c0_1")
    P.dma(SP, hps[:], hp[:, :], writes=["hps"], slot="c0_2")
    P.dma(SP, ngs[:], ngt[:, :], writes=["ngs"], slot="c0_3")
    for nm, t in (("ident", ident), ("i2", i2), ("triu", triu), ("negm", negm), ("strict", strict)):
        P.dma(SP, t[:], cd[nm][:, :], writes=[nm], slot="c_" + nm)
    P.dma(SP, bd[:], cd["bd"][:, :, :], writes=["bd"], slot="c0_5")
    P.dma(POOL, identb[:], cd["ident"][:, :], writes=["identb"], slot="c1")
    P.I(POOL, 'memset', dict(ap=ones[:], constant=1.0), writes=["ones"])
    P.I(POOL, 'memset', dict(ap=pc[:], constant=0.0), writes=["pc"])
    P.I(POOL, 'memset', dict(ap=S32[:], constant=0.0), writes=["S32"])
    P.I(POOL, 'memset', dict(ap=Sb[:], constant=0.0), writes=["Sb"])
    for k in range(8):
        r = slice(k * 128, (k + 1) * 128)
        P.dma(POOL, ws[:, k, :], wqkvz[r, :], writes=["ws"], slot="w0")
        P.dma(POOL, wbas[:, k, :], wba[r, :], writes=["wbas"], slot="w1")
    for k in range(4):
        P.dma(POOL, wos[:, k, :], wo[k * 128:(k + 1) * 128, :], writes=["wos"], slot="w2")
    P.I(ACT, 'activation', dict(out=negA[:], in_=hps[:, 0:4], func=AF.Exp), reads=["hps"], writes=["negA"])
    P.I(DVE, 'tensor_scalar', dict(out=negA[:], in0=negA[:], scalar1=-1.0, scalar2=None, op0=ALU.mult),
        reads=["negA"], writes=["negA"])

    if xsrc is None:
        xTv = xT.rearrange("(c p) t -> p c t", p=128)
        xsrc = lambda t0, n: xTv[:, :, t0:t0 + n]
    if mdst is None:
        mixv = mixT.rearrange("(c p) t -> p c t", p=128)
        mdst = lambda c, t0, n: mixv[:, c, t0:t0 + n]
    pp = [0]

    def proj_bank():
        pp[0] ^= 1
        return psum[pp[0]], f"pp{pp[0]}"

    PN = psum[2:4]
    PSET = psum[4]
    PREC = psum[5]
    PMISC = psum[6]

    tcount = [0]
    ng = T // G
    for g in range(ng):
        s = g % 2
        t0 = g * G
        xbt, xk = xb[s], f"xb{s}"
        oTt, oTk = oT[s], f"oT{s}"
        P.dma(POOL, xbt[:], xsrc(t0, G), writes=[xk], slot=f"ldx{s}")
        for ch in range(12):
            kind, h = ch // 4, ch % 4
            pb, pbk = proj_bank()
            col = kind * 512 + h * 128
            for k in range(8):
                P.I(PE, 'matmul', MM(pb[:, :G], ws[:, k, col:col + 128], xbt[:, k, :], start=(k == 0), stop=(k == 7)),
                    reads=["ws", xk], writes=[pbk])
            pck = ("pc", ch)
            P.I(ACT, 'activation', dict(out=pc[:, ch, 3:3 + G], in_=pb[:, :G], func=AF.Copy), reads=[pbk, "pc"], writes=[pck])
            ca = cacc[ch % 2]
            cak = f"cacc{ch % 2}"
            P.I(DVE, 'tensor_scalar', dict(out=ca[:], in0=pc[:, ch, 0:G], scalar1=cws[:, ch, 0:1], scalar2=None, op0=ALU.mult),
                reads=[pck, "cws"], writes=[cak])
            for j in range(1, 4):
                P.I(DVE, 'scalar_tensor_tensor', dict(out=ca[:], in0=pc[:, ch, j:j + G], scalar=cws[:, ch, j:j + 1], in1=ca[:],
                                                      op0=ALU.mult, op1=ALU.add), reads=[pck, cak], writes=[cak])
            P.I(POOL, 'tensor_copy', dict(out=pc[:, ch, 0:3], in_=pc[:, ch, G:G + 3]), reads=[pck, cak], writes=[pck])
            if kind == 2:
                P.I(ACT, 'activation', dict(out=vTf[:, h, :], in_=ca[:], func=AF.Silu), reads=[cak], writes=[("vTf", h)])
            else:
                dst, dk_ = (qT, ("qT", h)) if kind == 0 else (kT, ("kT", h))
                P.I(ACT, 'activation', dict(out=qkf[:], in_=ca[:], func=AF.Silu), reads=[cak], writes=["qkf"])
                P.I(POOL, 'tensor_tensor', dict(out=sqt[:], in0=qkf[:], in1=qkf[:], op=ALU.mult), reads=["qkf"], writes=["sqt"])
                pq, pqk = proj_bank()
                P.I(PE, 'matmul', MM(pq[:, :G], ones[:], sqt[:]), reads=["ones", "sqt"], writes=[pqk])
                P.I(DVE, 'tensor_scalar', dict(out=rin[:], in0=pq[:, :G], scalar1=NORM_EPS, scalar2=None, op0=ALU.add),
                    reads=[pqk], writes=["rin"])
                P.I(ACT, 'activation', dict(out=rin[:], in_=rin[:], func=AF.Sqrt), reads=["rin"], writes=["rin"])
                P.I(DVE, 'reciprocal', dict(out=rin[:], in_=rin[:]), reads=["rin"], writes=["rin"])
                scl = 128 ** -0.5 if kind == 0 else 1.0
                P.I(DVE, 'scalar_tensor_tensor', dict(out=dst[:, h, :], in0=qkf[:], scalar=scl, in1=rin[:], op0=ALU.mult, op1=ALU.mult),
                    reads=["qkf", "rin"], writes=[dk_])
        B2, B3, B4, B5, B7 = psum[2], psum[3], psum[4], psum[5], psb
        v4 = lambda bank: bank[:, :].rearrange("p (h t) -> p h t", h=NH)
        bin_ = lambda ap: ap.unsqueeze(2).broadcast_to([128, NH, 128])
        bmid = lambda ap: ap.unsqueeze(1).broadcast_to([128, NH, 128])
        kks = [("kT", h) for h in range(NH)]
        qks = [("qT", h) for h in range(NH)]

        def chain_a(ti, u, I):
            tsl = slice(ti * 128, (ti + 1) * 128)
            smt, smk = sm[u], f"sm{u}"
            pvt, pvtk = None, None

            pb, pbk = proj_bank()
            for k in range(8):
                I(PE, 'matmul', MM(pb[:, :], xbt[:, k, tsl], ws[:, k, 1536:2048], start=(k == 0), stop=(k == 7)),
                    reads=["ws", xk], writes=[pbk])
            I(ACT, 'activation', dict(out=zs[u][:], in_=pb[:, :], func=AF.Silu), reads=[pbk], writes=[f"zs{u}"])
            I(POOL, 'tensor_tensor', dict(out=zs[u][:], in0=zs[u][:], in1=ngs[:], op=ALU.mult), reads=[f"zs{u}", "ngs"], writes=[f"zs{u}"])
            for k in range(8):
                I(PE, 'matmul', MM(PMISC[:, 256:264], xbt[:, k, tsl], wbas[:, k, :], start=(k == 0), stop=(k == 7)),
                    reads=["wbas", xk], writes=["B6"])
            I(ACT, 'activation', dict(out=smt[:, 0:4], in_=PMISC[:, 256:260], func=AF.Sigmoid), reads=["B6"], writes=[(smk, "beta")])
            I(DVE, 'tensor_tensor', dict(out=smt[:, 32:36], in0=PMISC[:, 260:264], in1=hps[:, 4:8], op=ALU.add),
                reads=["B6", "hps"], writes=[(smk, "tmp")])
            I(ACT, 'activation', dict(out=smt[:, 32:36], in_=smt[:, 32:36], func=AF.Exp), reads=[(smk, "tmp")], writes=[(smk, "tmp")])
            I(ACT, 'activation', dict(out=smt[:, 32:36], in_=smt[:, 32:36], func=AF.Ln, bias=ones[:, 0:1]), reads=[(smk, "tmp"), "ones"], writes=[(smk, "tmp")])
            I(DVE, 'tensor_tensor', dict(out=smt[:, 4:8], in0=smt[:, 32:36], in1=negA[:], op=ALU.mult),
                reads=[(smk, "tmp"), "negA"], writes=[(smk, "g")])
            I(PE, 'matmul', MM(PMISC[:, 264:268], triu[:], smt[:, 4:8]), reads=["triu", (smk, "g")], writes=["B6"])
            I(PE, 'matmul', MM(PMISC[:, 268:272], ones[:], smt[:, 4:8]), reads=["ones", (smk, "g")], writes=["B6"])
            I(DVE, 'tensor_copy', dict(out=smt[:, 8:12], in_=PMISC[:, 264:268]), reads=["B6"], writes=[(smk, "gc")])
            I(DVE, 'tensor_scalar', dict(out=smt[:, 12:16], in0=PMISC[:, 264:268], scalar1=-1.0, scalar2=None, op0=ALU.mult),
                reads=["B6"], writes=[(smk, "negc")])
            I(ACT, 'activation', dict(out=smt[:, 16:20], in_=PMISC[:, 264:268], func=AF.Exp), reads=["B6"], writes=[(smk, "egc")])
            I(DVE, 'tensor_scalar', dict(out=smt[:, 20:24], in0=smt[:, 16:20], scalar1=-1.0, scalar2=None, op0=ALU.mult),
                reads=[(smk, "egc")], writes=[(smk, "negegc")])
            I(DVE, 'tensor_tensor', dict(out=smt[:, 24:28], in0=PMISC[:, 268:272], in1=smt[:, 8:12], op=ALU.subtract),
                reads=["B6", (smk, "gc")], writes=[(smk, "kdecs")])
            I(ACT, 'activation', dict(out=smt[:, 24:28], in_=smt[:, 24:28], func=AF.Exp), reads=[(smk, "kdecs")], writes=[(smk, "kdecs")])
            I(ACT, 'activation', dict(out=smt[:, 28:32], in_=PMISC[:, 268:272], func=AF.Exp), reads=["B6"], writes=[(smk, "etot")])

            I(POOL, 'tensor_tensor', dict(out=gTri4[:], in0=bmid(triu[:]), in1=bin_(smt[:, 4:8]), op=ALU.mult),
                reads=["triu", (smk, "g")], writes=["gTri4"])
            I(POOL, 'tensor_scalar', dict(out=ngTri4[:], in0=gTri4[:], scalar1=-1.0, scalar2=None, op0=ALU.mult),
                reads=["gTri4"], writes=["ngTri4"])
            for h in range(NH):
                r = B2[:, h * 128:(h + 1) * 128]
                I(PE, 'matmul', MM(r, ones[:], gTri4[:, h, :], start=True, stop=False), reads=["ones", "gTri4"], writes=["B2"])
                I(PE, 'matmul', MM(r, ngTri4[:, h, :], ones[:], start=False, stop=False), reads=["ones", "ngTri4"], writes=["B2"])
                I(PE, 'matmul', MM(r, ident[:], negm[:], start=False, stop=True), reads=["ident", "negm"], writes=["B2"])
            I(ACT, 'activation', dict(out=ET4[:], in_=v4(B2), func=AF.Exp), reads=["B2"], writes=["ET4"])
            I(POOL, 'tensor_tensor', dict(out=ETs4[:], in0=ET4[:], in1=bmid(strict[:]), op=ALU.mult), reads=["ET4", "strict"], writes=["ETs4"])
            for h in range(NH):
                I(PE, 'matmul', MM(B3[:, h * 128:(h + 1) * 128], kT[:, h, tsl], kT[:, h, tsl]), reads=[kks[h]], writes=["B3"])
            I(DVE, 'tensor_tensor', dict(out=TT4[:], in0=v4(B3), in1=bin_(smt[:, 0:4]), op=ALU.mult), reads=["B3", (smk, "beta")], writes=["TT4"])
            I(DVE, 'tensor_tensor', dict(out=TT4[:], in0=TT4[:], in1=ETs4[:], op=ALU.mult), reads=["TT4", "ETs4"], writes=["TT4"])
            for h in range(NH):
                I(PE, 'matmul', MM(B2[:, h * 128:(h + 1) * 128], kT[:, h, tsl], qT[:, h, tsl]), reads=[kks[h], qks[h]], writes=["B2"])
            I(DVE, 'tensor_tensor', dict(out=attnT4[u][:], in0=v4(B2), in1=ET4[:], op=ALU.mult), reads=["B2", "ET4"], writes=[("attnT4", u)])
            for h in range(NH):
                I(PE, 'transpose', dict(out=B3[:, h * 128:(h + 1) * 128], in_=TT4[:, h, :], identity=ident[:]), reads=["TT4", "ident"], writes=["B3"])
            I(DVE, 'tensor_tensor', dict(out=TL4[:], in0=v4(B3), in1=bmid(ident[:]), op=ALU.add), reads=["B3", "ident"], writes=["TL4"])
            I(POOL, 'tensor_tensor', dict(out=TT4[:], in0=TT4[:], in1=bmid(ident[:]), op=ALU.add), reads=["TT4", "ident"], writes=["TT4"])
            I(DVE, 'scalar_tensor_tensor', dict(out=Nn4[0][:], in0=TT4[:], scalar=-1.0, in1=bmid(i2[:]), op0=ALU.mult, op1=ALU.add),
                reads=["TT4", "i2"], writes=[("Nn4", 0)])
            I(POOL, 'tensor_tensor', dict(out=Nn4[0][:], in0=Nn4[0][:], in1=bmid(bd[:, 0, :]), op=ALU.mult), reads=[("Nn4", 0), "bd"], writes=[("Nn4", 0)])
            I(DVE, 'scalar_tensor_tensor', dict(out=Mm4[0][:], in0=TL4[:], scalar=-1.0, in1=bmid(i2[:]), op0=ALU.mult, op1=ALU.add),
                reads=["TL4", "i2"], writes=[("Mm4", 0)])
            I(POOL, 'tensor_tensor', dict(out=Mm4[0][:], in0=Mm4[0][:], in1=bmid(bd[:, 0, :]), op=ALU.mult), reads=[("Mm4", 0), "bd"], writes=[("Mm4", 0)])
            for h in range(NH):
                I(PE, 'transpose', dict(out=B7[:, h * 128:(h + 1) * 128], in_=kT[:, h, tsl], identity=identb[:]), reads=[kks[h], "identb"], writes=["B7"])
            I(DVE, 'tensor_tensor', dict(out=Kdec4[u][:], in0=B7[:, 0:512].rearrange("p (h t) -> p h t", h=NH), in1=bin_(smt[:, 24:28]), op=ALU.mult),
                reads=["B7", (smk, "kdecs")], writes=[("Kdec4", u)])
            pvt, pvtk = proj_bank()
            for h in range(NH):
                I(PE, 'transpose', dict(out=pvt[:, h * 128:(h + 1) * 128], in_=vTf[:, h, tsl], identity=ident[:]), reads=[("vTf", h), "ident"], writes=[pvtk])
            I(ACT, 'activation', dict(out=Vtok4[u][:], in_=v4(pvt), func=AF.Copy), reads=[pvtk], writes=[("Vtok4", u)])
            for l in range(1, 7):
                cur, nxt = (l - 1) % 2, l % 2
                last = (l == 6)
                for h in range(NH):
                    I(PE, 'matmul', MM(B2[:, h * 128:(h + 1) * 128], TL4[:, h, :], Nn4[cur][:, h, :]), reads=["TL4", ("Nn4", cur)], writes=["B2"])
                I(DVE, 'scalar_tensor_tensor', dict(out=Pn4[:], in0=v4(B2), scalar=-1.0, in1=bmid(i2[:]), op0=ALU.mult, op1=ALU.add),
                    reads=["B2", "i2"], writes=["Pn4"])
                if not last:
                    for h in range(NH):
                        I(PE, 'matmul', MM(B3[:, h * 128:(h + 1) * 128], TT4[:, h, :], Mm4[cur][:, h, :]), reads=["TT4", ("Mm4", cur)], writes=["B3"])
                    I(ACT, 'activation', dict(out=Pm4[:], in_=v4(B3), func=AF.Copy, scale=-1.0), reads=["B3"], writes=["Pm4"])
                    I(POOL, 'tensor_tensor', dict(out=Pm4[:], in0=Pm4[:], in1=bmid(i2[:]), op=ALU.add), reads=["Pm4", "i2"], writes=["Pm4"])
                for h in range(NH):
                    I(PE, 'matmul', MM(B2[:, h * 128:(h + 1) * 128], Mm4[cur][:, h, :], Pn4[:, h, :]), reads=[("Mm4", cur), "Pn4"], writes=["B2"])
                dstN, dkN = (Nfin4[u], ("Nfin4", u)) if last else (Nn4[nxt], ("Nn4", nxt))
                I(DVE, 'tensor_tensor', dict(out=dstN[:], in0=v4(B2), in1=bmid(bd[:, l, :]), op=ALU.mult), reads=["B2", "bd"], writes=[dkN])
                if not last:
                    for h in range(NH):
                        I(PE, 'matmul', MM(B3[:, h * 128:(h + 1) * 128], Nn4[cur][:, h, :], Pm4[:, h, :]), reads=[("Nn4", cur), "Pm4"], writes=["B3"])
                    I(DVE, 'tensor_tensor', dict(out=Mm4[nxt][:], in0=v4(B3), in1=bmid(bd[:, l, :]), op=ALU.mult), reads=["B3", "bd"], writes=[("Mm4", nxt)])

        def chain_b(ti, u, I):
            tsl = slice(ti * 128, (ti + 1) * 128)
            smt, smk = sm[u], f"sm{u}"

            for h in range(NH):
                I(PE, 'matmul', MM(B4[:, h * 128:(h + 1) * 128], kT[:, h, tsl], Sb[:, h, :]), reads=[kks[h], "Sb"], writes=["B4"])
            for h in range(NH):
                I(PE, 'matmul', MM(B5[:, h * 128:(h + 1) * 128], qT[:, h, tsl], Sb[:, h, :]), reads=[qks[h], "Sb"], writes=["B5"])
            I(DVE, 'tensor_tensor', dict(out=Vres4[:], in0=v4(B4), in1=bin_(smt[:, 20:24]), op=ALU.mult), reads=["B4", (smk, "negegc")], writes=["Vres4"])
            I(POOL, 'tensor_tensor', dict(out=Vres4[:], in0=Vres4[:], in1=Vtok4[u][:], op=ALU.add), reads=["Vres4", ("Vtok4", u)], writes=["Vres4"])
            I(DVE, 'tensor_tensor', dict(out=tQS4[:], in0=v4(B5), in1=bin_(smt[:, 16:20]), op=ALU.mult), reads=["B5", (smk, "egc")], writes=["tQS4"])
            for h in range(NH):
                I(PE, 'matmul', MM(B4[:, h * 128:(h + 1) * 128], Nfin4[u][:, h, :], Vres4[:, h, :]), reads=[("Nfin4", u), "Vres4"], writes=["B4"])
            I(DVE, 'tensor_tensor', dict(out=Vn4[:], in0=v4(B4), in1=bin_(smt[:, 0:4]), op=ALU.mult), reads=["B4", (smk, "beta")], writes=["Vn4"])
            for h in range(NH):
                I(PE, 'matmul', MM(B5[:, h * 128:(h + 1) * 128], attnT4[u][:, h, :], Vn4[:, h, :]), reads=[("attnT4", u), "Vn4"], writes=["B5"])
            for h in range(NH):
                I(PE, 'matmul', MM(B4[:, h * 128:(h + 1) * 128], Kdec4[u][:, h, :], Vn4[:, h, :]), reads=[("Kdec4", u), "Vn4"], writes=["B4"])
            I(DVE, 'tensor_tensor', dict(out=osb4[:], in0=v4(B5), in1=tQS4[:], op=ALU.add), reads=["B5", "tQS4"], writes=["osb4"])
            I(POOL, 'tensor_tensor', dict(out=S32[:], in0=S32[:], in1=bin_(smt[:, 28:32]), op=ALU.mult), reads=["S32", (smk, "etot")], writes=["S32"])
            I(DVE, 'tensor_tensor', dict(out=S32[:], in0=S32[:], in1=v4(B4), op=ALU.add), reads=["S32", "B4"], writes=["S32"])
            I(ACT, 'activation', dict(out=Sb[:], in_=S32[:], func=AF.Copy), reads=["S32"], writes=["Sb"])
            I(POOL, 'tensor_tensor', dict(out=tmp4[:], in0=osb4[:], in1=osb4[:], op=ALU.mult), reads=["osb4"], writes=["tmp4"])
            I(DVE, 'tensor_reduce', dict(out=smt[:, 36:40], in_=tmp4[:], axis=AX.X, op=ALU.add), reads=["tmp4"], writes=[(smk, "ss")])
            I(DVE, 'tensor_scalar', dict(out=smt[:, 40:44], in0=smt[:, 36:40], scalar1=1.0 / 128, scalar2=NORM_EPS, op0=ALU.mult, op1=ALU.add),
                reads=[(smk, "ss")], writes=[(smk, "ms")])
            I(ACT, 'activation', dict(out=smt[:, 40:44], in_=smt[:, 40:44], func=AF.Sqrt), reads=[(smk, "ms")], writes=[(smk, "ms")])
            I(DVE, 'reciprocal', dict(out=smt[:, 44:48], in_=smt[:, 40:44]), reads=[(smk, "ms")], writes=[(smk, "rs")])
            I(DVE, 'tensor_tensor', dict(out=osb4[:], in0=osb4[:], in1=bin_(smt[:, 44:48]), op=ALU.mult), reads=["osb4", (smk, "rs")], writes=["osb4"])
            I(POOL, 'tensor_tensor', dict(out=ofin[u][:].rearrange("p (h t) -> p h t", h=NH), in0=osb4[:], in1=zs[u][:].rearrange("p (h t) -> p h t", h=NH), op=ALU.mult),
                reads=["osb4", f"zs{u}"], writes=[("ofin", u)])
            for h in range(NH):
                I(PE, 'transpose', dict(out=B7[:, 512 + h * 128:512 + (h + 1) * 128], in_=ofin[u][:, h * 128:(h + 1) * 128], identity=identb[:]),
                    reads=[("ofin", u), "identb"], writes=["B7"])
            I(ACT, 'activation', dict(out=oTt[:, :, tsl], in_=B7[:, 512:1024].rearrange("p (c t) -> p c t", c=4), func=AF.Copy),
                reads=["B7"], writes=[(oTk, ti)])

        def run_interleaved(fa, fb):
            la, lb = [], []
            if fa is not None:
                fa(lambda *a, **k: la.append((a, k)))
            if fb is not None:
                fb(lambda *a, **k: lb.append((a, k)))
            na, nb = len(la), len(lb)
            ia = ib = 0
            while ia < na or ib < nb:
                if ib >= nb or (ia < na and ia * max(nb, 1) <= ib * max(na, 1)):
                    a, k = la[ia]; ia += 1
                else:
                    a, k = lb[ib]; ib += 1
                P.I(*a, **k)

        us = []
        for ti in range(NT):
            tcount[0] += 1
            us.append(tcount[0] % 2)
        for ti in range(NT + 1):
            fa = (lambda I, ti=ti: chain_a(ti, us[ti], I)) if ti < NT else None
            fb = (lambda I, ti=ti: chain_b(ti - 1, us[ti - 1], I)) if ti >= 1 else None
            run_interleaved(fa, fb)

        okeys = [(oTk, ti) for ti in range(NT)]
        for c in range(8 if STAGE[0] >= 6 else 0):
            pw, pwk = proj_bank()
            for k in range(4):
                P.I(PE, 'matmul', MM(pw[:, :G], wos[:, k, c * 128:(c + 1) * 128], oTt[:, k, :], start=(k == 0), stop=(k == 3)),
                    reads=["wos"] + okeys, writes=[pwk])
            m, mk = mo[c % 2], f"mo{c % 2}"
            P.I(ACT, 'activation', dict(out=m[:], in_=pw[:, :G], func=AF.Copy), reads=[pwk], writes=[mk])
            P.dma(SP, mdst(c, t0, G), m[:], reads=[mk], writes=[("mix", g, c)], slot=f"st{c % 2}")
    print("gdn ops:", P.stats())
    if standalone:
        return P.finish()
    P.end_phase()


def gdn_inputs(xT, a_w_in, a_conv, a_a_log, a_dt_bias, a_norm_g, a_w_out, hh):
    hs = slice(hh * 512, (hh + 1) * 512)
    secs = [a_w_in[:, s * 1024:(s + 1) * 1024][:, hs] for s in range(4)]
    wqkvz = np.ascontiguousarray(np.concatenate(secs, 1))
    wba = np.ascontiguousarray(np.concatenate([a_w_in[:, 4096 + hh * 4:4096 + hh * 4 + 4], a_w_in[:, 4104 + hh * 4:4104 + hh * 4 + 4]], 1))
    cwl = []
    for kind in range(3):
        for h in range(4):
            c0 = kind * 1024 + hh * 512 + h * 128
            cwl.append(a_conv[:, c0:c0 + 128].T)
    cw = np.ascontiguousarray(np.stack(cwl, 1))
    hp = np.concatenate([a_a_log[hh * 4:hh * 4 + 4], a_dt_bias[hh * 4:hh * 4 + 4]])
    hp = np.ascontiguousarray(np.broadcast_to(hp[None, :], (128, 8)))
    ngt = np.ascontiguousarray(np.broadcast_to(np.tile(a_norm_g, 4)[None, :], (128, 512)))
    wo = np.ascontiguousarray(a_w_out[hs, :])
    d = {"xT": xT, "wqkvz": wqkvz, "wba": wba, "cw": cw, "hp": hp, "ngt": ngt, "wo": wo}
    d.update({"c_" + k: v for k, v in gdn_consts().items()})
    return d


D = 1024
MB = 256
ROPE_THETA = 500000.0
NEG = -30000.0


def moba_consts(T):
    half = 16
    inv_freq = np.power(np.float32(ROPE_THETA), -np.arange(half, dtype=np.float32) / half).astype(np.float32)
    ang = (np.arange(T, dtype=np.float32)[:, None] * inv_freq[None, :]).astype(np.float32)
    cos, sin = np.cos(ang).astype(np.float32).T, np.sin(ang).astype(np.float32).T
    C = np.ones((128, T), np.float32)
    S = np.zeros((128, T), np.float32)
    C[0:16], C[16:32] = cos, cos
    S[0:16], S[16:32] = -sin, sin
    scale = np.float32(128 ** -0.5)
    tabs = np.ascontiguousarray(np.stack([C * scale, S * scale, C, S]))
    nb = T // MB
    own = np.arange(nb)[:, None]
    n = np.arange(nb)[None, :]
    gm = np.where(n < own, 0.0, -1e30).astype(np.float32).reshape(1, nb * nb)
    gm = np.ascontiguousarray(np.broadcast_to(gm, (128, nb * nb)))
    oh = np.zeros((nb, nb, 128), np.float32)
    oh[np.arange(nb), np.arange(nb), :] = 1.0
    oh = np.ascontiguousarray(oh.reshape(nb, nb * 128))
    k = np.arange(128)[:, None, None]
    j = np.arange(2)[None, :, None]
    q = np.arange(256)[None, None, :]
    caus = np.where(j * 128 + k <= q, 0.0, NEG).astype(np.float32)
    return {"tabs": tabs, "c_gm": gm, "c_oh": oh, "c_caus": np.ascontiguousarray(caus), "c_ident": np.eye(128, dtype=np.float32)}


def build_moba(T, G=512, P=None, xsrc=None, mdst=None):
    standalone = P is None
    if standalone:
        P = Prog("moba")
        P.begin_phase("moba")
    NH = 4
    NB = T // MB
    NG = T // G
    NTILE = T // 128
    xT = P.dram_in("xT", [D, T]) if xsrc is None else None
    wq = P.dram_in("wq", [D, 512])
    wqs = P.dram_in("wqs", [D, 512])
    wk = P.dram_in("wk", [D, 512])
    wks = P.dram_in("wks", [D, 512])
    wv = P.dram_in("wv", [D, 512])
    wo = P.dram_in("wo", [512, D])
    tabs = P.dram_in("tabs", [4, 128, T])
    gmd = P.dram_in("c_gm", [128, NB * NB])
    ohd = P.dram_in("c_oh", [NB, NB * 128])
    causd = P.dram_in("c_caus", [128, 2, 256])
    identd = P.dram_in("c_ident", [128, 128])
    mixT = P.dram_out("mixT", [D, T]) if mdst is None else None

    wsb = [P.sb([128, 8, 128], BF16, f"w{i}") for i in range(5)]
    wos = P.sb([128, 4, D], BF16, "wos")
    qT = P.sb([128, T], BF16, "qT")
    kT = P.sb([128, T], BF16, "kT")
    vtok = P.sb([128, NTILE, 128], BF16, "vtok")
    oTall = P.sb([128, NH, T], BF16, "oTall")
    xb = [P.sb([128, 8, G], BF16, f"xb{i}") for i in range(2)]
    tb = [P.sb([128, 4, G], F32, f"tb{i}") for i in range(2)]
    t1 = P.sb([128, G], F32, "t1")
    t2 = P.sb([128, G], F32, "t2")
    kf = P.sb([128, G], F32, "kf")
    ks = P.sb([128, 2], F32, "ks")
    kmb = P.sb([128, NB], BF16, "kmb")
    gms = P.sb([128, NB * NB], F32, "gms")
    ohs = P.sb([NB, NB * 128], BF16, "ohs")
    caus = P.sb([128, 2, 256], BF16, "caus")
    ident = P.sb([128, 128], F32, "ident")
    identb = P.sb([128, 128], BF16, "identb")
    onesb = P.sb([128, 128], BF16, "onesb")
    gmt = P.sb([128, 2, NB], F32, "gmt")
    mx = P.sb([128, 16], F32, "mx")
    pen = P.sb([128, 2, NB], F32, "pen")
    penT = [P.sb([NB, 256], BF16, f"penT{i}") for i in range(2)]
    PT = [P.sb([128, 512], BF16, f"PT{i}") for i in range(3)]
    rec = P.sb([128, 256], F32, "rec")
    mo = [P.sb([128, G], F32, f"mo{i}") for i in range(2)]
    psum = [P.ps([128, 512], F32, f"psum{i}") for i in range(8)]
    PA, PB, BG = psum[0], psum[1], psum[1]
    SB = psum[2:4]
    OB = psum[4:6]
    DN = psum[6:8]

    P.dma(SP, gms[:], gmd[:, :], writes=["gms"], slot="c_gm")
    P.dma(SP, ident[:], identd[:, :], writes=["ident"], slot="c_id")
    P.dma(POOL, ohs[:], ohd[:, :], writes=["ohs"], slot="c_oh")
    P.dma(POOL, caus[:], causd[:, :, :], writes=["caus"], slot="c_caus")
    P.dma(POOL, identb[:], identd[:, :], writes=["identb"], slot="c_idb")
    P.I(POOL, 'memset', dict(ap=onesb[:], constant=1.0), writes=["onesb"])
    for k in range(4):
        P.dma(POOL, wos[:, k, :], wo[k * 128:(k + 1) * 128, :], writes=["wos"], slot="w_o")

    if xsrc is None:
        xTv = xT.rearrange("(c p) t -> p c t", p=128)
        xsrc = lambda t0, n: xTv[:, :, t0:t0 + n]
    tabv = tabs.rearrange("n p t -> p n t")
    if mdst is None:
        mixv = mixT.rearrange("(c p) t -> p c t", p=128)
        mdst = lambda c, t0, n: mixv[:, c, t0:t0 + n]
    wd = [wq, wqs, wk, wks, wv]
    sbi = [0]
    odi = [0]
    gi = [0]
    for h in range(NH):
        for i in range(5):
            for k in range(8):
                P.dma(POOL, wsb[i][:, k, :], wd[i][k * 128:(k + 1) * 128, h * 128:(h + 1) * 128], writes=[f"w{i}"], slot=f"w{i}")
        for g in range(NG):
            gi[0] += 1
            s = gi[0] % 2
            t0 = g * G
            xbt, xk, tbt, tk = xb[s], f"xb{s}", tb[s], f"tb{s}"
            P.dma(POOL, xbt[:], xsrc(t0, G), writes=[xk], slot=f"ldx{s}")
            P.dma(SP, tbt[:], tabv[:, :, t0:t0 + G], writes=[tk], slot=f"ldt{s}")
            for w in range(2):
                for k in range(8):
                    P.I(PE, 'matmul', MM(PA[:, :G], wsb[2 * w][:, k, :], xbt[:, k, :], start=(k == 0), stop=(k == 7)),
                        reads=[f"w{2 * w}", xk], writes=["PA"])
                for k in range(8):
                    P.I(PE, 'matmul', MM(PB[:, :G], wsb[2 * w + 1][:, k, :], xbt[:, k, :], start=(k == 0), stop=(k == 7)),
                        reads=[f"w{2 * w + 1}", xk], writes=["PB"])
                P.I(DVE, 'tensor_tensor', dict(out=t1[:], in0=PA[:, :G], in1=tbt[:, 2 * w, :], op=ALU.mult), reads=["PA", tk], writes=["t1"])
                P.I(DVE, 'tensor_tensor', dict(out=t2[:], in0=PB[:, :G], in1=tbt[:, 2 * w + 1, :], op=ALU.mult), reads=["PB", tk], writes=["t2"])
                if w == 0:
                    P.I(POOL, 'tensor_tensor', dict(out=qT[:, t0:t0 + G], in0=t1[:], in1=t2[:], op=ALU.add), reads=["t1", "t2"], writes=[("qT", g)])
                else:
                    P.I(POOL, 'tensor_tensor', dict(out=kf[:], in0=t1[:], in1=t2[:], op=ALU.add), reads=["t1", "t2"], writes=["kf"])
                    P.I(ACT, 'activation', dict(out=kT[:, t0:t0 + G], in_=kf[:], func=AF.Copy), reads=["kf"], writes=[("kT", g)])
                    P.I(DVE, 'tensor_reduce', dict(out=ks[:], in_=kf[:].rearrange("p (b t) -> p b t", b=2), axis=AX.X, op=ALU.add),
                        reads=["kf"], writes=["ks"])
                    P.I(ACT, 'activation', dict(out=kmb[:, 2 * g:2 * g + 2], in_=ks[:], func=AF.Copy), reads=["ks"], writes=[("kmb", g)])
            for ti in range(4):
                for k in range(8):
                    P.I(PE, 'matmul', MM(PA[:, ti * 128:(ti + 1) * 128], xbt[:, k, ti * 128:(ti + 1) * 128], wsb[4][:, k, :],
                                         start=(k == 0), stop=(k == 7)), reads=["w4", xk], writes=["PA"])
            P.I(ACT, 'activation', dict(out=vtok[:, 4 * g:4 * g + 4, :], in_=PA[:, :].rearrange("p (a d) -> p a d", a=4), func=AF.Copy),
                reads=["PA"], writes=[("vtok", g)])
        qkeys = [("qT", g) for g in range(NG)]
        kkeys = [("kT", g) for g in range(NG)]
        vkeys = [("vtok", g) for g in range(NG)]
        mkeys = [("kmb", g) for g in range(NG)]
        def gate_ops(qb):
            own, q0 = qb, qb * 256
            pT, pTk = penT[qb % 2], f"penT{qb % 2}"
            qk_ = [("qT", q0 // G)]
            for j in range(2):
                P.I(PE, 'matmul', MM(BG[:, j * NB:(j + 1) * NB], qT[:, q0 + j * 128:q0 + (j + 1) * 128], kmb[:, :]),
                    reads=qk_ + mkeys, writes=["PB"])
            P.I(DVE, 'tensor_tensor', dict(out=gmt[:], in0=BG[:, 0:2 * NB].rearrange("p (j n) -> p j n", j=2),
                                           in1=gms[:, own * NB:(own + 1) * NB].unsqueeze(1).broadcast_to([128, 2, NB]), op=ALU.add),
                reads=["PB", "gms"], writes=["gmt"])
            for j in range(2):
                P.I(DVE, 'max', dict(out=mx[:, j * 8:(j + 1) * 8], in_=gmt[:, j, :]), reads=["gmt"], writes=[("mx", j)])
            for j in range(2):
                P.I(DVE, 'tensor_scalar', dict(out=pen[:, j, :], in0=gmt[:, j, :], scalar1=mx[:, j * 8 + 2:j * 8 + 3], scalar2=None, op0=ALU.is_ge),
                    reads=["gmt", ("mx", j)], writes=[("pen", j)])
            P.I(DVE, 'tensor_scalar', dict(out=pen[:], in0=pen[:], scalar1=-1.0, scalar2=-NEG, op0=ALU.add, op1=ALU.mult),
                reads=[("pen", 0), ("pen", 1)], writes=["pen"])
            for j in range(2):
                P.I(PE, 'transpose', dict(out=BG[0:NB, 128 + j * 128:128 + (j + 1) * 128], in_=pen[:, j, :], identity=ident[:]),
                    reads=["pen", "ident"], writes=["PB"])
            P.I(ACT, 'activation', dict(out=pT[:], in_=BG[0:NB, 128:384], func=AF.Copy), reads=["PB"], writes=[pTk])

        def score_ops(qb, n):
            own, q0 = qb, qb * 256
            qsl = slice(q0, q0 + 256)
            qk_ = [("qT", q0 // G)]
            x, y = n % 2, n % 3
            sbk, ptk = f"SB{x}", f"PT{y}"
            for c in range(2):
                kc = 2 * n + c
                reg = SB[x][:, c * 256:(c + 1) * 256]
                P.I(PE, 'matmul', MM(reg, kT[:, kc * 128:(kc + 1) * 128], qT[:, qsl], start=True, stop=False),
                    reads=[("kT", kc * 128 // G)] + qk_, writes=[sbk])
                if n < own:
                    P.I(PE, 'matmul', MM(reg, ohs[:, n * 128:(n + 1) * 128], penT[qb % 2][:], start=False, stop=True),
                        reads=["ohs", f"penT{qb % 2}"], writes=[sbk])
                else:
                    P.I(PE, 'matmul', MM(reg, identb[:], caus[:, c, :], start=False, stop=True), reads=["identb", "caus"], writes=[sbk])
            P.I(ACT, 'activation', dict(out=PT[y][:], in_=SB[x][:, :], func=AF.Exp), reads=[sbk], writes=[ptk])

        def pv_ops(qb, n, ob, dn, obk, dnk):
            own = qb
            y = n % 3
            ptk = f"PT{y}"
            for c in range(2):
                kc = 2 * n + c
                first, lastc = (n == 0 and c == 0), (n == own and c == 1)
                P.I(PE, 'matmul', MM(ob[:, 0:256], vtok[:, kc, :], PT[y][:, c * 256:(c + 1) * 256], start=first, stop=lastc),
                    reads=[("vtok", kc // 4), ptk], writes=[obk])
                P.I(PE, 'matmul', MM(dn[:, 0:256], onesb[:], PT[y][:, c * 256:(c + 1) * 256], start=first, stop=lastc),
                    reads=["onesb", ptk], writes=[dnk])

        for qb in range(NB):
            own = qb
            q0 = qb * 256
            qsl = slice(q0, q0 + 256)
            ob, dn = OB[qb % 2], DN[qb % 2]
            obk, dnk = f"OB{qb % 2}", f"DN{qb % 2}"
            score_ops(qb, 0)
            if qb + 1 < NB:
                gate_ops(qb + 1)
            for n in range(1, own + 1):
                score_ops(qb, n)
                pv_ops(qb, n - 1, ob, dn, obk, dnk)
            pv_ops(qb, own, ob, dn, obk, dnk)
            P.I(DVE, 'reciprocal', dict(out=rec[:], in_=dn[:, 0:256]), reads=[dnk], writes=["rec"])
            P.I(DVE, 'tensor_tensor', dict(out=oTall[:, h, qsl], in0=ob[:, 0:256], in1=rec[:], op=ALU.mult), reads=[obk, "rec"],
                writes=[("oT", h, q0 // G, (q0 // 256) % 2)])
    for g in range(NG):
        t0 = g * G
        okeys = [("oT", h, g, j) for h in range(NH) for j in range(2)]
        for c in range(8):
            pw, pwk = (PA, "PA") if c % 2 == 0 else (PB, "PB")
            for k in range(4):
                P.I(PE, 'matmul', MM(pw[:, :G], wos[:, k, c * 128:(c + 1) * 128], oTall[:, k, t0:t0 + G], start=(k == 0), stop=(k == 3)),
                    reads=["wos"] + okeys, writes=[pwk])
            m, mk = mo[c % 2], f"mo{c % 2}"
            P.I(ACT, 'activation', dict(out=m[:], in_=pw[:, :G], func=AF.Copy), reads=[pwk], writes=[mk])
            P.dma(SP, mdst(c, t0, G), m[:], reads=[mk], writes=[("mix", g, c)], slot=f"st{c % 2}")
    print("moba ops:", P.stats())
    if standalone:
        return P.finish()
    P.end_phase()


def moba_inputs(xT, b_w_qkv, b_w_out, hh, T):
    hs = slice(hh * 512, (hh + 1) * 512)
    wq = b_w_qkv[:, 0:1024][:, hs]
    wk = b_w_qkv[:, 1024:2048][:, hs]
    wv = b_w_qkv[:, 2048:3072][:, hs]
    perm = np.arange(512)
    for h in range(4):
        perm[h * 128:h * 128 + 16] = np.arange(h * 128 + 16, h * 128 + 32)
        perm[h * 128 + 16:h * 128 + 32] = np.arange(h * 128, h * 128 + 16)
    d = {"xT": xT, "wq": np.ascontiguousarray(wq), "wqs": np.ascontiguousarray(wq[:, perm]),
         "wk": np.ascontiguousarray(wk), "wks": np.ascontiguousarray(wk[:, perm]), "wv": np.ascontiguousarray(wv),
         "wo": np.ascontiguousarray(b_w_out[hs, :])}
    d.update(moba_consts(T))
    return d


B_, T_FULL = 4, 8192


NLAYERS = [4]
KINDS = [0, 1, 2, 0]
NOAG = [False]
AGUNUSED = [False]


def build_fused(T):
    H = T // 2
    P = Prog("fused")
    x0T = P.dram_in("x0T", [D, T])
    x0h = P.dram_in("x0h", [D, H])
    outT = P.dram_out("outT", [D, H])
    mixp = P.dram_tmp("mixp", [2 * D, H])
    msum = P.dram_tmp("msum", [D, H])
    hout = P.dram_tmp("hout", [D, H])
    hfull = P.dram_tmp("hfull", [2 * D, H])
    x0v = x0T.rearrange("(c p) t -> p c t", p=128)
    x0hv = x0h.rearrange("(c p) t -> p c t", p=128)
    outv = outT.rearrange("(c p) t -> p c t", p=128)
    msv = msum.rearrange("(c p) t -> p c t", p=128)
    hov = hout.rearrange("(c p) t -> p c t", p=128)
    hf4 = hfull.rearrange("(c r p) t -> r p c t", c=8, r=2, p=128)
    mp5 = mixp.rearrange("(cp r ci p) t -> r cp ci p t", cp=4, r=2, ci=2, p=128)
    hfv = [hf4[r] for r in range(2)]
    mdst = lambda c, t0, n: mp5[t0 // H, c // 2, c % 2][:, t0 % H:t0 % H + n]
    NL = NLAYERS[0]
    for i in range(NL):
        kind = KINDS[i]
        P.dram_prefix = f"L{i}_"
        if i == 0 or NOAG[0] or AGUNUSED[0]:
            xsrc = lambda t0, n: x0v[:, :, t0:t0 + n]
        else:
            xsrc = lambda t0, n: hfv[t0 // H][:, :, t0 % H:t0 % H + n]
        P.begin_phase(f"m{i}")
        if kind == 0:
            build_gdn(T, P=P, xsrc=xsrc, mdst=mdst)
        elif kind == 1:
            build_moba(T, P=P, xsrc=xsrc, mdst=mdst)
        else:
            build_ret(T, P=P, xsrc=xsrc, mdst=mdst)
        P.begin_phase(f"rs{i}")
        for c in range(4):
            P.cc("ReduceScatter", ALU.add, mixp[c * 512:(c + 1) * 512, :], msum[c * 256:(c + 1) * 256, :], slot=f"cc_rs{i}")
        P.end_phase()
        P.begin_phase(f"p{i}")
        hs = x0hv if i == 0 else hov
        od = outv if i == NL - 1 else hov
        build_P(H, P=P, hsrc=lambda t0, n, hs=hs: hs[:, :, t0:t0 + n], masrc=lambda t0, n: msv[:, :, t0:t0 + n],
                odst=lambda t0, n, od=od: od[:, :, t0:t0 + n], single_mix=True)
        if i < NL - 1 and not NOAG[0]:
            P.begin_phase(f"ag{i}")
            for c in range(8):
                P.cc("AllGather", ALU.bypass, hout[c * 128:(c + 1) * 128, :], hfull[c * 256:(c + 1) * 256, :], slot=f"cc_ag{i}")
            P.end_phase()
    print("fused ops:", P.stats())
    return P.finish()


def _lnp(g1, b1, g2, b2):
    lay = lambda v: v.reshape(8, 128).T
    return np.ascontiguousarray(np.concatenate([lay(g1), lay(b1), lay(g2), lay(b2)], axis=1).astype(np.float32))


def _ret_inputs(c_w_in, c_gn_g, c_w_out, hh, T):
    heads = [2 * hh, 2 * hh + 1]
    hq = slice(hh * 512, (hh + 1) * 512)
    hv = slice(hh * 1024, (hh + 1) * 1024)
    tabs, gam = ret_tables(heads, T)
    return {"wq": np.ascontiguousarray(c_w_in[:, 0:1024][:, hq]), "wk": np.ascontiguousarray(c_w_in[:, 1024:2048][:, hq]),
            "wv": np.ascontiguousarray(c_w_in[:, 2048:4096][:, hv]), "wg": np.ascontiguousarray(c_w_in[:, 4096:6144][:, hv]),
            "wo": np.ascontiguousarray(c_w_out[hv, :]),
            "gng": np.ascontiguousarray(np.broadcast_to(c_gn_g[hv][None, :], (128, 1024))), "tabs": tabs,
            "cmask": np.triu(np.ones((128, 128), np.float32)), "ident": np.eye(128, dtype=np.float32),
            "gam": np.ascontiguousarray(np.broadcast_to(np.array(gam, np.float32)[None, :], (128, 2)))}


def _core_inputs(c, T, x, a_w_in, a_conv, a_a_log, a_dt_bias, a_norm_g, a_w_out, b_w_qkv, b_w_out,
                 c_w_in, c_gn_g, c_w_out, f_w13, f_w2, ln1_g, ln1_b, ln2_g, ln2_b, cache):
    b, r = c // 2, c % 2
    H = T // 2
    xT = cache.setdefault(("xT", b), np.ascontiguousarray(x[b].T))
    d = {"x0T": xT, "x0h": np.ascontiguousarray(xT[:, r * H:(r + 1) * H])}
    for i in range(NLAYERS[0]):
        kind = KINDS[i]
        j = sum(1 for q in range(i) if KINDS[q] == kind) % {0: 2, 1: 1, 2: 1}[kind]
        key = ("layer", i, r)
        if key not in cache:
            if kind == 0:
                li = gdn_inputs(None, a_w_in[j], a_conv[j], a_a_log[j], a_dt_bias[j], a_norm_g[j], a_w_out[j], hh=r)
            elif kind == 1:
                li = moba_inputs(None, b_w_qkv[j], b_w_out[j], hh=r, T=T)
            else:
                li = _ret_inputs(c_w_in[j], c_gn_g[j], c_w_out[j], hh=r, T=T)
            li.pop("xT", None)
            li["w13"] = f_w13[i]
            li["w2"] = f_w2[i]
            li["lnp"] = _lnp(ln1_g[i], ln1_b[i], ln2_g[i], ln2_b[i])
            cache[key] = {f"L{i}_{k}": v for k, v in li.items()}
        d.update(cache[key])
    return d


def kernel(x, a_w_in, a_conv, a_a_log, a_dt_bias, a_norm_g, a_w_out, b_w_qkv, b_w_out,
           c_w_in, c_gn_g, c_w_out, f_w13, f_w2, ln1_g, ln1_b, ln2_g, ln2_b):
    f32 = lambda a: np.ascontiguousarray(np.asarray(a, dtype=np.float32))
    args = [f32(a) for a in (x, a_w_in, a_conv, a_a_log, a_dt_bias, a_norm_g, a_w_out, b_w_qkv, b_w_out,
                             c_w_in, c_gn_g, c_w_out, f_w13, f_w2, ln1_g, ln1_b, ln2_g, ln2_b)]
    T = args[0].shape[1]
    cores = list(range(8))
    cache = {}
    ims = [_core_inputs(c, T, *args, cache=cache) for c in cores]
    nc = build_fused(T)
    res = run_bass_kernel_spmd(nc, ims, core_ids=cores).results
    out = np.empty((B_, T, D), np.float32)
    H = T // 2
    for c in cores:
        out[c // 2, (c % 2) * H:(c % 2 + 1) * H, :] = res[c]["outT"].T
    return out
```
